# Optimizing a Trainium2 kernel written in Bass

```python
import jax, jax.numpy as jnp
from jax import lax
import numpy as np

D_MODEL = 1024
BATCH = 4
SEQ = 8192
DEPTH = 2

HEAD_DIM = 64
N_HEADS = 4
BRANCH_WIDTH = N_HEADS * HEAD_DIM
N_BRANCHES = 4
Q_BLOCK = 128
RMS_EPS = 1e-6

DSW_PATTERNS = ((128, 1), (512, 4), (2048, 16))

MLA_Q_RANK = 192
MLA_KV_RANK = 128
MLA_NOPE_DIM = 64
MLA_ROPE_DIM = 32
MLA_V_DIM = 64
ROPE_THETA = 10000.0

NSA_CMP_BLOCK = 32
NSA_CMP_STRIDE = 16
NSA_CMP_HIDDEN = 128
NSA_SEL_BLOCK = 64
NSA_TOP_N = 16
NSA_WINDOW = 512
NSA_KV_DIM = 64
NSA_FORCE_SCORE = 1e9

N_ALIBI_HEADS = 8

IN_SPLITS = (
    ("a_q", BRANCH_WIDTH), ("a_k", BRANCH_WIDTH), ("a_v", BRANCH_WIDTH), ("a_gate", BRANCH_WIDTH),
    ("b_cq", MLA_Q_RANK), ("b_ckv", MLA_KV_RANK), ("b_kpe", MLA_ROPE_DIM), ("b_gate", BRANCH_WIDTH),
    ("c_q", BRANCH_WIDTH), ("c_kc", NSA_KV_DIM), ("c_vc", NSA_KV_DIM), ("c_ks", NSA_KV_DIM),
    ("c_vs", NSA_KV_DIM), ("c_kw", NSA_KV_DIM), ("c_vw", NSA_KV_DIM), ("c_g", 3 * N_HEADS),
    ("c_gate", BRANCH_WIDTH),
    ("d_q", BRANCH_WIDTH), ("d_k", BRANCH_WIDTH), ("d_v", BRANCH_WIDTH), ("d_gate", BRANCH_WIDTH),
)
D_IN = sum(w for _, w in IN_SPLITS)

kernel_name = "hybrid_gated_dilated_mla_nsa_stickbreaking"


def rmsnorm(x, g):
    xf = x.astype(jnp.float32)
    y = xf * lax.rsqrt(jnp.mean(xf * xf, axis=-1, keepdims=True) + RMS_EPS)
    return (y * g.astype(jnp.float32)).astype(x.dtype)


def split_columns(p):
    out, off = {}, 0
    for name, w in IN_SPLITS:
        out[name] = p[..., off:off + w]
        off += w
    return out


def alibi_slopes():
    s = 2.0 ** (-8.0 * (np.arange(N_ALIBI_HEADS) + 1) / N_ALIBI_HEADS)
    s = jnp.asarray(s, jnp.float32)
    return s[0::2], s[1::2]


def masked_softmax(s, mask):
    s = jnp.where(mask, s, -jnp.inf)
    m = jnp.max(s, axis=-1, keepdims=True)
    m = jnp.where(jnp.isfinite(m), m, 0.0)
    e = jnp.exp(s - m)
    return e / jnp.maximum(jnp.sum(e, axis=-1, keepdims=True), 1e-30)


def sweep_query_blocks(block_fn, seq):
    out = lax.map(block_fn, jnp.arange(seq // Q_BLOCK))
    nq, b, tq, h, d = out.shape
    return jnp.swapaxes(out, 0, 1).reshape(b, nq * tq, h, d)


def rope(x, pos):
    half = x.shape[-1] // 2
    inv = ROPE_THETA ** (-jnp.arange(half, dtype=jnp.float32) / half)
    ang = pos.astype(jnp.float32)[:, None] * inv[None, :]
    cos, sin = jnp.cos(ang)[:, None, :], jnp.sin(ang)[:, None, :]
    xf = x.astype(jnp.float32)
    x1, x2 = xf[..., :half], xf[..., half:]
    return jnp.concatenate([x1 * cos - x2 * sin, x1 * sin + x2 * cos], axis=-1).astype(x.dtype)


def dilated_window_attention(q, k, v, slopes):
    B, S, H, Dh = q.shape
    scale = Dh ** -0.5

    def block(qb):
        t0 = qb * Q_BLOCK
        t = t0 + jnp.arange(Q_BLOCK)
        qblk = lax.dynamic_slice_in_dim(q, t0, Q_BLOCK, axis=1)
        outs, lses = [], []
        for window, dil in DSW_PATTERNS:
            dist = dil * jnp.arange(window // dil + 1)
            idx = t[:, None] - dist[None, :]
            valid = idx >= 0
            idxc = jnp.maximum(idx, 0)
            kg, vg = k[:, idxc], v[:, idxc]
            s = jnp.einsum('bqhd,bqjhd->bhqj', qblk, kg).astype(jnp.float32) * scale
            s = s - slopes[:, None, None] * dist.astype(jnp.float32)
            s = jnp.where(valid[None, None], s, -jnp.inf)
            lse = jax.nn.logsumexp(s, axis=-1)
            p = jnp.exp(s - lse[..., None])
            outs.append(jnp.einsum('bhqj,bqjhd->bqhd', p.astype(v.dtype), vg))
            lses.append(lse)
        w = jax.nn.softmax(jnp.stack(lses, axis=0), axis=0)
        o = 0.0
        for i in range(len(DSW_PATTERNS)):
            o = o + jnp.transpose(w[i], (0, 2, 1))[..., None].astype(v.dtype) * outs[i]
        return o

    return sweep_query_blocks(block, S)


def blocked_causal_softmax(q, k, v, scale):
    B, S, H, _ = q.shape
    kpos = jnp.arange(S)

    def block(qb):
        t0 = qb * Q_BLOCK
        t = t0 + jnp.arange(Q_BLOCK)
        qblk = lax.dynamic_slice_in_dim(q, t0, Q_BLOCK, axis=1)
        s = jnp.einsum('bqhd,bkhd->bhqk', qblk, k).astype(jnp.float32) * scale
        s = jnp.where(kpos[None, :] <= t[:, None], s, -jnp.inf)
        p = jax.nn.softmax(s, axis=-1)
        return jnp.einsum('bhqk,bkhd->bqhd', p.astype(v.dtype), v)

    return sweep_query_blocks(block, S)


def mla_attention(cq, ckv, kpe, q_norm_g, w_uq, kv_norm_g, w_ukv):
    B, S, _ = cq.shape
    pos = jnp.arange(S)
    q = (rmsnorm(cq, q_norm_g) @ w_uq).reshape(B, S, N_HEADS, MLA_NOPE_DIM + MLA_ROPE_DIM)
    q = jnp.concatenate([q[..., :MLA_NOPE_DIM], rope(q[..., MLA_NOPE_DIM:], pos)], axis=-1)
    kv = (rmsnorm(ckv, kv_norm_g) @ w_ukv).reshape(B, S, N_HEADS, MLA_NOPE_DIM + MLA_V_DIM)
    k_nope, v = kv[..., :MLA_NOPE_DIM], kv[..., MLA_NOPE_DIM:]
    k_pe = rope(kpe[:, :, None, :], pos)
    k = jnp.concatenate([k_nope, jnp.broadcast_to(k_pe, (B, S, N_HEADS, MLA_ROPE_DIM))], axis=-1)
    return blocked_causal_softmax(q, k, v, (MLA_NOPE_DIM + MLA_ROPE_DIM) ** -0.5)


def compress_blocks(tok, pos_emb, w1, b1, w2, b2):
    B, S, Dk = tok.shape
    n_cmp = (S - NSA_CMP_BLOCK) // NSA_CMP_STRIDE + 1
    idx = np.arange(n_cmp)[:, None] * NSA_CMP_STRIDE + np.arange(NSA_CMP_BLOCK)[None, :]
    blocks = tok[:, idx] + pos_emb
    hid = jax.nn.gelu(blocks.reshape(B, n_cmp, NSA_CMP_BLOCK * Dk) @ w1 + b1)
    return hid @ w2 + b2


def nsa_attention(q, kc, vc, ks, vs, kw, vw, gates, pos_emb, w1, b1, w2, b2, slopes):
    B, S, H, Dh = q.shape
    scale = Dh ** -0.5
    k_cmp = compress_blocks(kc, pos_emb[0], w1[0], b1[0], w2[0], b2[0])
    v_cmp = compress_blocks(vc, pos_emb[1], w1[1], b1[1], w2[1], b2[1])
    n_cmp = k_cmp.shape[1]
    cmp_end = jnp.arange(n_cmp) * NSA_CMP_STRIDE + NSA_CMP_BLOCK - 1
    n_sel = S // NSA_SEL_BLOCK
    top_n = min(NSA_TOP_N, n_sel)
    r_sel, r_cmp = NSA_SEL_BLOCK // NSA_CMP_STRIDE, NSA_CMP_BLOCK // NSA_CMP_STRIDE
    src = (jnp.arange(n_sel)[:, None, None] * r_sel - jnp.arange(r_sel)[None, :, None]
           - jnp.arange(r_cmp)[None, None, :]).reshape(n_sel, -1)
    sel_map = jnp.sum(src[..., None] == jnp.arange(n_cmp), axis=1).astype(jnp.float32)
    kw_pad = jnp.pad(kw, ((0, 0), (NSA_WINDOW, 0), (0, 0)))
    vw_pad = jnp.pad(vw, ((0, 0), (NSA_WINDOW, 0), (0, 0)))
    blk = jnp.arange(n_sel)
    sel_off = jnp.arange(NSA_SEL_BLOCK)

    def block(qb):
        t0 = qb * Q_BLOCK
        t = t0 + jnp.arange(Q_BLOCK)
        qblk = lax.dynamic_slice_in_dim(q, t0, Q_BLOCK, axis=1)
        gblk = lax.dynamic_slice_in_dim(gates, t0, Q_BLOCK, axis=1)
        dist_c = (t[:, None] - cmp_end[None, :]).astype(jnp.float32)
        s = jnp.einsum('bqhd,bnd->bhqn', qblk, k_cmp).astype(jnp.float32) * scale
        p_cmp = masked_softmax(s - slopes[:, None, None] * dist_c, dist_c >= 0)
        o_cmp = jnp.einsum('bhqn,bnd->bqhd', p_cmp.astype(v_cmp.dtype), v_cmp)
        p_sel = jnp.einsum('bhqn,jn->bqj', p_cmp, sel_map)
        cur = t // NSA_SEL_BLOCK
        valid = blk[None, :] * NSA_SEL_BLOCK <= t[:, None]
        forced = (blk[None, :] == 0) | (blk[None, :] == cur[:, None]) | (blk[None, :] == cur[:, None] - 1)
        score = jnp.where(valid, jnp.where(forced, NSA_FORCE_SCORE, p_sel), -jnp.inf)
        _, sel = lax.top_k(score, top_n)
        tok = (sel[..., None] * NSA_SEL_BLOCK + sel_off).reshape(B, Q_BLOCK, top_n * NSA_SEL_BLOCK)
        ks_g = jax.vmap(lambda a, i: a[i])(ks, tok)
        vs_g = jax.vmap(lambda a, i: a[i])(vs, tok)
        dist_s = (t[None, :, None] - tok).astype(jnp.float32)
        s = jnp.einsum('bqhd,bqkd->bhqk', qblk, ks_g).astype(jnp.float32) * scale
        s = s - slopes[None, :, None, None] * dist_s[:, None]
        p = masked_softmax(s, (dist_s >= 0)[:, None])
        o_slc = jnp.einsum('bhqk,bqkd->bqhd', p.astype(vs_g.dtype), vs_g)
        kwin = lax.dynamic_slice_in_dim(kw_pad, t0, Q_BLOCK + NSA_WINDOW, axis=1)
        vwin = lax.dynamic_slice_in_dim(vw_pad, t0, Q_BLOCK + NSA_WINDOW, axis=1)
        spos = t0 - NSA_WINDOW + jnp.arange(Q_BLOCK + NSA_WINDOW)
        dist_w = t[:, None] - spos[None, :]
        mask_w = (dist_w >= 0) & (dist_w < NSA_WINDOW) & (spos[None, :] >= 0)
        s = jnp.einsum('bqhd,bkd->bhqk', qblk, kwin).astype(jnp.float32) * scale
        p = masked_softmax(s - slopes[:, None, None] * dist_w.astype(jnp.float32), mask_w)
        o_win = jnp.einsum('bhqk,bkd->bqhd', p.astype(vwin.dtype), vwin)
        return gblk[..., 0:1] * o_cmp + gblk[..., 1:2] * o_slc + gblk[..., 2:3] * o_win

    return sweep_query_blocks(block, S)


def stick_breaking_attention(q, k, v):
    B, S, H, Dh = q.shape
    scale = Dh ** -0.5
    kpos = jnp.arange(S)

    def block(qb):
        t0 = qb * Q_BLOCK
        t = t0 + jnp.arange(Q_BLOCK)
        qblk = lax.dynamic_slice_in_dim(q, t0, Q_BLOCK, axis=1)
        z = jnp.einsum('bqhd,bkhd->bhqk', qblk, k).astype(jnp.float32) * scale
        mask = kpos[None, :] < t[:, None]
        log1m = jnp.where(mask, jax.nn.log_sigmoid(-z), 0.0)
        after = lax.cumsum(log1m, axis=3, reverse=True) - log1m
        a = jnp.where(mask, jnp.exp(jax.nn.log_sigmoid(z) + after), 0.0)
        return jnp.einsum('bhqk,bkhd->bqhd', a.astype(v.dtype), v)

    return sweep_query_blocks(block, S)


def setup_inputs(seed: int = 0) -> dict:
    key = jax.random.key(seed)
    ks = jax.random.split(key, 18)
    f32 = jnp.float32

    def nrm(k, shape, fan_in):
        return jax.random.normal(k, shape, f32) * (fan_in ** -0.5)

    def gain(k, shape):
        return 1.0 + 0.05 * jax.random.normal(k, shape, f32)

    return {
        "x": jax.random.normal(ks[0], (BATCH, SEQ, D_MODEL), f32),
        "norm_g": gain(ks[1], (DEPTH, D_MODEL)),
        "w_in": nrm(ks[2], (DEPTH, D_MODEL, D_IN), D_MODEL),
        "mla_q_norm": gain(ks[3], (DEPTH, MLA_Q_RANK)),
        "mla_w_uq": nrm(ks[4], (DEPTH, MLA_Q_RANK, N_HEADS * (MLA_NOPE_DIM + MLA_ROPE_DIM)), MLA_Q_RANK),
        "mla_kv_norm": gain(ks[5], (DEPTH, MLA_KV_RANK)),
        "mla_w_ukv": nrm(ks[6], (DEPTH, MLA_KV_RANK, N_HEADS * (MLA_NOPE_DIM + MLA_V_DIM)), MLA_KV_RANK),
        "nsa_pos": 0.1 * jax.random.normal(ks[7], (DEPTH, 2, NSA_CMP_BLOCK, NSA_KV_DIM), f32),
        "nsa_w1": nrm(ks[8], (DEPTH, 2, NSA_CMP_BLOCK * NSA_KV_DIM, NSA_CMP_HIDDEN), NSA_CMP_BLOCK * NSA_KV_DIM),
        "nsa_b1": 0.02 * jax.random.normal(ks[9], (DEPTH, 2, NSA_CMP_HIDDEN), f32),
        "nsa_w2": nrm(ks[10], (DEPTH, 2, NSA_CMP_HIDDEN, NSA_KV_DIM), NSA_CMP_HIDDEN),
        "nsa_b2": 0.02 * jax.random.normal(ks[11], (DEPTH, 2, NSA_KV_DIM), f32),
        "w_branch": nrm(ks[12], (DEPTH, N_BRANCHES, BRANCH_WIDTH, D_MODEL), BRANCH_WIDTH),
        "w_merge": nrm(ks[13], (DEPTH, N_BRANCHES, D_MODEL, D_MODEL), D_MODEL),
        "b_merge": 0.02 * jax.random.normal(ks[14], (DEPTH, N_BRANCHES, D_MODEL), f32),
        "w_out": nrm(ks[15], (DEPTH, D_MODEL, D_MODEL), D_MODEL),
        "final_norm_g": gain(ks[16], (D_MODEL,)),
    }


def reference(x, norm_g, w_in, mla_q_norm, mla_w_uq, mla_kv_norm, mla_w_ukv, nsa_pos, nsa_w1,
              nsa_b1, nsa_w2, nsa_b2, w_branch, w_merge, b_merge, w_out, final_norm_g):
    B, S, D = x.shape
    slopes_a, slopes_c = alibi_slopes()
    for l in range(DEPTH):
        h = rmsnorm(x, norm_g[l])
        c = split_columns(h @ w_in[l])
        heads = lambda a: a.reshape(B, S, N_HEADS, HEAD_DIM)
        flat = lambda a: a.reshape(B, S, BRANCH_WIDTH)
        y_a = flat(dilated_window_attention(heads(c["a_q"]), heads(c["a_k"]), heads(c["a_v"]), slopes_a))
        y_a = y_a * jax.nn.silu(c["a_gate"])
        y_b = flat(mla_attention(c["b_cq"], c["b_ckv"], c["b_kpe"], mla_q_norm[l], mla_w_uq[l],
                                 mla_kv_norm[l], mla_w_ukv[l]))
        y_b = y_b * jax.nn.silu(c["b_gate"])
        g_c = jax.nn.sigmoid(c["c_g"]).reshape(B, S, N_HEADS, 3)
        y_c = flat(nsa_attention(heads(c["c_q"]), c["c_kc"], c["c_vc"], c["c_ks"], c["c_vs"],
                                 c["c_kw"], c["c_vw"], g_c, nsa_pos[l], nsa_w1[l], nsa_b1[l],
                                 nsa_w2[l], nsa_b2[l], slopes_c))
        y_c = y_c * jax.nn.silu(c["c_gate"])
        y_d = flat(stick_breaking_attention(heads(c["d_q"]), heads(c["d_k"]), heads(c["d_v"])))
        y_d = y_d * jax.nn.silu(c["d_gate"])
        merged = 0.0
        for i, y in enumerate((y_a, y_b, y_c, y_d)):
            gate = jax.nn.sigmoid(h @ w_merge[l, i] + b_merge[l, i])
            merged = merged + gate * (y @ w_branch[l, i])
        x = x + merged @ w_out[l]
    return rmsnorm(x, final_norm_g)
```

```python
import contextlib
import numpy as np
import ml_dtypes
import concourse.bass as bass
import concourse.mybir as mybir
from concourse.bass_utils import run_bass_kernel_spmd

F32 = mybir.dt.float32
BF16 = mybir.dt.bfloat16
ALU = mybir.AluOpType
AF = mybir.ActivationFunctionType
AX = mybir.AxisListType

D_MODEL = 1024
DEPTH = 2
RMS_EPS = 1e-6
BIG = 30000.0

IN_SPLITS = (
    ("a_q", 256), ("a_k", 256), ("a_v", 256), ("a_gate", 256),
    ("b_cq", 192), ("b_ckv", 128), ("b_kpe", 32), ("b_gate", 256),
    ("c_q", 256), ("c_kc", 64), ("c_vc", 64), ("c_ks", 64),
    ("c_vs", 64), ("c_kw", 64), ("c_vw", 64), ("c_g", 12),
    ("c_gate", 256),
    ("d_q", 256), ("d_k", 256), ("d_v", 256), ("d_gate", 256),
)
OFF = {}
_o = 0
for _n, _w in IN_SPLITS:
    OFF[_n] = _o
    _o += _w
D_IN = _o

ENGS = ("tensor", "vector", "scalar", "gpsimd", "sync")


class Prog:
    NDMA = 32

    def __init__(self, nc, stack):
        self.nc = nc
        self.sem = {e: stack.enter_context(nc.semaphore(f"s_{e}")) for e in ENGS if e != "sync"}
        self.dsem = [stack.enter_context(nc.semaphore(f"d{i}")) for i in range(self.NDMA)]
        self.pbA = stack.enter_context(nc.semaphore("pbA"))
        self.pbB = stack.enter_context(nc.semaphore("pbB"))
        self.nphase = 0
        self.excl = set()
        self.semobj = {("c", e): self.sem[e] for e in self.sem}
        for i in range(self.NDMA):
            self.semobj[("d", i)] = self.dsem[i]
        self.begin()

    def begin(self):
        self.ops = {e: [] for e in ENGS}
        self.lastw = {}
        self.readers = {}
        if not hasattr(self, "cnt"):
            self.cnt = {e: 0 for e in ENGS}
            self.dcnt = [0] * self.NDMA
            self.dnext = [0, 0]
            self.waited = {e: {} for e in ENGS}

    def _wait(self, eng, tok):
        key, val, src = tok
        w = self.waited[eng]
        if w.get(key, 0) >= val:
            return
        w[key] = val
        sem = self.semobj[key]
        self.ops[eng].append(lambda e, sem=sem, val=val: e.wait_ge(sem, val))

    def _deps(self, eng, reads, writes, is_dma=False):
        def need(t):
            return is_dma or t[2] != eng or eng != "tensor"
        for b in reads:
            for t in self.lastw.get(b, {}).values():
                if need(t):
                    self._wait(eng, t)
            if b in self.excl:
                for t in self.readers.get(b, ()):
                    if t[2] != eng:
                        self._wait(eng, t)
        for b in writes:
            for t in self.lastw.get(b, {}).values():
                if need(t):
                    self._wait(eng, t)
            for t in self.readers.get(b, ()):
                if need(t):
                    self._wait(eng, t)

    def _record(self, tok, reads, writes):
        for b in writes:
            if tok[2] == "dma":
                self.lastw.setdefault(b, {})[tok[0]] = tok
            else:
                self.lastw[b] = {tok[0]: tok}
            self.readers[b] = []
        for b in reads:
            self.readers.setdefault(b, []).append(tok)

    def op(self, eng, fn, reads=(), writes=()):
        self._deps(eng, reads, writes)
        self.cnt[eng] += 1
        tok = (("c", eng), self.cnt[eng], eng)
        sem = self.sem[eng]
        self.ops[eng].append(lambda e, fn=fn, sem=sem: fn(e).then_inc(sem, 1))
        self._record(tok, reads, writes)
        return tok

    def dma(self, eng, out, in_, reads=(), writes=(), **kw):
        self._deps(eng, reads, writes, is_dma=True)
        half = self.NDMA // 2
        qi = 0 if eng == "sync" else 1
        i = qi * half + self.dnext[qi]
        self.dnext[qi] = (self.dnext[qi] + 1) % half
        key = ("d", i)
        if self.dcnt[i] > 0:
            self._wait(eng, (key, self.dcnt[i], "dma"))
        self.dcnt[i] += 16
        tok = (key, self.dcnt[i], "dma")
        sem = self.dsem[i]
        self.ops[eng].append(lambda e, out=out, in_=in_, sem=sem, kw=kw: e.dma_start(out=out, in_=in_, **kw).then_inc(sem, 16))
        self._record(tok, reads, writes)
        return tok

    def finish(self):
        final = []
        for e in ENGS:
            if e != "sync" and self.cnt[e] > 0:
                final.append((("c", e), self.cnt[e], e))
        for i in range(self.NDMA):
            if self.dcnt[i] > 0:
                final.append((("d", i), self.dcnt[i], "dma"))
        for e in ENGS:
            for t in final:
                self._wait(e, t)
        ops = self.ops
        with self.nc.Block() as block:
            @block.tensor
            def _(e):
                for f in ops["tensor"]:
                    f(e)

            @block.vector
            def _(e):
                for f in ops["vector"]:
                    f(e)

            @block.scalar
            def _(e):
                for f in ops["scalar"]:
                    f(e)

            @block.gpsimd
            def _(e):
                for f in ops["gpsimd"]:
                    f(e)

            @block.sync
            def _(e):
                for f in ops["sync"]:
                    f(e)
        self.begin()


def _bf(a):
    return np.asarray(a, dtype=np.float32).astype(ml_dtypes.bfloat16)


def make_consts(S):
    c = {}
    k = np.arange(128)[:, None]
    c["ident"] = _bf(np.eye(128))
    j = np.arange(1024)[None, :]
    c["caus"] = _bf(np.where(k <= j - 512, 0.0, -BIG))
    c["antic"] = _bf(np.where(k > j - 512, 0.0, -BIG))
    j = np.arange(256)[None, :]
    c["band"] = _bf(np.where((j - k >= 0) & (j - k <= 128), 0.0, -BIG))
    j = np.arange(2560)[None, :]
    c["cm"] = _bf(np.where(j >= 16 * k + 31, 0.0, -BIG))
    kk = np.arange(S)[None, :]
    c["ewide"] = _bf(np.where(k == kk // 64, BIG, 0.0))
    c["ntri"] = _bf(np.where(k >= np.arange(128)[None, :], -1.0, 0.0))
    c["nones"] = _bf(-np.ones((128, 128)))
    half = 16
    inv = (10000.0 ** (-np.arange(half, dtype=np.float32) / half)).astype(np.float32)
    pos = np.arange(S, dtype=np.float32)
    ang = (pos[None, :] * inv[:, None]).astype(np.float32)
    cos = np.cos(ang).astype(np.float32)
    sin = np.sin(ang).astype(np.float32)
    c["ropec"] = np.concatenate([cos, cos], 0).astype(np.float32)
    c["ropes"] = np.concatenate([-sin, sin], 0).astype(np.float32)
    t = np.arange(S)
    a, b = t // 128, t % 128
    c["kaug"] = _bf(np.stack([np.ones(S), 128.0 * a, np.ones(S), b], 0))
    ncmp = S // 16 - 1
    pc = 16 * np.arange(512) + 31
    ac, bc = pc // 128, pc % 128
    c["kaugc"] = _bf(np.stack([np.ones(512), 128.0 * ac, np.ones(512), bc], 0))
    sl = 2.0 ** (-(np.arange(8) + 1.0))
    sa, sc = sl[0::2], sl[1::2]
    c["qaug_a"] = _bf(np.stack([np.stack([-s * 128.0 * a, s * np.ones(S), -s * b, s * np.ones(S)], 0) for s in sa], 0))
    c["qaug_c"] = _bf(np.stack([np.stack([-s * 128.0 * a, s * np.ones(S), -s * b, s * np.ones(S)], 0) for s in sc], 0))
    qq = np.arange(128)[:, None]
    jj = np.arange(256)[None, :] - 128
    cur = (qq >= 64).astype(np.int64)
    mult = np.where((jj <= cur) & (jj != cur) & (jj != cur - 1), 1.0, 0.0)
    add = np.where(jj > cur, -1.0, np.where(jj == cur, 2e9, np.where(jj == cur - 1, 3e9, 0.0)))
    c["tk_mult"] = mult.astype(np.float32)
    c["tk_add"] = add.astype(np.float32)
    n_sel = S // 64
    sm = np.zeros((512, 129), np.float32)
    for jv in range(min(n_sel, 128)):
        for aa in range(4):
            for cc in range(2):
                n = jv * 4 - aa - cc
                if 0 <= n < ncmp:
                    sm[n, jv] += 1.0
    sm[:, 128] = 1.0
    c["selmap"] = _bf(sm)
    return c


CONST_DT = {"ropec": F32, "ropes": F32, "tk_mult": F32, "tk_add": F32}


class Builder:
    def __init__(self, S, debug=(), n_layers=DEPTH, out_rows=None):
        self.S = S
        self.NG = S // 512
        self.NT = S // 128
        self.debug = set(debug)
        self.n_layers = n_layers
        nc = self.nc = bass.Bass("TRN2", target_bir_lowering=False)
        self.stack = contextlib.ExitStack()
        self.p = Prog(nc, self.stack)
        self.consts_np = make_consts(S)
        self.inp = {}
        self.dr = {}

    def ext_in(self, name, shape, dt=F32):
        t = self.nc.dram_tensor(name, list(shape), dt, kind="ExternalInput").ap()
        self.inp[name] = t
        return t

    def scratch(self, name, shape, dt=BF16):
        kind = "ExternalOutput" if name in self.debug else "Internal"
        t = self.nc.dram_tensor(name, list(shape), dt, kind=kind).ap()
        self.dr[name] = t
        return t

    def declare(self):
        S = self.S
        L = DEPTH
        self.ext_in("x", [S, D_MODEL])
        self.ext_in("norm_g", [L, D_MODEL])
        self.ext_in("w_in", [L, D_MODEL, D_IN])
        self.ext_in("mla_q_norm", [L, 192])
        self.ext_in("mla_w_uq", [L, 192, 384])
        self.ext_in("mla_kv_norm", [L, 128])
        self.ext_in("mla_w_ukv", [L, 128, 512])
        self.ext_in("nsa_pos", [L, 2, 32, 64])
        self.ext_in("nsa_w1", [L, 2, 2048, 128])
        self.ext_in("nsa_b1", [L, 2, 128])
        self.ext_in("nsa_w2", [L, 2, 128, 64])
        self.ext_in("nsa_b2", [L, 2, 64])
        self.ext_in("w_branch", [L, 4, 256, D_MODEL])
        self.ext_in("w_merge", [L, 4, D_MODEL, D_MODEL])
        self.ext_in("b_merge", [L, 4, D_MODEL])
        self.ext_in("w_out", [L, D_MODEL, D_MODEL])
        self.ext_in("final_norm_g", [D_MODEL])
        self.cst = {}
        for k, v in self.consts_np.items():
            self.cst[k] = self.ext_in("c_" + k, v.shape, CONST_DT.get(k, BF16))
        sc = self.scratch
        sc("hT", [D_MODEL, S])
        for m in "acd":
            sc(f"q{m}T", [256, S])
            sc(f"g{m}T", [256, S])
        sc("gbT", [256, S])
        sc("kaT", [256, S]); sc("kdT", [256, S])
        sc("va", [S, 256]); sc("vd", [S, 256]); sc("vb", [S, 256])
        sc("vs", [S, 64]); sc("vw", [S, 64])
        sc("kcT", [64, S]); sc("vcT", [64, S]); sc("ksT", [64, S]); sc("kwT", [64, S])
        sc("cgT", [12, S])
        sc("qbT", [4, 96, S]); sc("kbT", [4, 96, S])
        for m in "abcd":
            sc(f"y{m}T", [256, S])
        sc("x1", [S, D_MODEL], F32)
        self.out = self.nc.dram_tensor("out", [S, D_MODEL], F32, kind="ExternalOutput").ap()

    def phase_proj(self, l, x_src):
        nc, p, S = self.nc, self.p, self.S
        I, C, D = self.inp, self.cst, self.dr
        with contextlib.ExitStack() as st:
            def T(name, shape, dt):
                return st.enter_context(nc.sbuf_tensor(f"P{l}_{name}", list(shape), dt))

            def PS(name, shape, dt=F32):
                p.excl.add(name)
                return st.enter_context(nc.psum_tensor(f"P{l}_{name}", list(shape), dt))

            Win = T("Win", [128, 8, D_IN], BF16)
            wst = [T(f"wst{i}", [128, D_IN], F32) for i in range(2)]
            gT = T("gT", [128, 8], F32)
            ident = T("ident", [128, 128], BF16)
            Wuq = T("Wuq", [128, 2, 384], BF16)
            Wuqs = T("Wuqs", [128, 2, 384], BF16)
            uqst = T("uqst", [128, 2, 384], F32)
            gq = T("gq", [128, 2], F32)
            Wk = T("Wk", [128, 4, 64], BF16)
            Wv = T("Wv", [128, 4, 64], BF16)
            kvst = T("kvst", [128, 4, 128], F32)
            gkv = T("gkv", [128, 1], F32)
            Wkpe = T("Wkpe", [128, 8, 96], BF16)
            xst = [T(f"xst{i}", [128, D_MODEL], F32) for i in range(2)]
            sq = T("sq", [128, D_MODEL], F32)
            xn = [T(f"xn{i}", [128, D_MODEL], BF16) for i in range(2)]
            ss = T("ss", [128, 4], F32)
            rs = T("rs", [128, 4], F32)
            hTgs = [T(f"hTg{i}", [128, 8, 512], BF16) for i in range(2)]
            fst = [T(f"fst{i}", [128, 512], BF16) for i in range(3)]
            vst = [T(f"vst{i}", [128, 512], BF16) for i in range(2)]
            v2st = [T(f"v2st{i}", [128, 128], BF16) for i in range(2)]
            vbst = [T(f"vbst{i}", [128, 256], BF16) for i in range(2)]
            lat = T("lat", [128, 320], F32)
            cqn = T("cqn", [128, 192], BF16)
            ckvn = T("ckvn", [128, 128], BF16)
            cqnT = T("cqnT", [128, 2, 512], BF16)
            ckvnT = T("ckvnT", [128, 512], BF16)
            ropec = [T(f"ropec{i}", [96, 512], F32) for i in range(2)]
            ropes = [T(f"ropes{i}", [96, 512], F32) for i in range(2)]
            t1 = T("t1", [96, 512], F32)
            t2 = T("t2", [96, 512], F32)
            qbst = [T(f"qbst{i}", [96, 512], BF16) for i in range(2)]
            kbst = [T(f"kbst{i}", [64, 512], BF16) for i in range(2)]
            kpst = T("kpst", [96, 512], BF16)

            Fps = [PS(f"F{i}", [128, 512]) for i in range(2)]
            Tps = [PS(f"T{i}", [128, 1024], BF16) for i in range(2)]
            Vps = PS("V", [128, 512])
            Lps = PS("L", [128, 512])
            Mps = PS("M", [128, 1024], BF16)
            M2ps = PS("M2", [128, 512])

            p.dma("sync", ident[:], C["ident"], writes=["ident"])
            p.dma("sync", gT[:], I["norm_g"][l].rearrange("(c p) -> p c", p=128), writes=["gT"],
                  allow_slow_non_contiguous=True)
            for c in range(8):
                b = c % 2
                p.dma("sync", wst[b][:], I["w_in"][l, c * 128:(c + 1) * 128, :], writes=[f"wst{b}"])
                p.op("vector", lambda e, c=c, b=b: e.tensor_scalar(out=Win[:, c, :], in0=wst[b][:], scalar1=gT[:, c:c + 1], scalar2=None, op0=ALU.mult),
                     reads=[f"wst{b}", "gT"], writes=["Win"])
            k0 = OFF["b_kpe"] - 64
            p.op("vector", lambda e: e.tensor_copy(out=Wkpe[:, :, 0:64], in_=Win[:, :, k0:k0 + 64]), reads=["Win"], writes=["Wkpe"])
            p.op("vector", lambda e: e.tensor_copy(out=Wkpe[:, :, 64:80], in_=Win[:, :, k0 + 80:k0 + 96]), reads=["Win"], writes=["Wkpe"])
            p.op("vector", lambda e: e.tensor_copy(out=Wkpe[:, :, 80:96], in_=Win[:, :, k0 + 64:k0 + 80]), reads=["Win"], writes=["Wkpe"])
            p.dma("sync", gq[:, 0:1], I["mla_q_norm"][l, 0:128].rearrange("(p o) -> p o", o=1), writes=["gq"], allow_slow_non_contiguous=True)
            p.dma("sync", gq[0:64, 1:2], I["mla_q_norm"][l, 128:192].rearrange("(p o) -> p o", o=1), writes=["gq"], allow_slow_non_contiguous=True)
            p.dma("sync", uqst[:, 0, :], I["mla_w_uq"][l, 0:128, :], writes=["uqst"])
            p.dma("sync", uqst[0:64, 1, :], I["mla_w_uq"][l, 128:192, :], writes=["uqst"])
            p.op("vector", lambda e: e.tensor_scalar(out=Wuq[:, 0, :], in0=uqst[:, 0, :], scalar1=gq[:, 0:1], scalar2=None, op0=ALU.mult), reads=["uqst", "gq"], writes=["Wuq"])
            p.op("vector", lambda e: e.tensor_scalar(out=Wuq[0:64, 1, :], in0=uqst[0:64, 1, :], scalar1=gq[0:64, 1:2], scalar2=None, op0=ALU.mult), reads=["uqst", "gq"], writes=["Wuq"])
            p.op("vector", lambda e: e.tensor_copy(out=Wuqs[:, 0, :], in_=Wuq[:, 0, :]), reads=["Wuq"], writes=["Wuqs"])
            p.op("vector", lambda e: e.tensor_copy(out=Wuqs[0:64, 1, :], in_=Wuq[0:64, 1, :]), reads=["Wuq"], writes=["Wuqs"])
            for h in range(4):
                o = h * 96 + 64
                for (dlo, slo) in ((o, o + 16), (o + 16, o)):
                    p.op("vector", lambda e, dlo=dlo, slo=slo: e.tensor_copy(out=Wuqs[:, 0, dlo:dlo + 16], in_=Wuq[:, 0, slo:slo + 16]), reads=["Wuq"], writes=["Wuqs"])
                    p.op("vector", lambda e, dlo=dlo, slo=slo: e.tensor_copy(out=Wuqs[0:64, 1, dlo:dlo + 16], in_=Wuq[0:64, 1, slo:slo + 16]), reads=["Wuq"], writes=["Wuqs"])
            p.dma("sync", gkv[:, 0:1], I["mla_kv_norm"][l].rearrange("(p o) -> p o", o=1), writes=["gkv"], allow_slow_non_contiguous=True)
            p.dma("sync", kvst[:], I["mla_w_ukv"][l].rearrange("r (h c) -> r h c", h=4), writes=["kvst"])
            p.op("vector", lambda e: e.tensor_scalar(out=Wk[:], in0=kvst[:, :, 0:64], scalar1=gkv[:, 0:1], scalar2=None, op0=ALU.mult), reads=["kvst", "gkv"], writes=["Wk"])
            p.op("vector", lambda e: e.tensor_scalar(out=Wv[:], in0=kvst[:, :, 64:128], scalar1=gkv[:, 0:1], scalar2=None, op0=ALU.mult), reads=["kvst", "gkv"], writes=["Wv"])

            if getattr(self, "stop", 0) == 1:
                p.finish(); return
            FM = []
            for m, nm in (("a", "a_q"), ("c", "c_q"), ("d", "d_q")):
                for cc in range(2):
                    FM.append((OFF[nm] + cc * 128, 128, "copy", 0.125, [(f"q{m}T", cc * 128, 0, 128)]))
            for dst, nm in (("kaT", "a_k"), ("kdT", "d_k")):
                for cc in range(2):
                    FM.append((OFF[nm] + cc * 128, 128, "copy", 1.0, [(dst, cc * 128, 0, 128)]))
            FM.append((OFF["c_kc"], 128, "copy", 1.0, [("kcT", 0, 0, 64), ("vcT", 0, 64, 128)]))
            FM.append((OFF["c_ks"], 128, "copy", 1.0, [("ksT", 0, 0, 64)]))
            FM.append((OFF["c_kw"], 128, "copy", 1.0, [("kwT", 0, 0, 64)]))
            for dst, nm in (("gaT", "a_gate"), ("gbT", "b_gate"), ("gcT", "c_gate"), ("gdT", "d_gate")):
                for cc in range(2):
                    FM.append((OFF[nm] + cc * 128, 128, "silu", 1.0, [(dst, cc * 128, 0, 128)]))
            FM.append((OFF["c_g"], 12, "sigmoid", 1.0, [("cgT", 0, 0, 12)]))

            fcount = [0]

            def fm_chunk(g, col0, ncols, kind, scale, dsts, wsrc=None):
                hTg, hk = hTgs[g % 2], f"hTg{g % 2}"
                i = fcount[0]
                fcount[0] += 1
                fb = i % 2
                sb = i % 3
                for c in range(8):
                    lhs = Win[:, c, col0:col0 + ncols] if wsrc is None else wsrc[:, c, :]
                    p.op("tensor", lambda e, c=c, lhs=lhs, fb=fb, hTg=hTg: e.matmul(Fps[fb][0:ncols, :], lhsT=lhs, rhs=hTg[:, c, :], start=(c == 0), stop=(c == 7)),
                         reads=[hk, "Win", "Wkpe"], writes=[f"F{fb}"])
                return fb, sb

            for g in range(self.NG):
                c0 = g * 512
                rb = g % 2
                hTg, hk = hTgs[g % 2], f"hTg{g % 2}"
                p.dma("sync", ropec[rb][64:96, :], C["ropec"][:, c0:c0 + 512], writes=[f"ropec{rb}"])
                p.dma("sync", ropes[rb][64:96, :], C["ropes"][:, c0:c0 + 512], writes=[f"ropes{rb}"])
                for t in range(4):
                    r0 = c0 + t * 128
                    xb = t % 2
                    p.dma("sync", xst[xb][:], x_src[r0:r0 + 128, :], writes=[f"xst{xb}"])
                    p.op("scalar", lambda e, xb=xb: e.activation(out=sq[:], in_=xst[xb][:], func=AF.Square), reads=[f"xst{xb}"], writes=["sq"])
                    p.op("vector", lambda e, t=t: e.reduce_sum(out=ss[:, t:t + 1], in_=sq[:], axis=AX.X), reads=["sq"], writes=["ss"])
                    p.op("vector", lambda e, t=t: e.tensor_scalar(out=rs[:, t:t + 1], in0=ss[:, t:t + 1], scalar1=1.0 / D_MODEL, scalar2=RMS_EPS, op0=ALU.mult, op1=ALU.add), reads=["ss"], writes=["rs"])
                    p.op("scalar", lambda e, t=t: e.sqrt(out=rs[:, t:t + 1], in_=rs[:, t:t + 1]), reads=["rs"], writes=["rs"])
                    p.op("vector", lambda e, t=t: e.reciprocal(out=rs[:, t:t + 1], in_=rs[:, t:t + 1]), reads=["rs"], writes=["rs"])
                    p.op("scalar", lambda e, xb=xb, t=t: e.activation(out=xn[xb][:], in_=xst[xb][:], func=AF.Copy, scale=rs[:, t:t + 1]), reads=[f"xst{xb}", "rs"], writes=[f"xn{xb}"])
                    for c in range(8):
                        p.op("tensor", lambda e, c=c, xb=xb: e.transpose(Tps[xb][:, c * 128:(c + 1) * 128], xn[xb][:, c * 128:(c + 1) * 128], ident[:]),
                             reads=[f"xn{xb}", "ident"], writes=[f"T{xb}"])
                    p.op("vector", lambda e, xb=xb, t=t, hTg=hTg: e.tensor_copy(out=hTg[:, :, t * 128:(t + 1) * 128], in_=Tps[xb][:].rearrange("p (c s) -> p c s", c=8)),
                         reads=[f"T{xb}"], writes=[hk])
                p.dma("gpsimd", D["hT"].rearrange("(c p) s -> p c s", p=128)[:, :, c0:c0 + 512], hTg[:], reads=[hk])
                if getattr(self, "stop", 0) == 2:
                    continue
                for (col0, ncols, kind, scale, dsts) in FM:
                    fb, sb = fm_chunk(g, col0, ncols, kind, scale, dsts)
                    if kind == "copy":
                        p.op("vector", lambda e, fb=fb, sb=sb, ncols=ncols, scale=scale: e.tensor_scalar(out=fst[sb][0:ncols, :], in0=Fps[fb][0:ncols, :], scalar1=scale, scalar2=None, op0=ALU.mult),
                             reads=[f"F{fb}"], writes=[f"fst{sb}"])
                    else:
                        fn = AF.Silu if kind == "silu" else AF.Sigmoid
                        p.op("scalar", lambda e, fb=fb, sb=sb, ncols=ncols, fn=fn: e.activation(out=fst[sb][0:ncols, :], in_=Fps[fb][0:ncols, :], func=fn),
                             reads=[f"F{fb}"], writes=[f"fst{sb}"])
                    for (dst, dr0, rlo, rhi) in dsts:
                        p.dma("gpsimd", D[dst][dr0:dr0 + (rhi - rlo), c0:c0 + 512], fst[sb][rlo:rhi, :], reads=[f"fst{sb}"])
                if getattr(self, "stop", 0) == 3:
                    continue
                fb1, _ = fm_chunk(g, k0, 96, "copy", 1.0, None)
                p.op("vector", lambda e, fb1=fb1, rb=rb: e.tensor_tensor(out=t1[64:96, :], in0=Fps[fb1][64:96, :], in1=ropec[rb][64:96, :], op=ALU.mult),
                     reads=[f"F{fb1}", f"ropec{rb}"], writes=["t1"])
                fb2, _ = fm_chunk(g, 0, 96, "copy", 1.0, None, wsrc=Wkpe)
                p.op("vector", lambda e, fb2=fb2, rb=rb: e.tensor_tensor(out=t2[64:96, :], in0=Fps[fb2][64:96, :], in1=ropes[rb][64:96, :], op=ALU.mult),
                     reads=[f"F{fb2}", f"ropes{rb}"], writes=["t2"])
                p.op("vector", lambda e: e.tensor_tensor(out=kpst[64:96, :], in0=t1[64:96, :], in1=t2[64:96, :], op=ALU.add), reads=["t1", "t2"], writes=["kpst"])
                for h in range(4):
                    p.dma("gpsimd", D["kbT"][h, 64:96, c0:c0 + 512], kpst[64:96, :], reads=["kpst"])
                if getattr(self, "stop", 0) == 4:
                    continue
                for t in range(4):
                    r0 = c0 + t * 128
                    vb = t % 2
                    lhs_t = lambda c, t=t: hTg[:, c, t * 128:(t + 1) * 128]
                    for (ps, pname, o0, col0, ncols) in ((Vps, "V", 0, OFF["a_v"], 256), (Vps, "V", 256, OFF["d_v"], 256),
                                                         (Lps, "L", 0, OFF["b_cq"], 352), (Lps, "L", 352, OFF["c_vs"], 64), (Lps, "L", 416, OFF["c_vw"], 64)):
                        for c in range(8):
                            p.op("tensor", lambda e, c=c, ps=ps, o0=o0, col0=col0, ncols=ncols, t=t, hTg=hTg: e.matmul(ps[:, o0:o0 + ncols], lhsT=hTg[:, c, t * 128:(t + 1) * 128], rhs=Win[:, c, col0:col0 + ncols], start=(c == 0), stop=(c == 7)),
                                 reads=[hk, "Win"], writes=[pname])
                    p.op("scalar", lambda e, vb=vb: e.copy(out=vst[vb][:], in_=Vps[:]), reads=["V"], writes=[f"vst{vb}"])
                    p.dma("gpsimd", D["va"][r0:r0 + 128, :], vst[vb][:, 0:256], reads=[f"vst{vb}"])
                    p.dma("gpsimd", D["vd"][r0:r0 + 128, :], vst[vb][:, 256:512], reads=[f"vst{vb}"])
                    p.op("scalar", lambda e, vb=vb: e.copy(out=v2st[vb][:], in_=Lps[:, 352:480]), reads=["L"], writes=[f"v2st{vb}"])
                    p.dma("gpsimd", D["vs"][r0:r0 + 128, :], v2st[vb][:, 0:64], reads=[f"v2st{vb}"])
                    p.dma("gpsimd", D["vw"][r0:r0 + 128, :], v2st[vb][:, 64:128], reads=[f"v2st{vb}"])
                    if getattr(self, "stop", 0) == 5:
                        continue
                    p.op("scalar", lambda e: e.copy(out=lat[:], in_=Lps[:, 0:320]), reads=["L"], writes=["lat"])
                    if getattr(self, "stop", 0) == 63:
                        continue
                    p.op("scalar", lambda e: e.activation(out=sq[:, 0:320], in_=lat[:], func=AF.Square), reads=["lat"], writes=["sq"])
                    if getattr(self, "stop", 0) == 64:
                        continue
                    p.op("vector", lambda e: e.reduce_sum(out=ss[:, 0:1], in_=sq[:, 0:192], axis=AX.X), reads=["sq"], writes=["ss"])
                    p.op("vector", lambda e: e.reduce_sum(out=ss[:, 1:2], in_=sq[:, 192:320], axis=AX.X), reads=["sq"], writes=["ss"])
                    if getattr(self, "stop", 0) == 61:
                        continue
                    p.op("vector", lambda e: e.tensor_scalar(out=rs[:, 0:1], in0=ss[:, 0:1], scalar1=1.0 / 192, scalar2=RMS_EPS, op0=ALU.mult, op1=ALU.add), reads=["ss"], writes=["rs"])
                    p.op("vector", lambda e: e.tensor_scalar(out=rs[:, 1:2], in0=ss[:, 1:2], scalar1=1.0 / 128, scalar2=RMS_EPS, op0=ALU.mult, op1=ALU.add), reads=["ss"], writes=["rs"])
                    p.op("scalar", lambda e: e.sqrt(out=rs[:, 0:2], in_=rs[:, 0:2]), reads=["rs"], writes=["rs"])
                    p.op("vector", lambda e: e.reciprocal(out=rs[:, 0:2], in_=rs[:, 0:2]), reads=["rs"], writes=["rs"])
                    if getattr(self, "stop", 0) == 62:
                        continue
                    p.op("vector", lambda e: e.tensor_scalar(out=cqn[:], in0=lat[:, 0:192], scalar1=rs[:, 0:1], scalar2=None, op0=ALU.mult), reads=["lat", "rs"], writes=["cqn"])
                    p.op("vector", lambda e: e.tensor_scalar(out=ckvn[:], in0=lat[:, 192:320], scalar1=rs[:, 1:2], scalar2=None, op0=ALU.mult), reads=["lat", "rs"], writes=["ckvn"])
                    if getattr(self, "stop", 0) == 6:
                        continue
                    p.op("tensor", lambda e: e.transpose(Mps[:, 0:128], cqn[:, 0:128], ident[:]), reads=["cqn", "ident"], writes=["M"])
                    p.op("tensor", lambda e: e.transpose(Mps[0:64, 128:256], cqn[:, 128:192], ident[:]), reads=["cqn", "ident"], writes=["M"])
                    p.op("tensor", lambda e: e.transpose(Mps[:, 256:384], ckvn[:], ident[:]), reads=["ckvn", "ident"], writes=["M"])
                    p.op("vector", lambda e, t=t: e.tensor_copy(out=cqnT[:, 0, t * 128:(t + 1) * 128], in_=Mps[:, 0:128]), reads=["M"], writes=["cqnT"])
                    p.op("vector", lambda e, t=t: e.tensor_copy(out=cqnT[0:64, 1, t * 128:(t + 1) * 128], in_=Mps[0:64, 128:256]), reads=["M"], writes=["cqnT"])
                    p.op("vector", lambda e, t=t: e.tensor_copy(out=ckvnT[:, t * 128:(t + 1) * 128], in_=Mps[:, 256:384]), reads=["M"], writes=["ckvnT"])
                    if getattr(self, "stop", 0) == 7:
                        continue
                    p.op("tensor", lambda e, t=t: e.matmul(M2ps[:, 0:256], lhsT=ckvnT[:, t * 128:(t + 1) * 128], rhs=Wv[:].rearrange("p h c -> p (h c)"), start=True, stop=True),
                         reads=["ckvnT", "Wv"], writes=["M2"])
                    p.op("scalar", lambda e, vb=vb: e.copy(out=vbst[vb][:], in_=M2ps[:, 0:256]), reads=["M2"], writes=[f"vbst{vb}"])
                    p.dma("gpsimd", D["vb"][r0:r0 + 128, :], vbst[vb][:], reads=[f"vbst{vb}"])
                if getattr(self, "stop", 0) in (5, 6, 7, 8, 61, 62, 63, 64):
                    continue
                for h in range(4):
                    qb = h % 2
                    hc = slice(h * 96, (h + 1) * 96)
                    for (W, pname, ps) in ((Wuq, "M2", M2ps), (Wuqs, "L", Lps)):
                        p.op("tensor", lambda e, W=W, ps=ps, hc=hc: e.matmul(ps[0:96, :], lhsT=W[:, 0, hc], rhs=cqnT[:, 0, :], start=True, stop=False), reads=["Wuq", "Wuqs", "cqnT"], writes=[pname])
                        p.op("tensor", lambda e, W=W, ps=ps, hc=hc: e.matmul(ps[0:96, :], lhsT=W[0:64, 1, hc], rhs=cqnT[0:64, 1, :], start=False, stop=True), reads=["Wuq", "Wuqs", "cqnT"], writes=[pname])
                    p.op("scalar", lambda e, qb=qb: e.copy(out=qbst[qb][0:64, :], in_=M2ps[0:64, :]), reads=["M2"], writes=[f"qbst{qb}"])
                    p.op("vector", lambda e, rb=rb: e.tensor_tensor(out=t1[64:96, :], in0=M2ps[64:96, :], in1=ropec[rb][64:96, :], op=ALU.mult), reads=["M2", f"ropec{rb}"], writes=["t1"])
                    p.op("vector", lambda e, rb=rb: e.tensor_tensor(out=t2[64:96, :], in0=Lps[64:96, :], in1=ropes[rb][64:96, :], op=ALU.mult), reads=["L", f"ropes{rb}"], writes=["t2"])
                    p.op("vector", lambda e, qb=qb: e.tensor_tensor(out=qbst[qb][64:96, :], in0=t1[64:96, :], in1=t2[64:96, :], op=ALU.add), reads=["t1", "t2"], writes=[f"qbst{qb}"])
                    p.dma("gpsimd", D["qbT"][h, :, c0:c0 + 512], qbst[qb][:], reads=[f"qbst{qb}"])
                    p.op("tensor", lambda e, h=h: e.matmul(Vps[0:64, :], lhsT=Wk[:, h, :], rhs=ckvnT[:], start=True, stop=True), reads=["Wk", "ckvnT"], writes=["V"])
                    p.op("scalar", lambda e, qb=qb: e.copy(out=kbst[qb][:], in_=Vps[0:64, :]), reads=["V"], writes=[f"kbst{qb}"])
                    p.dma("gpsimd", D["kbT"][h, 0:64, c0:c0 + 512], kbst[qb][:], reads=[f"kbst{qb}"])
            p.finish()

    def _attn_common(self, st, tag, ns=4):
        nc, p = self.nc, self.p
        C = self.cst

        def T(name, shape, dt):
            return st.enter_context(nc.sbuf_tensor(f"{tag}_{name}", list(shape), dt))

        def PS(name, shape, dt=F32):
            p.excl.add(name)
            return st.enter_context(nc.psum_tensor(f"{tag}_{name}", list(shape), dt))

        cm = dict(T=T, PS=PS)
        cm["ident"] = T("ident", [128, 128], BF16)
        cm["zeros"] = T("zeros", [128, 128], BF16)
        cm["caus"] = T("caus", [128, 1024], BF16)
        p.dma("sync", cm["ident"][:], C["ident"], writes=["ident"])
        p.dma("sync", cm["caus"][:], C["caus"], writes=["caus"])
        p.op("vector", lambda e: e.memset(cm["zeros"][:], 0.0), writes=["zeros"])
        cm["Sps"] = [PS(f"S{i}", [128, 512]) for i in range(ns)]
        cm["accs"] = [(PS("acc", [128, 512]), "acc"), (PS("acc2", [128, 512]), "acc2")]
        cm["acc"] = cm["accs"][0][0]
        self._acc_i = 0
        cm["Pt"] = [T(f"Pt{i}", [128, 512], BF16) for i in range(ns)]
        cm["ns"] = ns
        cm["skew"] = 2 if ns <= 4 else 3
        cm["rinv"] = T("rinv", [64, 512], F32)
        cm["ytmp"] = T("ytmp", [64, 512], F32)
        cm["yst"] = [T(f"yst{i}", [64, 512], BF16) for i in range(2)]
        self._step = 0
        return cm

    def _next_acc(self, cm):
        self._acc_i += 1
        return cm["accs"][self._acc_i % 2]

    def _run_tile(self, cm, blocks, exp_scale, acc=None, acckey="acc"):
        p = self.p
        acc = cm["acc"] if acc is None else acc
        Sps, Pt, zeros, caus = cm["Sps"], cm["Pt"], cm["zeros"], cm["caus"]
        p.op("tensor", lambda e: e.matmul(acc[:, :], lhsT=zeros[:], rhs=caus[:, 0:512], start=True, stop=False),
             reads=["zeros", "caus"], writes=[acckey])

        def pv(bl, sb, last):
            p.op("tensor", lambda e, bl=bl, sb=sb: e.matmul(bl["pv_out"], lhsT=bl["pv_lhsT"], rhs=Pt[sb][0:bl["kr"], 0:bl["n"]], start=False, stop=False),
                 reads=[f"Pt{sb}"] + bl["rd"], writes=[acckey])

        ns = cm["ns"]
        pend = []
        for i, bl in enumerate(blocks):
            sb = self._step % ns
            self._step += 1
            nq = len(bl["qk"])
            for j, (lh, rh) in enumerate(bl["qk"]):
                p.op("tensor", lambda e, lh=lh, rh=rh, sb=sb, bl=bl, j=j, nq=nq: e.matmul(Sps[sb][0:bl["kr"], 0:bl["n"]], lhsT=lh, rhs=rh, start=(j == 0), stop=(j == nq - 1)),
                     reads=bl["rd"] + ["ident", "caus"], writes=[f"S{sb}"])
            p.op("scalar", lambda e, sb=sb, bl=bl: e.activation(out=Pt[sb][0:bl["kr"], 0:bl["n"]], in_=Sps[sb][0:bl["kr"], 0:bl["n"]], func=AF.Exp, scale=exp_scale),
                 reads=[f"S{sb}"], writes=[f"Pt{sb}"])
            pend.append((bl, sb))
            if len(pend) > cm.get("skew", 2):
                pv(pend[0][0], pend[0][1], False)
                pend.pop(0)
        for (bl, sb) in pend:
            pv(bl, sb, False)
        p.op("tensor", lambda e: e.matmul(acc[:, :], lhsT=zeros[:], rhs=caus[:, 0:512], start=False, stop=True),
             reads=["zeros", "caus"], writes=[acckey])

    def _finish_tile(self, cm, g, h, gate_tile, dst, acc=None, acckey="acc", norm=True, gkey="GT"):
        p = self.p
        acc = cm["acc"] if acc is None else acc
        rinv, ytmp = cm["rinv"], cm["ytmp"]
        yb = (g + h) % 2
        yst = cm["yst"][yb]
        c0 = g * 512
        if norm:
            p.op("vector", lambda e: e.tensor_scalar(out=rinv[:], in0=acc[64:128, :], scalar1=1e-30, scalar2=None, op0=ALU.max), reads=[acckey], writes=["rinv"])
            p.op("vector", lambda e: e.reciprocal(out=rinv[:], in_=rinv[:]), reads=["rinv"], writes=["rinv"])
            p.op("vector", lambda e: e.tensor_tensor(out=ytmp[:], in0=acc[0:64, :], in1=rinv[:], op=ALU.mult), reads=[acckey, "rinv"], writes=["ytmp"])
            p.op("gpsimd", lambda e: e.tensor_tensor(out=yst[:], in0=ytmp[:], in1=gate_tile[0:64, c0:c0 + 512], op=ALU.mult), reads=["ytmp", gkey], writes=[f"yst{yb}"])
        else:
            p.op("vector", lambda e: e.tensor_tensor(out=yst[:], in0=acc[0:64, :], in1=gate_tile[0:64, c0:c0 + 512], op=ALU.mult), reads=[acckey, gkey], writes=[f"yst{yb}"])
        p.dma("gpsimd", dst[h * 64:(h + 1) * 64, c0:c0 + 512], yst[:], reads=[f"yst{yb}"])

    def _load_vaug(self, Vt, key, src, h, dil, pieces=8):
        p, NT = self.p, self.NT
        p.op("gpsimd", lambda e: e.memset(Vt[:, :, 64:128], 1.0), writes=[key])
        njb = NT // dil
        view = src[:, h * 64:(h + 1) * 64].rearrange("(jb kk r) c -> kk jb r c", kk=128, r=dil)
        tv = Vt[:, :, 0:64].rearrange("p (jb r) c -> p jb r c", r=dil)
        step = max(1, njb // pieces) if dil == 1 else 1
        for j0 in range(0, njb, step):
            p.dma("sync", tv[:, j0:j0 + step], view[:, j0:j0 + step], writes=[key])

    def phase_a(self, l):
        nc, p, S = self.nc, self.p, self.S
        C, D = self.cst, self.dr
        with contextlib.ExitStack() as st:
            cm = self._attn_common(st, f"A{l}")
            T = cm["T"]
            band = T("band", [128, 256], BF16)
            p.dma("sync", band[:], C["band"], writes=["band"])
            KT = T("KT", [68, S], BF16)
            QT = T("QT", [68, S], BF16)
            GT = T("GT", [64, S], BF16)
            Vd = {d: T(f"V{d}", [128, self.NT, 128], BF16) for d in (1, 4, 16)}
            ident = cm["ident"]
            for h in range(4):
                p.dma("sync", KT[0:64, :], D["kaT"][h * 64:(h + 1) * 64, :], writes=["KT"])
                p.dma("sync", KT[64:68, :], C["kaug"], writes=["KT"])
                p.dma("sync", QT[0:64, :], D["qaT"][h * 64:(h + 1) * 64, :], writes=["QT"])
                p.dma("sync", QT[64:68, :], C["qaug_a"][h], writes=["QT"])
                p.dma("sync", GT[:], D["gaT"][h * 64:(h + 1) * 64, :], writes=["GT"])
                for d in (1, 4, 16):
                    self._load_vaug(Vd[d], f"V{d}", D["va"], h, d)
                for g in range(self.NG):
                    blocks = []
                    q0 = 512 * g
                    accx, acck = self._next_acc(cm)
                    for kb in range(4 * g - 1, 4 * g + 4):
                        if kb < 0:
                            continue
                        lo = max(0, 128 * kb - q0)
                        hi = min(512, 128 * kb - q0 + 256)
                        b0 = lo + q0 - 128 * kb
                        n = hi - lo
                        blocks.append(dict(qk=[(KT[:, 128 * kb:128 * kb + 128], QT[:, q0 + lo:q0 + hi]), (ident[:], band[:, b0:b0 + n])],
                                           kr=128, n=n, pv_lhsT=Vd[1][:, kb, :], pv_out=accx[:, lo:hi], rd=["KT", "QT", "band", "V1"]))
                    for r in range(4):
                        for jb in (g - 1, g):
                            if jb < 0:
                                continue
                            b0 = 0 if jb == g else 128
                            blocks.append(dict(qk=[(KT[:, 512 * jb + r:512 * jb + 512:4], QT[:, q0 + r:q0 + 512:4]), (ident[:], band[:, b0:b0 + 128])],
                                               kr=128, n=128, pv_lhsT=Vd[4][:, jb * 4 + r, :], pv_out=accx[:, r:512:4], rd=["KT", "QT", "band", "V4"]))
                    jb0, o = g // 4, 32 * (g % 4)
                    for r in range(16):
                        for jb in (jb0 - 1, jb0):
                            if jb < 0:
                                continue
                            b0 = o if jb == jb0 else 128 + o
                            blocks.append(dict(qk=[(KT[:, 2048 * jb + r:2048 * jb + 2048:16], QT[:, q0 + r:q0 + 512:16]), (ident[:], band[:, b0:b0 + 32])],
                                               kr=128, n=32, pv_lhsT=Vd[16][:, jb * 16 + r, :], pv_out=accx[:, r:512:16], rd=["KT", "QT", "band", "V16"]))
                    self._run_tile(cm, blocks, 1.0, acc=accx, acckey=acck)
                    self._finish_tile(cm, g, h, GT, D["yaT"], acc=accx, acckey=acck)
            p.finish()

    def phase_b(self, l):
        nc, p, S = self.nc, self.p, self.S
        C, D = self.cst, self.dr
        with contextlib.ExitStack() as st:
            cm = self._attn_common(st, f"B{l}", ns=6)
            T = cm["T"]
            KTs = [T(f"KT{i}", [96, S], BF16) for i in range(2)]
            QTs = [T(f"QT{i}", [96, S], BF16) for i in range(2)]
            GTs = [T(f"GT{i}", [64, S], BF16) for i in range(2)]
            V1s = [T(f"V1{i}", [128, self.NT, 128], BF16) for i in range(2)]
            ident, caus = cm["ident"], cm["caus"]
            for h in range(4):
                hb = h % 2
                KT, QT, GT, V1 = KTs[hb], QTs[hb], GTs[hb], V1s[hb]
                kK, kQ, kG, kV = f"KT{hb}", f"QT{hb}", f"GT{hb}", f"V1{hb}"
                p.dma("sync", KT[:], D["kbT"][h], writes=[kK])
                p.dma("sync", QT[:], D["qbT"][h], writes=[kQ])
                p.dma("sync", GT[:], D["gbT"][h * 64:(h + 1) * 64, :], writes=[kG])
                self._load_vaug(V1, kV, D["vb"], h, 1)
                for g in range(self.NG):
                    blocks = []
                    q0 = 512 * g
                    accx, acck = self._next_acc(cm)
                    for kb in range(0, 4 * g + 4):
                        o = kb - 4 * g
                        if o < 0:
                            blocks.append(dict(qk=[(KT[:, 128 * kb:128 * kb + 128], QT[:, q0:q0 + 512])], kr=128, n=512,
                                               pv_lhsT=V1[:, kb, :], pv_out=accx[:, :], rd=[kK, kQ, kV]))
                        else:
                            lo = 128 * o
                            n = 512 - lo
                            blocks.append(dict(qk=[(KT[:, 128 * kb:128 * kb + 128], QT[:, q0 + lo:q0 + 512]), (ident[:], caus[:, 512:512 + n])], kr=128, n=n,
                                               pv_lhsT=V1[:, kb, :], pv_out=accx[:, lo:512], rd=[kK, kQ, kV]))
                    self._run_tile(cm, blocks, 96.0 ** -0.5, acc=accx, acckey=acck)
                    self._finish_tile(cm, g, h, GT, D["ybT"], acc=accx, acckey=acck, gkey=kG)
            p.finish()

    def phase_d(self, l):
        nc, p, S = self.nc, self.p, self.S
        C, D = self.cst, self.dr
        with contextlib.ExitStack() as st:
            cm = self._attn_common(st, f"D{l}")
            T, PS = cm["T"], cm["PS"]
            KTs = [T(f"KT{i}", [64, S], BF16) for i in range(2)]
            QTs = [T(f"QT{i}", [64, S], BF16) for i in range(2)]
            GTs = [T(f"GT{i}", [64, S], BF16) for i in range(2)]
            V1s = [T(f"V1{i}", [128, self.NT, 128], BF16) for i in range(2)]
            ntri = T("ntri", [128, 128], BF16)
            nones = T("nones", [128, 128], BF16)
            p.dma("sync", ntri[:], C["ntri"], writes=["ntri"])
            p.dma("sync", nones[:], C["nones"], writes=["nones"])
            Ef = [T(f"Ef{i}", [128, 512], F32) for i in range(2)]
            Lp = [T(f"Lp{i}", [128, 512], BF16) for i in range(2)]
            Lsum = T("Lsum", [128, 512], F32)
            Lsb = T("Lsb", [128, 512], BF16)
            Xps = [PS(f"X{i}", [128, 512]) for i in range(2)]
            Sps, Pt = cm["Sps"], cm["Pt"]
            ident, caus, zeros = cm["ident"], cm["caus"], cm["zeros"]
            for h in range(4):
                hb = h % 2
                KT, QT, GT, V1 = KTs[hb], QTs[hb], GTs[hb], V1s[hb]
                kK, kQ, kG, kV = f"KT{hb}", f"QT{hb}", f"GT{hb}", f"V1{hb}"
                p.dma("sync", KT[:], D["kdT"][h * 64:(h + 1) * 64, :], writes=[kK])
                p.dma("sync", QT[:], D["qdT"][h * 64:(h + 1) * 64, :], writes=[kQ])
                p.dma("sync", GT[:], D["gdT"][h * 64:(h + 1) * 64, :], writes=[kG])
                self._load_vaug(V1, kV, D["vd"], h, 1)
                for g in range(self.NG):
                    q0 = 512 * g
                    acc, acck = self._next_acc(cm)
                    p.op("tensor", lambda e, acc=acc: e.matmul(acc[0:64, :], lhsT=zeros[:, 0:64], rhs=caus[:, 0:512], start=True, stop=False), reads=["zeros", "caus"], writes=[acck])
                    p.op("gpsimd", lambda e: e.memset(Lsum[:], 0.0), writes=["Lsum"])
                    p.op("gpsimd", lambda e: e.memset(Lsb[:], 0.0), writes=["Lsb"])
                    blks = []
                    for kb in range(4 * g + 3, -1, -1):
                        o = kb - 4 * g
                        lo = 128 * o if o >= 0 else 0
                        n = 512 - lo
                        qk = [(KT[:, 128 * kb:128 * kb + 128], QT[:, q0 + lo:q0 + 512])]
                        if o >= 0:
                            qk.append((ident[:], caus[:, 511:511 + n]))
                        blks.append((kb, lo, n, qk))
                    nb = len(blks)

                    def z1(i):
                        kb, lo, n, qk = blks[i]
                        sb = i % 2
                        for j, (lh, rh) in enumerate(qk):
                            p.op("tensor", lambda e, lh=lh, rh=rh, sb=sb, n=n, j=j, nq=len(qk): e.matmul(Sps[sb][:, 0:n], lhsT=lh, rhs=rh, start=(j == 0), stop=(j == nq - 1)),
                                 reads=[kK, kQ, "ident", "caus"], writes=[f"S{sb}"])

                    def EE(i):
                        kb, lo, n, qk = blks[i]
                        sb = i % 2
                        p.op("scalar", lambda e, sb=sb, n=n: e.activation(out=Ef[sb][:, 0:n], in_=Sps[sb][:, 0:n], func=AF.Exp), reads=[f"S{sb}"], writes=[f"Ef{sb}"])

                    def LP(i):
                        kb, lo, n, qk = blks[i]
                        sb = i % 2
                        p.op("scalar", lambda e, sb=sb, n=n: e.activation(out=Lp[sb][:, 0:n], in_=Ef[sb][:, 0:n], func=AF.Ln, bias=1.0), reads=[f"Ef{sb}"], writes=[f"Lp{sb}"])

                    def XX(i):
                        kb, lo, n, qk = blks[i]
                        sb = i % 2
                        mm = qk + [(ntri[:], Lp[sb][:, 0:n]), (nones[:], Lsb[:, lo:512])]
                        for j, (lh, rh) in enumerate(mm):
                            p.op("tensor", lambda e, lh=lh, rh=rh, sb=sb, n=n, j=j, nq=len(mm): e.matmul(Xps[sb][:, 0:n], lhsT=lh, rhs=rh, start=(j == 0), stop=(j == nq - 1)),
                                 reads=[kK, kQ, "ident", "caus", "ntri", "nones", f"Lp{sb}", "Lsb"], writes=[f"X{sb}"])

                    def AA(i):
                        kb, lo, n, qk = blks[i]
                        sb = i % 2
                        p.op("scalar", lambda e, sb=sb, n=n: e.activation(out=Pt[sb][:, 0:n], in_=Xps[sb][:, 0:n], func=AF.Exp), reads=[f"X{sb}"], writes=[f"Pt{sb}"])

                    def PV(i):
                        kb, lo, n, qk = blks[i]
                        sb = i % 2
                        p.op("tensor", lambda e, sb=sb, n=n, kb=kb, lo=lo, acc=acc, V1=V1: e.matmul(acc[0:64, lo:512], lhsT=V1[:, kb, 0:64], rhs=Pt[sb][:, 0:n], start=False, stop=False),
                             reads=[f"Pt{sb}", kV], writes=[acck])

                    def LS(i):
                        kb, lo, n, qk = blks[i]
                        sb = i % 2
                        if i < nb - 1:
                            p.op("vector", lambda e, sb=sb, n=n, lo=lo: e.tensor_tensor(out=Lsum[:, lo:512], in0=Lsum[:, lo:512], in1=Lp[sb][:, 0:n], op=ALU.add), reads=["Lsum", f"Lp{sb}"], writes=["Lsum"])
                            p.op("vector", lambda e: e.tensor_copy(out=Lsb[:], in_=Lsum[:]), reads=["Lsum"], writes=["Lsb"])

                    z1(0); EE(0)
                    if nb > 1:
                        z1(1); EE(1)
                    LP(0)
                    for i in range(nb):
                        if i + 2 < nb:
                            z1(i + 2); EE(i + 2)
                        if i + 1 < nb:
                            LP(i + 1)
                        XX(i); AA(i)
                        if i >= 1:
                            PV(i - 1)
                        LS(i)
                    PV(nb - 1)
                    p.op("tensor", lambda e, acc=acc: e.matmul(acc[0:64, :], lhsT=zeros[:, 0:64], rhs=caus[:, 0:512], start=False, stop=True), reads=["zeros", "caus"], writes=[acck])
                    self._finish_tile(cm, g, h, GT, D["ydT"], norm=False, acc=acc, acckey=acck, gkey=kG)
            p.finish()

    def phase_m(self, l, x_src, last):
        nc, p, S = self.nc, self.p, self.S
        I, C, D = self.inp, self.cst, self.dr
        with contextlib.ExitStack() as st:
            def T(name, shape, dt):
                return st.enter_context(nc.sbuf_tensor(f"M{l}_{name}", list(shape), dt))

            def PS(name, shape, dt=F32):
                p.excl.add(name)
                return st.enter_context(nc.psum_tensor(f"M{l}_{name}", list(shape), dt))

            Wm = [T(f"Wm{i}", [128, 8, 1024], BF16) for i in range(4)]
            Wb = T("Wb", [128, 4, 2, 1024], BF16)
            Wo = T("Wo", [128, 8, 1024], BF16)
            bm = T("bm", [128, 4, 8], F32)
            gT = T("gT", [128, 8], F32)
            wst = [T(f"wst{i}", [128, 1024], F32) for i in range(2)]
            hTgs = [T(f"hTg{i}", [128, 8, 512], BF16) for i in range(2)]
            YTs = [T(f"YT{i}", [128, 4, 2, 512], BF16) for i in range(2)]
            Gs = [T(f"Gs{i}", [128, 512], F32) for i in range(2)]
            mrg = T("mrg", [128, 8, 512], F32)
            mrgb = T("mrgb", [128, 8, 512], BF16)
            tmp = T("tmp", [128, 512], F32)
            xres = [T(f"xres{i}", [128, 1024], F32) for i in range(2)]
            x1t = [T(f"x1t{i}", [128, 1024], F32) for i in range(2)]
            gfin = T("gfin", [128, 1024], F32)
            sq = T("sq", [128, 1024], F32)
            ss = T("ss", [128, 2], F32)
            Gp = [PS(f"Gp{i}", [128, 512]) for i in range(2)]
            Bp = [PS(f"Bp{i}", [128, 512]) for i in range(2)]
            Op = [PS(f"Op{i}", [128, 512]) for i in range(2)]

            p.dma("sync", gT[:], I["norm_g"][l].rearrange("(c p) -> p c", p=128), writes=["gT"], allow_slow_non_contiguous=True)
            for i in range(4):
                p.dma("sync", bm[:, i, :], I["b_merge"][l, i].rearrange("(c p) -> p c", p=128), writes=["bm"], allow_slow_non_contiguous=True)
            if last:
                p.dma("sync", gfin[:], I["final_norm_g"].partition_broadcast(128), writes=["gfin"])
            k = 0
            for i in range(4):
                for c in range(8):
                    b = k % 2
                    k += 1
                    p.dma("sync", wst[b][:], I["w_merge"][l, i, c * 128:(c + 1) * 128, :], writes=[f"wst{b}"])
                    p.op("vector", lambda e, i=i, c=c, b=b: e.tensor_scalar(out=Wm[i][:, c, :], in0=wst[b][:], scalar1=gT[:, c:c + 1], scalar2=None, op0=ALU.mult),
                         reads=[f"wst{b}", "gT"], writes=["Wm"])
            for i in range(4):
                for c in range(2):
                    b = k % 2
                    k += 1
                    p.dma("sync", wst[b][:], I["w_branch"][l, i, c * 128:(c + 1) * 128, :], writes=[f"wst{b}"])
                    p.op("vector", lambda e, i=i, c=c, b=b: e.tensor_copy(out=Wb[:, i, c, :], in_=wst[b][:]), reads=[f"wst{b}"], writes=["Wb"])
            for c in range(8):
                b = k % 2
                k += 1
                p.dma("sync", wst[b][:], I["w_out"][l, c * 128:(c + 1) * 128, :], writes=[f"wst{b}"])
                p.op("vector", lambda e, c=c, b=b: e.tensor_copy(out=Wo[:, c, :], in_=wst[b][:]), reads=[f"wst{b}"], writes=["Wo"])

            step = 0
            for g in range(self.NG):
                c0 = g * 512
                hTg, YT, hk, yk = hTgs[g % 2], YTs[g % 2], f"hTg{g % 2}", f"YT{g % 2}"
                p.dma("sync", hTg[:], D["hT"].rearrange("(c p) s -> p c s", p=128)[:, :, c0:c0 + 512], writes=[hk])
                for i, m in enumerate("abcd"):
                    p.dma("sync", YT[:, i, :, :], D[f"y{m}T"].rearrange("(c p) s -> p c s", p=128)[:, :, c0:c0 + 512], writes=[yk])
                for cc in range(8):
                    cs = slice(cc * 128, (cc + 1) * 128)
                    for i in range(4):
                        fb = step % 2
                        step += 1
                        for c in range(8):
                            p.op("tensor", lambda e, i=i, c=c, cs=cs, fb=fb, hTg=hTg: e.matmul(Gp[fb][:, :], lhsT=Wm[i][:, c, cs], rhs=hTg[:, c, :], start=(c == 0), stop=(c == 7)),
                                 reads=["Wm", hk], writes=[f"Gp{fb}"])
                        p.op("scalar", lambda e, i=i, cc=cc, fb=fb: e.activation(out=Gs[fb][:], in_=Gp[fb][:, :], func=AF.Sigmoid, bias=bm[:, i, cc:cc + 1]),
                             reads=[f"Gp{fb}", "bm"], writes=[f"Gs{fb}"])
                        for c in range(2):
                            p.op("tensor", lambda e, i=i, c=c, cs=cs, fb=fb, YT=YT: e.matmul(Bp[fb][:, :], lhsT=Wb[:, i, c, cs], rhs=YT[:, i, c, :], start=(c == 0), stop=(c == 1)),
                                 reads=["Wb", yk], writes=[f"Bp{fb}"])
                        if i == 0:
                            p.op("vector", lambda e, cc=cc, fb=fb: e.tensor_tensor(out=mrg[:, cc, :], in0=Bp[fb][:, :], in1=Gs[fb][:], op=ALU.mult), reads=[f"Bp{fb}", f"Gs{fb}"], writes=["mrg"])
                        else:
                            p.op("vector", lambda e, fb=fb: e.tensor_tensor(out=tmp[:], in0=Bp[fb][:, :], in1=Gs[fb][:], op=ALU.mult), reads=[f"Bp{fb}", f"Gs{fb}"], writes=["tmp"])
                            p.op("vector", lambda e, cc=cc: e.tensor_tensor(out=mrg[:, cc, :], in0=mrg[:, cc, :], in1=tmp[:], op=ALU.add), reads=["tmp", "mrg"], writes=["mrg"])
                    p.op("scalar", lambda e, cc=cc: e.copy(out=mrgb[:, cc, :], in_=mrg[:, cc, :]), reads=["mrg"], writes=["mrgb"])
                for t in range(4):
                    r0 = c0 + t * 128
                    xb = t % 2
                    p.dma("sync", xres[xb][:], x_src[r0:r0 + 128, :], writes=[f"xres{xb}"])
                    for half in range(2):
                        for cc in range(8):
                            p.op("tensor", lambda e, cc=cc, t=t, half=half: e.matmul(Op[half][:, :], lhsT=mrgb[:, cc, t * 128:(t + 1) * 128], rhs=Wo[:, cc, half * 512:(half + 1) * 512], start=(cc == 0), stop=(cc == 7)),
                                 reads=["mrgb", "Wo"], writes=[f"Op{half}"])
                        p.op("vector", lambda e, half=half, xb=xb: e.tensor_tensor(out=x1t[xb][:, half * 512:(half + 1) * 512], in0=Op[half][:, :], in1=xres[xb][:, half * 512:(half + 1) * 512], op=ALU.add),
                             reads=[f"Op{half}", f"xres{xb}"], writes=[f"x1t{xb}"])
                    if not last:
                        p.dma("gpsimd", D["x1"][r0:r0 + 128, :], x1t[xb][:], reads=[f"x1t{xb}"])
                    else:
                        p.op("scalar", lambda e, xb=xb: e.activation(out=sq[:], in_=x1t[xb][:], func=AF.Square), reads=[f"x1t{xb}"], writes=["sq"])
                        p.op("vector", lambda e: e.reduce_sum(out=ss[:, 0:1], in_=sq[:], axis=AX.X), reads=["sq"], writes=["ss"])
                        p.op("vector", lambda e: e.tensor_scalar(out=ss[:, 1:2], in0=ss[:, 0:1], scalar1=1.0 / D_MODEL, scalar2=RMS_EPS, op0=ALU.mult, op1=ALU.add), reads=["ss"], writes=["ss"])
                        p.op("scalar", lambda e: e.sqrt(out=ss[:, 1:2], in_=ss[:, 1:2]), reads=["ss"], writes=["ss"])
                        p.op("vector", lambda e: e.reciprocal(out=ss[:, 1:2], in_=ss[:, 1:2]), reads=["ss"], writes=["ss"])
                        p.op("scalar", lambda e, xb=xb: e.activation(out=sq[:], in_=x1t[xb][:], func=AF.Copy, scale=ss[:, 1:2]), reads=[f"x1t{xb}", "ss"], writes=["sq"])
                        p.op("vector", lambda e, xb=xb: e.tensor_tensor(out=x1t[xb][:], in0=sq[:], in1=gfin[:], op=ALU.mult), reads=["sq", "gfin"], writes=[f"x1t{xb}"])
                        p.dma("gpsimd", self.out[r0:r0 + 128, :], x1t[xb][:], reads=[f"x1t{xb}"])
            p.finish()

    def phase_c(self, l):
        nc, p, S = self.nc, self.p, self.S
        I, C, D = self.inp, self.cst, self.dr
        NT = self.NT
        ncmp = S // 16 - 1
        NCH = max(1, S // 2048)
        with contextlib.ExitStack() as st:
            cm = self._attn_common(st, f"C{l}")
            T, PS = cm["T"], cm["PS"]
            ident, caus, zeros = cm["ident"], cm["caus"], cm["zeros"]
            KcT = T("KcT", [68, 512], BF16)
            Vc = T("Vc", [128, 4, 128], BF16)
            with contextlib.ExitStack() as st2:
                def T2(name, shape, dt):
                    return st2.enter_context(nc.sbuf_tensor(f"C{l}_{name}", list(shape), dt))
                srcT = [T2("kcT", [64, S], BF16), T2("vcT", [64, S], BF16)]
                w1s = T2("w1s", [64, 32, 128], F32)
                W1 = T2("W1", [64, 32, 128], BF16)
                posf = T2("posf", [64, 32], F32)
                posT = T2("posT", [64, 32], BF16)
                b1 = T2("b1", [128, 1], F32)
                bias = T2("bias", [128, 1], F32)
                w2s = T2("w2s", [128, 64], F32)
                W2 = T2("W2", [128, 64], BF16)
                b2c = T2("b2c", [64, 1], F32)
                b2r = T2("b2r", [128, 64], F32)
                hb = T2("hb", [128, 512], F32)
                u = T2("u", [128, 512], F32)
                hg = T2("hg", [128, 512], BF16)
                Hps, Cps = cm["Sps"][0], cm["Sps"][1]
                p.op("vector", lambda e: e.memset(KcT[:], 0.0), writes=["KcT"])
                p.dma("sync", KcT[64:68, :], C["kaugc"], writes=["KcT"])
                p.op("gpsimd", lambda e: e.memset(Vc[:, :, 64:128], 1.0), writes=["Vc"])
                p.op("gpsimd", lambda e: e.memset(Vc[:, :, 0:64], 0.0), writes=["Vc"])
                p.dma("sync", srcT[0][:], D["kcT"], writes=["src0"])
                p.dma("sync", srcT[1][:], D["vcT"], writes=["src1"])
                for kv in range(2):
                    p.dma("sync", w1s[:], I["nsa_w1"][l, kv].rearrange("(ll d) h -> d ll h", d=64), writes=["w1s"])
                    p.op("vector", lambda e: e.tensor_copy(out=W1[:], in_=w1s[:]), reads=["w1s"], writes=["W1"])
                    p.dma("sync", posf[:], I["nsa_pos"][l, kv].rearrange("ll d -> d ll"), writes=["posf"], allow_slow_non_contiguous=True)
                    p.op("vector", lambda e: e.tensor_copy(out=posT[:], in_=posf[:]), reads=["posf"], writes=["posT"])
                    p.dma("sync", b1[:], I["nsa_b1"][l, kv].rearrange("(p o) -> p o", o=1), writes=["b1"], allow_slow_non_contiguous=True)
                    p.dma("sync", w2s[:], I["nsa_w2"][l, kv], writes=["w2s"])
                    p.op("vector", lambda e: e.tensor_copy(out=W2[:], in_=w2s[:]), reads=["w2s"], writes=["W2"])
                    for ll in range(32):
                        p.op("tensor", lambda e, ll=ll: e.matmul(Cps[:, 0:1], lhsT=W1[:, ll, :], rhs=posT[:, ll:ll + 1], start=(ll == 0), stop=(ll == 31)), reads=["W1", "posT"], writes=["S1"])
                    p.op("vector", lambda e: e.tensor_tensor(out=bias[:], in0=Cps[:, 0:1], in1=b1[:], op=ALU.add), reads=["S1", "b1"], writes=["bias"])
                    for ll in range(32):
                        p.op("tensor", lambda e, ll=ll, kv=kv: e.matmul(Hps[:, 0:ncmp], lhsT=W1[:, ll, :], rhs=srcT[kv][:, ll:ll + 16 * (ncmp - 1) + 1:16], start=(ll == 0), stop=(ll == 31)),
                             reads=["W1", f"src{kv}"], writes=["S0"])
                    p.op("scalar", lambda e: e.activation(out=hb[:, 0:ncmp], in_=Hps[:, 0:ncmp], func=AF.Identity, bias=bias[:, 0:1]), reads=["S0", "bias"], writes=["hb"])
                    p.op("vector", lambda e: e.tensor_tensor(out=u[:, 0:ncmp], in0=hb[:, 0:ncmp], in1=hb[:, 0:ncmp], op=ALU.mult), reads=["hb"], writes=["u"])
                    p.op("vector", lambda e: e.tensor_scalar(out=u[:, 0:ncmp], in0=u[:, 0:ncmp], scalar1=0.044715, scalar2=1.0, op0=ALU.mult, op1=ALU.add), reads=["u"], writes=["u"])
                    p.op("vector", lambda e: e.tensor_tensor(out=u[:, 0:ncmp], in0=u[:, 0:ncmp], in1=hb[:, 0:ncmp], op=ALU.mult), reads=["u", "hb"], writes=["u"])
                    p.op("scalar", lambda e: e.activation(out=u[:, 0:ncmp], in_=u[:, 0:ncmp], func=AF.Sigmoid, scale=1.5957691216057308), reads=["u"], writes=["u"])
                    p.op("vector", lambda e: e.memset(hg[:], 0.0), writes=["hg"])
                    p.op("vector", lambda e: e.tensor_tensor(out=hg[:, 0:ncmp], in0=u[:, 0:ncmp], in1=hb[:, 0:ncmp], op=ALU.mult), reads=["u", "hb"], writes=["hg"])
                    if kv == 0:
                        p.dma("sync", b2c[:], I["nsa_b2"][l, 0].rearrange("(p o) -> p o", o=1), writes=["b2c"], allow_slow_non_contiguous=True)
                        p.op("tensor", lambda e: e.matmul(Cps[0:64, 0:ncmp], lhsT=W2[:], rhs=hg[:, 0:ncmp], start=True, stop=True), reads=["W2", "hg"], writes=["S1"])
                        p.op("scalar", lambda e: e.activation(out=KcT[0:64, 0:ncmp], in_=Cps[0:64, 0:ncmp], func=AF.Identity, bias=b2c[:, 0:1]), reads=["S1", "b2c"], writes=["KcT"])
                    else:
                        p.dma("sync", b2r[:], I["nsa_b2"][l, 1].partition_broadcast(128), writes=["b2r"])
                        for ch in range(NCH):
                            p.op("tensor", lambda e, ch=ch: e.matmul(Cps[:, 0:64], lhsT=hg[:, ch * 128:(ch + 1) * 128], rhs=W2[:], start=True, stop=True), reads=["W2", "hg"], writes=["S1"])
                            p.op("vector", lambda e, ch=ch: e.tensor_tensor(out=Vc[:, ch, 0:64], in0=Cps[:, 0:64], in1=b2r[:], op=ALU.add), reads=["S1", "b2r"], writes=["Vc"])
                p.finish()
            KsT = T("KsT", [68, S], BF16)
            KwT = T("KwT", [68, S], BF16)
            Vs = T("Vs", [128, NT, 128], BF16)
            Vw = T("Vw", [128, NT, 128], BF16)
            ewide = T("ewide", [128, S], BF16)
            antic = T("antic", [128, 1024], BF16)
            cmk = T("cmk", [128, 2560], BF16)
            selmap = T("selmap", [128, 4, 128], BF16)
            tkm = T("tkm", [128, 256], F32)
            tka = T("tka", [128, 256], F32)
            Qt = [T(f"Qt{i}", [68, 4, 512], BF16) for i in range(2)]
            Gt = [T(f"Gt{i}", [64, 4, 512], BF16) for i in range(2)]
            cg = [T(f"cg{i}", [64, 12, 512], BF16) for i in range(2)]
            ysum = T("ysum", [64, 4, 512], F32)
            psel = T("psel", [128, 4, 128], F32)
            rq = T("rq", [128, 4], F32)
            sc = T("sc", [128, 128], F32)
            sc2 = T("sc2", [128, 128], F32)
            m8 = T("m8", [128, 16], F32)
            selb = T("selb", [128, 128], BF16)
            selm1T = T("selm1T", [128, 512], BF16)
            psQ = PS("psQ", [128, 4, 128])
            p.op("vector", lambda e: e.memset(psel[:], 0.0), writes=["psel"])
            acc2 = cm["accs"][1][0]
            TP = PS("TP", [128, 128], BF16)
            acc = cm["acc"]
            Sps, Pt = cm["Sps"], cm["Pt"]
            rinv, ytmp = cm["rinv"], cm["ytmp"]
            for (dst, src, aug) in ((KsT, "ksT", "kaug"), (KwT, "kwT", "kaug")):
                p.dma("sync", dst[0:64, :], D[src], writes=[src])
                p.dma("sync", dst[64:68, :], C[aug], writes=[src])
            self._load_vaug(Vs, "Vs", D["vs"], 0, 1)
            self._load_vaug(Vw, "Vw", D["vw"], 0, 1)
            p.dma("sync", ewide[:], C["ewide"], writes=["ewide"])
            p.dma("sync", antic[:], C["antic"], writes=["antic"])
            p.dma("sync", cmk[:], C["cm"], writes=["cmk"])
            p.dma("sync", selmap[:], C["selmap"].rearrange("(c n) j -> n c j", n=128)[:, :, 1:129], writes=["selmap"])
            p.dma("sync", tkm[:], C["tk_mult"], writes=["tkm"])
            p.dma("sync", tka[:], C["tk_add"], writes=["tka"])

            def branch_out(accx, acckey, h, gidx, cgb, first):
                p.op("vector", lambda e: e.tensor_scalar(out=rinv[:], in0=accx[64:128, :], scalar1=1e-30, scalar2=None, op0=ALU.max), reads=[acckey], writes=["rinv"])
                p.op("vector", lambda e: e.reciprocal(out=rinv[:], in_=rinv[:]), reads=["rinv"], writes=["rinv"])
                p.op("vector", lambda e: e.tensor_tensor(out=ytmp[:], in0=accx[0:64, :], in1=rinv[:], op=ALU.mult), reads=[acckey, "rinv"], writes=["ytmp"])
                if first:
                    p.op("vector", lambda e: e.tensor_tensor(out=ysum[:, h, :], in0=ytmp[:], in1=cg[cgb][:, gidx, :], op=ALU.mult), reads=["ytmp", f"cg{cgb}"], writes=["ysum"])
                else:
                    p.op("vector", lambda e: e.tensor_tensor(out=ytmp[:], in0=ytmp[:], in1=cg[cgb][:, gidx, :], op=ALU.mult), reads=["ytmp", f"cg{cgb}"], writes=["ytmp"])
                    p.op("vector", lambda e: e.tensor_tensor(out=ysum[:, h, :], in0=ysum[:, h, :], in1=ytmp[:], op=ALU.add), reads=["ytmp", "ysum"], writes=["ysum"])

            for g in range(self.NG):
                q0 = 512 * g
                qb = g % 2
                for h in range(4):
                    p.dma("sync", Qt[qb][0:64, h, :], D["qcT"][h * 64:(h + 1) * 64, q0:q0 + 512], writes=[f"Qt{qb}"])
                    p.dma("sync", Qt[qb][64:68, h, :], C["qaug_c"][h, :, q0:q0 + 512], writes=[f"Qt{qb}"])
                    p.dma("sync", Gt[qb][:, h, :], D["gcT"][h * 64:(h + 1) * 64, q0:q0 + 512], writes=[f"Gt{qb}"])
                for j in range(12):
                    p.dma("sync", cg[qb][:, j, :], D["cgT"][j:j + 1, q0:q0 + 512].partition_broadcast(64), writes=[f"cg{qb}"])
                chs = [ch for ch in range(NCH) if q0 - 2048 * ch >= 0]
                for h in range(4):
                    p.op("tensor", lambda e: e.matmul(acc[:, :], lhsT=zeros[:], rhs=caus[:, 0:512], start=True, stop=False), reads=["zeros", "caus"], writes=["acc"])
                    for ci, ch in enumerate(chs):
                        sb = self._step % cm["ns"]
                        self._step += 1
                        dlt = q0 - 2048 * ch
                        qk = [(KcT[:, ch * 128:(ch + 1) * 128], Qt[qb][:, h, :])]
                        if dlt < 2560:
                            qk.append((ident[:], cmk[:, dlt:dlt + 512]))
                        for j, (lh, rh) in enumerate(qk):
                            p.op("tensor", lambda e, lh=lh, rh=rh, sb=sb, j=j, nq=len(qk): e.matmul(Sps[sb][:, :], lhsT=lh, rhs=rh, start=(j == 0), stop=(j == nq - 1)),
                                 reads=["KcT", f"Qt{qb}", "ident", "cmk"], writes=[f"S{sb}"])
                        p.op("scalar", lambda e, sb=sb: e.activation(out=Pt[sb][:], in_=Sps[sb][:, :], func=AF.Exp), reads=[f"S{sb}"], writes=[f"Pt{sb}"])
                        p.op("tensor", lambda e, sb=sb, ch=ch: e.matmul(acc[:, :], lhsT=Vc[:, ch, :], rhs=Pt[sb][:], start=False, stop=False), reads=[f"Pt{sb}", "Vc"], writes=["acc"])
                        for qs in range(4):
                            p.op("tensor", lambda e, sb=sb, ch=ch, qs=qs, ci=ci, nch=len(chs): e.matmul(psQ[:, qs, :], lhsT=Pt[sb][:, qs * 128:(qs + 1) * 128], rhs=selmap[:, ch, :], start=(ci == 0), stop=(ci == nch - 1)),
                                 reads=[f"Pt{sb}", "selmap"], writes=["psQ"])
                    p.op("tensor", lambda e: e.matmul(acc[:, :], lhsT=zeros[:], rhs=caus[:, 0:512], start=False, stop=True), reads=["zeros", "caus"], writes=["acc"])
                    branch_out(acc, "acc", h, h * 3 + 0, qb, True)
                    for qs in range(4):
                        p.op("vector", lambda e, qs=qs: e.tensor_scalar(out=rq[:, qs:qs + 1], in0=psQ[:, qs, 127:128], scalar1=1e-30, scalar2=None, op0=ALU.max), reads=["psQ"], writes=["rq"])
                        p.op("vector", lambda e, qs=qs: e.reciprocal(out=rq[:, qs:qs + 1], in_=rq[:, qs:qs + 1]), reads=["rq"], writes=["rq"])
                        if h == 0:
                            p.op("vector", lambda e, qs=qs: e.tensor_scalar(out=psel[:, qs, 1:128], in0=psQ[:, qs, 0:127], scalar1=rq[:, qs:qs + 1], scalar2=None, op0=ALU.mult), reads=["psQ", "rq"], writes=["psel"])
                        else:
                            p.op("vector", lambda e, qs=qs: e.scalar_tensor_tensor(out=psel[:, qs, 1:128], in0=psQ[:, qs, 0:127], scalar=rq[:, qs:qs + 1], in1=psel[:, qs, 1:128], op0=ALU.mult, op1=ALU.add), reads=["psQ", "rq", "psel"], writes=["psel"])
                for qs in range(4):
                    m = 4 * g + qs
                    j0 = 128 - 2 * m
                    p.op("vector", lambda e, qs=qs, j0=j0: e.tensor_tensor(out=sc[:], in0=psel[:, qs, :], in1=tkm[:, j0:j0 + 128], op=ALU.mult), reads=["psel", "tkm"], writes=["sc"])
                    p.op("vector", lambda e, j0=j0: e.tensor_tensor(out=sc[:], in0=sc[:], in1=tka[:, j0:j0 + 128], op=ALU.add), reads=["sc", "tka"], writes=["sc"])
                    p.op("vector", lambda e: e.memset(sc[:, 0:1], 1e9), reads=["sc"], writes=["sc"])
                    p.op("vector", lambda e: e.max(out=m8[:, 0:8], in_=sc[:]), reads=["sc"], writes=["m8"])
                    p.op("vector", lambda e: e.match_replace(out=sc2[:], in_to_replace=m8[:, 0:8], in_values=sc[:], imm_value=-2.0), reads=["sc", "m8"], writes=["sc2"])
                    p.op("vector", lambda e: e.max(out=m8[:, 8:16], in_=sc2[:]), reads=["sc2"], writes=["m8"])
                    p.op("vector", lambda e: e.tensor_scalar(out=selb[:], in0=sc[:], scalar1=m8[:, 15:16], scalar2=1.0, op0=ALU.is_ge, op1=ALU.subtract), reads=["sc", "m8"], writes=["selb"])
                    p.op("tensor", lambda e: e.transpose(TP[:, :], selb[:], ident[:]), reads=["selb", "ident"], writes=["TP"])
                    p.op("vector", lambda e, qs=qs: e.tensor_copy(out=selm1T[:, qs * 128:(qs + 1) * 128], in_=TP[:, :]), reads=["TP"], writes=["selm1T"])
                for h in range(4):
                    blocks = []
                    for kb in range(0, 4 * g + 4):
                        o = kb - 4 * g
                        lo = 128 * o if o >= 0 else 0
                        n = 512 - lo
                        qk = [(KsT[:, 128 * kb:128 * kb + 128], Qt[qb][:, h, lo:512]), (ewide[:, 128 * kb:128 * kb + 128], selm1T[:, lo:512])]
                        if o >= 0:
                            qk.append((ident[:], caus[:, 512:512 + n]))
                        blocks.append(dict(qk=qk, kr=128, n=n, pv_lhsT=Vs[:, kb, :], pv_out=acc2[:, lo:512], rd=["ksT", f"Qt{qb}", "ewide", "selm1T", "Vs"]))
                    self._run_tile(cm, blocks, 1.0, acc=acc2, acckey="acc2")
                    branch_out(acc2, "acc2", h, h * 3 + 1, qb, False)
                    blocks = []
                    for kb in range(max(0, 4 * g - 4), 4 * g + 4):
                        o = kb - 4 * g
                        if o >= 0:
                            lo, hi = 128 * o, 512
                            msk = caus[:, 512:512 + (hi - lo)]
                        else:
                            lo, hi = 0, min(512, 640 + 128 * o)
                            msk = antic[:, -128 * o:-128 * o + hi]
                        blocks.append(dict(qk=[(KwT[:, 128 * kb:128 * kb + 128], Qt[qb][:, h, lo:hi]), (ident[:], msk)], kr=128, n=hi - lo,
                                           pv_lhsT=Vw[:, kb, :], pv_out=acc[:, lo:hi], rd=["kwT", f"Qt{qb}", "antic", "Vw"]))
                    self._run_tile(cm, blocks, 1.0)
                    branch_out(acc, "acc", h, h * 3 + 2, qb, False)
                    yb = (g + h) % 2
                    yst = cm["yst"][yb]
                    p.op("vector", lambda e, h=h, yst=yst, qb=qb: e.tensor_tensor(out=yst[:], in0=ysum[:, h, :], in1=Gt[qb][:, h, :], op=ALU.mult), reads=["ysum", f"Gt{qb}"], writes=[f"yst{yb}"])
                    p.dma("gpsimd", D["ycT"][h * 64:(h + 1) * 64, q0:q0 + 512], yst[:], reads=[f"yst{yb}"])
            p.finish()

    def build_all(self):
        self.declare()
        x_src = self.inp["x"]
        for l in range(self.n_layers):
            last = (l == self.n_layers - 1)
            self.phase_proj(l, x_src)
            self.phase_a(l)
            self.phase_b(l)
            self.phase_c(l)
            self.phase_d(l)
            self.phase_m(l, x_src, last)
            x_src = self.dr["x1"]
        return self.nc


_CACHE = {}


def _get_builder(S):
    if S not in _CACHE:
        b = Builder(S)
        b.build_all()
        _CACHE[S] = b
    return _CACHE[S]


def kernel(x, norm_g, w_in, mla_q_norm, mla_w_uq, mla_kv_norm, mla_w_ukv, nsa_pos, nsa_w1,
           nsa_b1, nsa_w2, nsa_b2, w_branch, w_merge, b_merge, w_out, final_norm_g):
    x = np.asarray(x, dtype=np.float32)
    Bn, S, Dm = x.shape
    bld = _get_builder(S)
    shared = dict(norm_g=norm_g, w_in=w_in, mla_q_norm=mla_q_norm, mla_w_uq=mla_w_uq, mla_kv_norm=mla_kv_norm,
                  mla_w_ukv=mla_w_ukv, nsa_pos=nsa_pos, nsa_w1=nsa_w1, nsa_b1=nsa_b1, nsa_w2=nsa_w2, nsa_b2=nsa_b2,
                  w_branch=w_branch, w_merge=w_merge, b_merge=b_merge, w_out=w_out, final_norm_g=final_norm_g)
    shared = {k: np.ascontiguousarray(np.asarray(v, dtype=np.float32)) for k, v in shared.items()}
    for k, v in bld.consts_np.items():
        shared["c_" + k] = v
    n_cores = 8
    in_maps = []
    for c in range(n_cores):
        m = dict(shared)
        m["x"] = np.ascontiguousarray(x[(c // 2) % Bn])
        in_maps.append(m)
    res = run_bass_kernel_spmd(bld.nc, in_maps, core_ids=list(range(n_cores)))
    out = np.empty((Bn, S, Dm), np.float32)
    half = S // 2
    for b in range(Bn):
        out[b, :half] = res.results[2 * b]["out"][:half]
        out[b, half:] = res.results[2 * b + 1]["out"][half:]
    return out
```

```python
import contextlib
import numpy as np
import ml_dtypes
import concourse.bass as bass
import concourse.mybir as mybir
from concourse.bass_utils import run_bass_kernel_spmd

F32 = mybir.dt.float32
BF16 = mybir.dt.bfloat16
ALU = mybir.AluOpType
AF = mybir.ActivationFunctionType
AX = mybir.AxisListType

D_MODEL = 1024
DEPTH = 2
RMS_EPS = 1e-6
BIG = 30000.0

IN_SPLITS = (
    ("a_q", 256), ("a_k", 256), ("a_v", 256), ("a_gate", 256),
    ("b_cq", 192), ("b_ckv", 128), ("b_kpe", 32), ("b_gate", 256),
    ("c_q", 256), ("c_kc", 64), ("c_vc", 64), ("c_ks", 64),
    ("c_vs", 64), ("c_kw", 64), ("c_vw", 64), ("c_g", 12),
    ("c_gate", 256),
    ("d_q", 256), ("d_k", 256), ("d_v", 256), ("d_gate", 256),
)
OFF = {}
_o = 0
for _n, _w in IN_SPLITS:
    OFF[_n] = _o
    _o += _w
D_IN = _o

ENGS = ("tensor", "vector", "scalar", "gpsimd", "sync")


class Prog:
    NDMA = 32

    def __init__(self, nc, stack):
        self.nc = nc
        self.sem = {e: stack.enter_context(nc.semaphore(f"s_{e}")) for e in ENGS if e != "sync"}
        self.dsem = [stack.enter_context(nc.semaphore(f"d{i}")) for i in range(self.NDMA)]
        self.pbA = stack.enter_context(nc.semaphore("pbA"))
        self.pbB = stack.enter_context(nc.semaphore("pbB"))
        self.nphase = 0
        self.excl = set()
        self.semobj = {("c", e): self.sem[e] for e in self.sem}
        for i in range(self.NDMA):
            self.semobj[("d", i)] = self.dsem[i]
        self.begin()

    def begin(self):
        self.ops = {e: [] for e in ENGS}
        self.lastw = {}
        self.readers = {}
        if not hasattr(self, "cnt"):
            self.cnt = {e: 0 for e in ENGS}
            self.dcnt = [0] * self.NDMA
            self.dnext = [0, 0]
            self.waited = {e: {} for e in ENGS}

    def _wait(self, eng, tok):
        key, val, src = tok
        w = self.waited[eng]
        if w.get(key, 0) >= val:
            return
        w[key] = val
        sem = self.semobj[key]
        self.ops[eng].append(lambda e, sem=sem, val=val: e.wait_ge(sem, val))

    def _deps(self, eng, reads, writes, is_dma=False):
        def need(t):
            return is_dma or t[2] != eng or eng != "tensor"
        for b in reads:
            for t in self.lastw.get(b, {}).values():
                if need(t):
                    self._wait(eng, t)
            if b in self.excl:
                for t in self.readers.get(b, ()):
                    if t[2] != eng:
                        self._wait(eng, t)
        for b in writes:
            for t in self.lastw.get(b, {}).values():
                if need(t):
                    self._wait(eng, t)
            for t in self.readers.get(b, ()):
                if need(t):
                    self._wait(eng, t)

    def _record(self, tok, reads, writes):
        for b in writes:
            if tok[2] == "dma":
                self.lastw.setdefault(b, {})[tok[0]] = tok
            else:
                self.lastw[b] = {tok[0]: tok}
            self.readers[b] = []
        for b in reads:
            self.readers.setdefault(b, []).append(tok)

    def op(self, eng, fn, reads=(), writes=()):
        self._deps(eng, reads, writes)
        self.cnt[eng] += 1
        tok = (("c", eng), self.cnt[eng], eng)
        sem = self.sem[eng]
        self.ops[eng].append(lambda e, fn=fn, sem=sem: fn(e).then_inc(sem, 1))
        self._record(tok, reads, writes)
        return tok

    def dma(self, eng, out, in_, reads=(), writes=(), **kw):
        self._deps(eng, reads, writes, is_dma=True)
        half = self.NDMA // 2
        qi = 0 if eng == "sync" else 1
        i = qi * half + self.dnext[qi]
        self.dnext[qi] = (self.dnext[qi] + 1) % half
        key = ("d", i)
        if self.dcnt[i] > 0:
            self._wait(eng, (key, self.dcnt[i], "dma"))
        self.dcnt[i] += 16
        tok = (key, self.dcnt[i], "dma")
        sem = self.dsem[i]
        self.ops[eng].append(lambda e, out=out, in_=in_, sem=sem, kw=kw: e.dma_start(out=out, in_=in_, **kw).then_inc(sem, 16))
        self._record(tok, reads, writes)
        return tok

    def finish(self):
        final = []
        for e in ENGS:
            if e != "sync" and self.cnt[e] > 0:
                final.append((("c", e), self.cnt[e], e))
        for i in range(self.NDMA):
            if self.dcnt[i] > 0:
                final.append((("d", i), self.dcnt[i], "dma"))
        for e in ENGS:
            for t in final:
                self._wait(e, t)
        ops = self.ops
        with self.nc.Block() as block:
            @block.tensor
            def _(e):
                for f in ops["tensor"]:
                    f(e)

            @block.vector
            def _(e):
                for f in ops["vector"]:
                    f(e)

            @block.scalar
            def _(e):
                for f in ops["scalar"]:
                    f(e)

            @block.gpsimd
            def _(e):
                for f in ops["gpsimd"]:
                    f(e)

            @block.sync
            def _(e):
                for f in ops["sync"]:
                    f(e)
        self.begin()


def _bf(a):
    return np.asarray(a, dtype=np.float32).astype(ml_dtypes.bfloat16)


def make_consts(S):
    c = {}
    k = np.arange(128)[:, None]
    c["ident"] = _bf(np.eye(128))
    j = np.arange(1024)[None, :]
    c["caus"] = _bf(np.where(k <= j - 512, 0.0, -BIG))
    c["antic"] = _bf(np.where(k > j - 512, 0.0, -BIG))
    j = np.arange(256)[None, :]
    c["band"] = _bf(np.where((j - k >= 0) & (j - k <= 128), 0.0, -BIG))
    j = np.arange(2560)[None, :]
    c["cm"] = _bf(np.where(j >= 16 * k + 31, 0.0, -BIG))
    kk = np.arange(S)[None, :]
    c["ewide"] = _bf(np.where(k == kk // 64, BIG, 0.0))
    c["ntri"] = _bf(np.where(k >= np.arange(128)[None, :], -1.0, 0.0))
    c["nones"] = _bf(-np.ones((128, 128)))
    half = 16
    inv = (10000.0 ** (-np.arange(half, dtype=np.float32) / half)).astype(np.float32)
    pos = np.arange(S, dtype=np.float32)
    ang = (pos[None, :] * inv[:, None]).astype(np.float32)
    cos = np.cos(ang).astype(np.float32)
    sin = np.sin(ang).astype(np.float32)
    c["ropec"] = np.concatenate([cos, cos], 0).astype(np.float32)
    c["ropes"] = np.concatenate([-sin, sin], 0).astype(np.float32)
    t = np.arange(S)
    a, b = t // 128, t % 128
    c["kaug"] = _bf(np.stack([np.ones(S), 128.0 * a, np.ones(S), b], 0))
    ncmp = S // 16 - 1
    pc = 16 * np.arange(512) + 31
    ac, bc = pc // 128, pc % 128
    c["kaugc"] = _bf(np.stack([np.ones(512), 128.0 * ac, np.ones(512), bc], 0))
    sl = 2.0 ** (-(np.arange(8) + 1.0))
    sa, sc = sl[0::2], sl[1::2]
    c["qaug_a"] = _bf(np.stack([np.stack([-s * 128.0 * a, s * np.ones(S), -s * b, s * np.ones(S)], 0) for s in sa], 0))
    c["qaug_c"] = _bf(np.stack([np.stack([-s * 128.0 * a, s * np.ones(S), -s * b, s * np.ones(S)], 0) for s in sc], 0))
    qq = np.arange(128)[:, None]
    jj = np.arange(256)[None, :] - 128
    cur = (qq >= 64).astype(np.int64)
    mult = np.where((jj <= cur) & (jj != cur) & (jj != cur - 1), 1.0, 0.0)
    add = np.where(jj > cur, -1.0, np.where(jj == cur, 2e9, np.where(jj == cur - 1, 3e9, 0.0)))
    c["tk_mult"] = mult.astype(np.float32)
    c["tk_add"] = add.astype(np.float32)
    n_sel = S // 64
    sm = np.zeros((512, 129), np.float32)
    for jv in range(min(n_sel, 128)):
        for aa in range(4):
            for cc in range(2):
                n = jv * 4 - aa - cc
                if 0 <= n < ncmp:
                    sm[n, jv] += 1.0
    sm[:, 128] = 1.0
    c["selmap"] = _bf(sm)
    return c


CONST_DT = {"ropec": F32, "ropes": F32, "tk_mult": F32, "tk_add": F32}


class Builder:
    def __init__(self, S, debug=(), n_layers=DEPTH, out_rows=None):
        self.S = S
        self.NG = S // 512
        self.NT = S // 128
        self.debug = set(debug)
        self.n_layers = n_layers
        nc = self.nc = bass.Bass("TRN2", target_bir_lowering=False)
        self.stack = contextlib.ExitStack()
        self.p = Prog(nc, self.stack)
        self.consts_np = make_consts(S)
        self.inp = {}
        self.dr = {}

    def ext_in(self, name, shape, dt=F32):
        t = self.nc.dram_tensor(name, list(shape), dt, kind="ExternalInput").ap()
        self.inp[name] = t
        return t

    def scratch(self, name, shape, dt=BF16):
        kind = "ExternalOutput" if name in self.debug else "Internal"
        t = self.nc.dram_tensor(name, list(shape), dt, kind=kind).ap()
        self.dr[name] = t
        return t

    def declare(self):
        S = self.S
        L = DEPTH
        self.ext_in("x", [S, D_MODEL])
        self.ext_in("norm_g", [L, D_MODEL])
        self.ext_in("w_in", [L, D_MODEL, D_IN])
        self.ext_in("mla_q_norm", [L, 192])
        self.ext_in("mla_w_uq", [L, 192, 384])
        self.ext_in("mla_kv_norm", [L, 128])
        self.ext_in("mla_w_ukv", [L, 128, 512])
        self.ext_in("nsa_pos", [L, 2, 32, 64])
        self.ext_in("nsa_w1", [L, 2, 2048, 128])
        self.ext_in("nsa_b1", [L, 2, 128])
        self.ext_in("nsa_w2", [L, 2, 128, 64])
        self.ext_in("nsa_b2", [L, 2, 64])
        self.ext_in("w_branch", [L, 4, 256, D_MODEL])
        self.ext_in("w_merge", [L, 4, D_MODEL, D_MODEL])
        self.ext_in("b_merge", [L, 4, D_MODEL])
        self.ext_in("w_out", [L, D_MODEL, D_MODEL])
        self.ext_in("final_norm_g", [D_MODEL])
        self.cst = {}
        for k, v in self.consts_np.items():
            self.cst[k] = self.ext_in("c_" + k, v.shape, CONST_DT.get(k, BF16))
        sc = self.scratch
        sc("hT", [D_MODEL, S])
        for m in "acd":
            sc(f"q{m}T", [256, S])
            sc(f"g{m}T", [256, S])
        sc("gbT", [256, S])
        sc("kaT", [256, S]); sc("kdT", [256, S])
        sc("va", [S, 256]); sc("vd", [S, 256]); sc("vb", [S, 256])
        sc("vs", [S, 64]); sc("vw", [S, 64])
        sc("kcT", [64, S]); sc("vcT", [64, S]); sc("ksT", [64, S]); sc("kwT", [64, S])
        sc("cgT", [12, S])
        sc("qbT", [4, 96, S]); sc("kbT", [4, 96, S])
        for m in "abcd":
            sc(f"y{m}T", [256, S])
        sc("x1", [S, D_MODEL], F32)
        self.out = self.nc.dram_tensor("out", [S, D_MODEL], F32, kind="ExternalOutput").ap()

    def phase_proj(self, l, x_src):
        nc, p, S = self.nc, self.p, self.S
        I, C, D = self.inp, self.cst, self.dr
        with contextlib.ExitStack() as st:
            def T(name, shape, dt):
                return st.enter_context(nc.sbuf_tensor(f"P{l}_{name}", list(shape), dt))

            def PS(name, shape, dt=F32):
                p.excl.add(name)
                return st.enter_context(nc.psum_tensor(f"P{l}_{name}", list(shape), dt))

            Win = T("Win", [128, 8, D_IN], BF16)
            wst = [T(f"wst{i}", [128, D_IN], F32) for i in range(2)]
            gT = T("gT", [128, 8], F32)
            ident = T("ident", [128, 128], BF16)
            Wuq = T("Wuq", [128, 2, 384], BF16)
            Wuqs = T("Wuqs", [128, 2, 384], BF16)
            uqst = T("uqst", [128, 2, 384], F32)
            gq = T("gq", [128, 2], F32)
            Wk = T("Wk", [128, 4, 64], BF16)
            Wv = T("Wv", [128, 4, 64], BF16)
            kvst = T("kvst", [128, 4, 128], F32)
            gkv = T("gkv", [128, 1], F32)
            Wkpe = T("Wkpe", [128, 8, 96], BF16)
            xst = [T(f"xst{i}", [128, D_MODEL], F32) for i in range(2)]
            sq = T("sq", [128, D_MODEL], F32)
            xn = [T(f"xn{i}", [128, D_MODEL], BF16) for i in range(2)]
            ss = T("ss", [128, 4], F32)
            rs = T("rs", [128, 4], F32)
            hTgs = [T(f"hTg{i}", [128, 8, 512], BF16) for i in range(2)]
            fst = [T(f"fst{i}", [128, 512], BF16) for i in range(3)]
            vst = [T(f"vst{i}", [128, 512], BF16) for i in range(2)]
            v2st = [T(f"v2st{i}", [128, 128], BF16) for i in range(2)]
            vbst = [T(f"vbst{i}", [128, 256], BF16) for i in range(2)]
            lat = T("lat", [128, 320], F32)
            cqn = T("cqn", [128, 192], BF16)
            ckvn = T("ckvn", [128, 128], BF16)
            cqnT = T("cqnT", [128, 2, 512], BF16)
            ckvnT = T("ckvnT", [128, 512], BF16)
            ropec = [T(f"ropec{i}", [96, 512], F32) for i in range(2)]
            ropes = [T(f"ropes{i}", [96, 512], F32) for i in range(2)]
            t1 = T("t1", [96, 512], F32)
            t2 = T("t2", [96, 512], F32)
            qbst = [T(f"qbst{i}", [96, 512], BF16) for i in range(2)]
            kbst = [T(f"kbst{i}", [64, 512], BF16) for i in range(2)]
            kpst = T("kpst", [96, 512], BF16)

            Fps = [PS(f"F{i}", [128, 512]) for i in range(2)]
            Tps = [PS(f"T{i}", [128, 1024], BF16) for i in range(2)]
            Vps = PS("V", [128, 512])
            Lps = PS("L", [128, 512])
            Mps = PS("M", [128, 1024], BF16)
            M2ps = PS("M2", [128, 512])

            p.dma("sync", ident[:], C["ident"], writes=["ident"])
            p.dma("sync", gT[:], I["norm_g"][l].rearrange("(c p) -> p c", p=128), writes=["gT"],
                  allow_slow_non_contiguous=True)
            for c in range(8):
                b = c % 2
                p.dma("sync", wst[b][:], I["w_in"][l, c * 128:(c + 1) * 128, :], writes=[f"wst{b}"])
                p.op("vector", lambda e, c=c, b=b: e.tensor_scalar(out=Win[:, c, :], in0=wst[b][:], scalar1=gT[:, c:c + 1], scalar2=None, op0=ALU.mult),
                     reads=[f"wst{b}", "gT"], writes=["Win"])
            k0 = OFF["b_kpe"] - 64
            p.op("vector", lambda e: e.tensor_copy(out=Wkpe[:, :, 0:64], in_=Win[:, :, k0:k0 + 64]), reads=["Win"], writes=["Wkpe"])
            p.op("vector", lambda e: e.tensor_copy(out=Wkpe[:, :, 64:80], in_=Win[:, :, k0 + 80:k0 + 96]), reads=["Win"], writes=["Wkpe"])
            p.op("vector", lambda e: e.tensor_copy(out=Wkpe[:, :, 80:96], in_=Win[:, :, k0 + 64:k0 + 80]), reads=["Win"], writes=["Wkpe"])
            p.dma("sync", gq[:, 0:1], I["mla_q_norm"][l, 0:128].rearrange("(p o) -> p o", o=1), writes=["gq"], allow_slow_non_contiguous=True)
            p.dma("sync", gq[0:64, 1:2], I["mla_q_norm"][l, 128:192].rearrange("(p o) -> p o", o=1), writes=["gq"], allow_slow_non_contiguous=True)
            p.dma("sync", uqst[:, 0, :], I["mla_w_uq"][l, 0:128, :], writes=["uqst"])
            p.dma("sync", uqst[0:64, 1, :], I["mla_w_uq"][l, 128:192, :], writes=["uqst"])
            p.op("vector", lambda e: e.tensor_scalar(out=Wuq[:, 0, :], in0=uqst[:, 0, :], scalar1=gq[:, 0:1], scalar2=None, op0=ALU.mult), reads=["uqst", "gq"], writes=["Wuq"])
            p.op("vector", lambda e: e.tensor_scalar(out=Wuq[0:64, 1, :], in0=uqst[0:64, 1, :], scalar1=gq[0:64, 1:2], scalar2=None, op0=ALU.mult), reads=["uqst", "gq"], writes=["Wuq"])
            p.op("vector", lambda e: e.tensor_copy(out=Wuqs[:, 0, :], in_=Wuq[:, 0, :]), reads=["Wuq"], writes=["Wuqs"])
            p.op("vector", lambda e: e.tensor_copy(out=Wuqs[0:64, 1, :], in_=Wuq[0:64, 1, :]), reads=["Wuq"], writes=["Wuqs"])
            for h in range(4):
                o = h * 96 + 64
                for (dlo, slo) in ((o, o + 16), (o + 16, o)):
                    p.op("vector", lambda e, dlo=dlo, slo=slo: e.tensor_copy(out=Wuqs[:, 0, dlo:dlo + 16], in_=Wuq[:, 0, slo:slo + 16]), reads=["Wuq"], writes=["Wuqs"])
                    p.op("vector", lambda e, dlo=dlo, slo=slo: e.tensor_copy(out=Wuqs[0:64, 1, dlo:dlo + 16], in_=Wuq[0:64, 1, slo:slo + 16]), reads=["Wuq"], writes=["Wuqs"])
            p.dma("sync", gkv[:, 0:1], I["mla_kv_norm"][l].rearrange("(p o) -> p o", o=1), writes=["gkv"], allow_slow_non_contiguous=True)
            p.dma("sync", kvst[:], I["mla_w_ukv"][l].rearrange("r (h c) -> r h c", h=4), writes=["kvst"])
            p.op("vector", lambda e: e.tensor_scalar(out=Wk[:], in0=kvst[:, :, 0:64], scalar1=gkv[:, 0:1], scalar2=None, op0=ALU.mult), reads=["kvst", "gkv"], writes=["Wk"])
            p.op("vector", lambda e: e.tensor_scalar(out=Wv[:], in0=kvst[:, :, 64:128], scalar1=gkv[:, 0:1], scalar2=None, op0=ALU.mult), reads=["kvst", "gkv"], writes=["Wv"])

            if getattr(self, "stop", 0) == 1:
                p.finish(); return
            FM = []
            for m, nm in (("a", "a_q"), ("c", "c_q"), ("d", "d_q")):
                for cc in range(2):
                    FM.append((OFF[nm] + cc * 128, 128, "copy", 0.125, [(f"q{m}T", cc * 128, 0, 128)]))
            for dst, nm in (("kaT", "a_k"), ("kdT", "d_k")):
                for cc in range(2):
                    FM.append((OFF[nm] + cc * 128, 128, "copy", 1.0, [(dst, cc * 128, 0, 128)]))
            FM.append((OFF["c_kc"], 128, "copy", 1.0, [("kcT", 0, 0, 64), ("vcT", 0, 64, 128)]))
            FM.append((OFF["c_ks"], 128, "copy", 1.0, [("ksT", 0, 0, 64)]))
            FM.append((OFF["c_kw"], 128, "copy", 1.0, [("kwT", 0, 0, 64)]))
            for dst, nm in (("gaT", "a_gate"), ("gbT", "b_gate"), ("gcT", "c_gate"), ("gdT", "d_gate")):
                for cc in range(2):
                    FM.append((OFF[nm] + cc * 128, 128, "silu", 1.0, [(dst, cc * 128, 0, 128)]))
            FM.append((OFF["c_g"], 12, "sigmoid", 1.0, [("cgT", 0, 0, 12)]))

            fcount = [0]

            def fm_chunk(g, col0, ncols, kind, scale, dsts, wsrc=None):
                hTg, hk = hTgs[g % 2], f"hTg{g % 2}"
                i = fcount[0]
                fcount[0] += 1
                fb = i % 2
                sb = i % 3
                for c in range(8):
                    lhs = Win[:, c, col0:col0 + ncols] if wsrc is None else wsrc[:, c, :]
                    p.op("tensor", lambda e, c=c, lhs=lhs, fb=fb, hTg=hTg: e.matmul(Fps[fb][0:ncols, :], lhsT=lhs, rhs=hTg[:, c, :], start=(c == 0), stop=(c == 7)),
                         reads=[hk, "Win", "Wkpe"], writes=[f"F{fb}"])
                return fb, sb

            for g in range(self.NG):
                c0 = g * 512
                rb = g % 2
                hTg, hk = hTgs[g % 2], f"hTg{g % 2}"
                p.dma("sync", ropec[rb][64:96, :], C["ropec"][:, c0:c0 + 512], writes=[f"ropec{rb}"])
                p.dma("sync", ropes[rb][64:96, :], C["ropes"][:, c0:c0 + 512], writes=[f"ropes{rb}"])
                for t in range(4):
                    r0 = c0 + t * 128
                    xb = t % 2
                    p.dma("sync", xst[xb][:], x_src[r0:r0 + 128, :], writes=[f"xst{xb}"])
                    p.op("scalar", lambda e, xb=xb: e.activation(out=sq[:], in_=xst[xb][:], func=AF.Square), reads=[f"xst{xb}"], writes=["sq"])
                    p.op("vector", lambda e, t=t: e.reduce_sum(out=ss[:, t:t + 1], in_=sq[:], axis=AX.X), reads=["sq"], writes=["ss"])
                    p.op("vector", lambda e, t=t: e.tensor_scalar(out=rs[:, t:t + 1], in0=ss[:, t:t + 1], scalar1=1.0 / D_MODEL, scalar2=RMS_EPS, op0=ALU.mult, op1=ALU.add), reads=["ss"], writes=["rs"])
                    p.op("scalar", lambda e, t=t: e.sqrt(out=rs[:, t:t + 1], in_=rs[:, t:t + 1]), reads=["rs"], writes=["rs"])
                    p.op("vector", lambda e, t=t: e.reciprocal(out=rs[:, t:t + 1], in_=rs[:, t:t + 1]), reads=["rs"], writes=["rs"])
                    p.op("scalar", lambda e, xb=xb, t=t: e.activation(out=xn[xb][:], in_=xst[xb][:], func=AF.Copy, scale=rs[:, t:t + 1]), reads=[f"xst{xb}", "rs"], writes=[f"xn{xb}"])
                    for c in range(8):
                        p.op("tensor", lambda e, c=c, xb=xb: e.transpose(Tps[xb][:, c * 128:(c + 1) * 128], xn[xb][:, c * 128:(c + 1) * 128], ident[:]),
                             reads=[f"xn{xb}", "ident"], writes=[f"T{xb}"])
                    p.op("vector", lambda e, xb=xb, t=t, hTg=hTg: e.tensor_copy(out=hTg[:, :, t * 128:(t + 1) * 128], in_=Tps[xb][:].rearrange("p (c s) -> p c s", c=8)),
                         reads=[f"T{xb}"], writes=[hk])
                p.dma("gpsimd", D["hT"].rearrange("(c p) s -> p c s", p=128)[:, :, c0:c0 + 512], hTg[:], reads=[hk])
                if getattr(self, "stop", 0) == 2:
                    continue
                for (col0, ncols, kind, scale, dsts) in FM:
                    fb, sb = fm_chunk(g, col0, ncols, kind, scale, dsts)
                    if kind == "copy":
                        p.op("vector", lambda e, fb=fb, sb=sb, ncols=ncols, scale=scale: e.tensor_scalar(out=fst[sb][0:ncols, :], in0=Fps[fb][0:ncols, :], scalar1=scale, scalar2=None, op0=ALU.mult),
                             reads=[f"F{fb}"], writes=[f"fst{sb}"])
                    else:
                        fn = AF.Silu if kind == "silu" else AF.Sigmoid
                        p.op("scalar", lambda e, fb=fb, sb=sb, ncols=ncols, fn=fn: e.activation(out=fst[sb][0:ncols, :], in_=Fps[fb][0:ncols, :], func=fn),
                             reads=[f"F{fb}"], writes=[f"fst{sb}"])
                    for (dst, dr0, rlo, rhi) in dsts:
                        p.dma("gpsimd", D[dst][dr0:dr0 + (rhi - rlo), c0:c0 + 512], fst[sb][rlo:rhi, :], reads=[f"fst{sb}"])
                if getattr(self, "stop", 0) == 3:
                    continue
                fb1, _ = fm_chunk(g, k0, 96, "copy", 1.0, None)
                p.op("vector", lambda e, fb1=fb1, rb=rb: e.tensor_tensor(out=t1[64:96, :], in0=Fps[fb1][64:96, :], in1=ropec[rb][64:96, :], op=ALU.mult),
                     reads=[f"F{fb1}", f"ropec{rb}"], writes=["t1"])
                fb2, _ = fm_chunk(g, 0, 96, "copy", 1.0, None, wsrc=Wkpe)
                p.op("vector", lambda e, fb2=fb2, rb=rb: e.tensor_tensor(out=t2[64:96, :], in0=Fps[fb2][64:96, :], in1=ropes[rb][64:96, :], op=ALU.mult),
                     reads=[f"F{fb2}", f"ropes{rb}"], writes=["t2"])
                p.op("vector", lambda e: e.tensor_tensor(out=kpst[64:96, :], in0=t1[64:96, :], in1=t2[64:96, :], op=ALU.add), reads=["t1", "t2"], writes=["kpst"])
                for h in range(4):
                    p.dma("gpsimd", D["kbT"][h, 64:96, c0:c0 + 512], kpst[64:96, :], reads=["kpst"])
                if getattr(self, "stop", 0) == 4:
                    continue
                for t in range(4):
                    r0 = c0 + t * 128
                    vb = t % 2
                    lhs_t = lambda c, t=t: hTg[:, c, t * 128:(t + 1) * 128]
                    for (ps, pname, o0, col0, ncols) in ((Vps, "V", 0, OFF["a_v"], 256), (Vps, "V", 256, OFF["d_v"], 256),
                                                         (Lps, "L", 0, OFF["b_cq"], 352), (Lps, "L", 352, OFF["c_vs"], 64), (Lps, "L", 416, OFF["c_vw"], 64)):
                        for c in range(8):
                            p.op("tensor", lambda e, c=c, ps=ps, o0=o0, col0=col0, ncols=ncols, t=t, hTg=hTg: e.matmul(ps[:, o0:o0 + ncols], lhsT=hTg[:, c, t * 128:(t + 1) * 128], rhs=Win[:, c, col0:col0 + ncols], start=(c == 0), stop=(c == 7)),
                                 reads=[hk, "Win"], writes=[pname])
                    p.op("scalar", lambda e, vb=vb: e.copy(out=vst[vb][:], in_=Vps[:]), reads=["V"], writes=[f"vst{vb}"])
                    p.dma("gpsimd", D["va"][r0:r0 + 128, :], vst[vb][:, 0:256], reads=[f"vst{vb}"])
                    p.dma("gpsimd", D["vd"][r0:r0 + 128, :], vst[vb][:, 256:512], reads=[f"vst{vb}"])
                    p.op("scalar", lambda e, vb=vb: e.copy(out=v2st[vb][:], in_=Lps[:, 352:480]), reads=["L"], writes=[f"v2st{vb}"])
                    p.dma("gpsimd", D["vs"][r0:r0 + 128, :], v2st[vb][:, 0:64], reads=[f"v2st{vb}"])
                    p.dma("gpsimd", D["vw"][r0:r0 + 128, :], v2st[vb][:, 64:128], reads=[f"v2st{vb}"])
                    if getattr(self, "stop", 0) == 5:
                        continue
                    p.op("scalar", lambda e: e.copy(out=lat[:], in_=Lps[:, 0:320]), reads=["L"], writes=["lat"])
                    if getattr(self, "stop", 0) == 63:
                        continue
                    p.op("scalar", lambda e: e.activation(out=sq[:, 0:320], in_=lat[:], func=AF.Square), reads=["lat"], writes=["sq"])
                    if getattr(self, "stop", 0) == 64:
                        continue
                    p.op("vector", lambda e: e.reduce_sum(out=ss[:, 0:1], in_=sq[:, 0:192], axis=AX.X), reads=["sq"], writes=["ss"])
                    p.op("vector", lambda e: e.reduce_sum(out=ss[:, 1:2], in_=sq[:, 192:320], axis=AX.X), reads=["sq"], writes=["ss"])
                    if getattr(self, "stop", 0) == 61:
                        continue
                    p.op("vector", lambda e: e.tensor_scalar(out=rs[:, 0:1], in0=ss[:, 0:1], scalar1=1.0 / 192, scalar2=RMS_EPS, op0=ALU.mult, op1=ALU.add), reads=["ss"], writes=["rs"])
                    p.op("vector", lambda e: e.tensor_scalar(out=rs[:, 1:2], in0=ss[:, 1:2], scalar1=1.0 / 128, scalar2=RMS_EPS, op0=ALU.mult, op1=ALU.add), reads=["ss"], writes=["rs"])
                    p.op("scalar", lambda e: e.sqrt(out=rs[:, 0:2], in_=rs[:, 0:2]), reads=["rs"], writes=["rs"])
                    p.op("vector", lambda e: e.reciprocal(out=rs[:, 0:2], in_=rs[:, 0:2]), reads=["rs"], writes=["rs"])
                    if getattr(self, "stop", 0) == 62:
                        continue
                    p.op("vector", lambda e: e.tensor_scalar(out=cqn[:], in0=lat[:, 0:192], scalar1=rs[:, 0:1], scalar2=None, op0=ALU.mult), reads=["lat", "rs"], writes=["cqn"])
                    p.op("vector", lambda e: e.tensor_scalar(out=ckvn[:], in0=lat[:, 192:320], scalar1=rs[:, 1:2], scalar2=None, op0=ALU.mult), reads=["lat", "rs"], writes=["ckvn"])
                    if getattr(self, "stop", 0) == 6:
                        continue
                    p.op("tensor", lambda e: e.transpose(Mps[:, 0:128], cqn[:, 0:128], ident[:]), reads=["cqn", "ident"], writes=["M"])
                    p.op("tensor", lambda e: e.transpose(Mps[0:64, 128:256], cqn[:, 128:192], ident[:]), reads=["cqn", "ident"], writes=["M"])
                    p.op("tensor", lambda e: e.transpose(Mps[:, 256:384], ckvn[:], ident[:]), reads=["ckvn", "ident"], writes=["M"])
                    p.op("vector", lambda e, t=t: e.tensor_copy(out=cqnT[:, 0, t * 128:(t + 1) * 128], in_=Mps[:, 0:128]), reads=["M"], writes=["cqnT"])
                    p.op("vector", lambda e, t=t: e.tensor_copy(out=cqnT[0:64, 1, t * 128:(t + 1) * 128], in_=Mps[0:64, 128:256]), reads=["M"], writes=["cqnT"])
                    p.op("vector", lambda e, t=t: e.tensor_copy(out=ckvnT[:, t * 128:(t + 1) * 128], in_=Mps[:, 256:384]), reads=["M"], writes=["ckvnT"])
                    if getattr(self, "stop", 0) == 7:
                        continue
                    p.op("tensor", lambda e, t=t: e.matmul(M2ps[:, 0:256], lhsT=ckvnT[:, t * 128:(t + 1) * 128], rhs=Wv[:].rearrange("p h c -> p (h c)"), start=True, stop=True),
                         reads=["ckvnT", "Wv"], writes=["M2"])
                    p.op("scalar", lambda e, vb=vb: e.copy(out=vbst[vb][:], in_=M2ps[:, 0:256]), reads=["M2"], writes=[f"vbst{vb}"])
                    p.dma("gpsimd", D["vb"][r0:r0 + 128, :], vbst[vb][:], reads=[f"vbst{vb}"])
                if getattr(self, "stop", 0) in (5, 6, 7, 8, 61, 62, 63, 64):
                    continue
                for h in range(4):
                    qb = h % 2
                    hc = slice(h * 96, (h + 1) * 96)
                    for (W, pname, ps) in ((Wuq, "M2", M2ps), (Wuqs, "L", Lps)):
                        p.op("tensor", lambda e, W=W, ps=ps, hc=hc: e.matmul(ps[0:96, :], lhsT=W[:, 0, hc], rhs=cqnT[:, 0, :], start=True, stop=False), reads=["Wuq", "Wuqs", "cqnT"], writes=[pname])
                        p.op("tensor", lambda e, W=W, ps=ps, hc=hc: e.matmul(ps[0:96, :], lhsT=W[0:64, 1, hc], rhs=cqnT[0:64, 1, :], start=False, stop=True), reads=["Wuq", "Wuqs", "cqnT"], writes=[pname])
                    p.op("scalar", lambda e, qb=qb: e.copy(out=qbst[qb][0:64, :], in_=M2ps[0:64, :]), reads=["M2"], writes=[f"qbst{qb}"])
                    p.op("vector", lambda e, rb=rb: e.tensor_tensor(out=t1[64:96, :], in0=M2ps[64:96, :], in1=ropec[rb][64:96, :], op=ALU.mult), reads=["M2", f"ropec{rb}"], writes=["t1"])
                    p.op("vector", lambda e, rb=rb: e.tensor_tensor(out=t2[64:96, :], in0=Lps[64:96, :], in1=ropes[rb][64:96, :], op=ALU.mult), reads=["L", f"ropes{rb}"], writes=["t2"])
                    p.op("vector", lambda e, qb=qb: e.tensor_tensor(out=qbst[qb][64:96, :], in0=t1[64:96, :], in1=t2[64:96, :], op=ALU.add), reads=["t1", "t2"], writes=[f"qbst{qb}"])
                    p.dma("gpsimd", D["qbT"][h, :, c0:c0 + 512], qbst[qb][:], reads=[f"qbst{qb}"])
                    p.op("tensor", lambda e, h=h: e.matmul(Vps[0:64, :], lhsT=Wk[:, h, :], rhs=ckvnT[:], start=True, stop=True), reads=["Wk", "ckvnT"], writes=["V"])
                    p.op("scalar", lambda e, qb=qb: e.copy(out=kbst[qb][:], in_=Vps[0:64, :]), reads=["V"], writes=[f"kbst{qb}"])
                    p.dma("gpsimd", D["kbT"][h, 0:64, c0:c0 + 512], kbst[qb][:], reads=[f"kbst{qb}"])
            p.finish()

    def _attn_common(self, st, tag, ns=4):
        nc, p = self.nc, self.p
        C = self.cst

        def T(name, shape, dt):
            return st.enter_context(nc.sbuf_tensor(f"{tag}_{name}", list(shape), dt))

        def PS(name, shape, dt=F32):
            p.excl.add(name)
            return st.enter_context(nc.psum_tensor(f"{tag}_{name}", list(shape), dt))

        cm = dict(T=T, PS=PS)
        cm["ident"] = T("ident", [128, 128], BF16)
        cm["zeros"] = T("zeros", [128, 128], BF16)
        cm["caus"] = T("caus", [128, 1024], BF16)
        p.dma("sync", cm["ident"][:], C["ident"], writes=["ident"])
        p.dma("sync", cm["caus"][:], C["caus"], writes=["caus"])
        p.op("vector", lambda e: e.memset(cm["zeros"][:], 0.0), writes=["zeros"])
        cm["Sps"] = [PS(f"S{i}", [128, 512]) for i in range(ns)]
        cm["accs"] = [(PS("acc", [128, 512]), "acc"), (PS("acc2", [128, 512]), "acc2")]
        cm["acc"] = cm["accs"][0][0]
        self._acc_i = 0
        cm["Pt"] = [T(f"Pt{i}", [128, 512], BF16) for i in range(ns)]
        cm["ns"] = ns
        cm["skew"] = 2 if ns <= 4 else 3
        cm["rinv"] = T("rinv", [64, 512], F32)
        cm["ytmp"] = T("ytmp", [64, 512], F32)
        cm["yst"] = [T(f"yst{i}", [64, 512], BF16) for i in range(2)]
        self._step = 0
        return cm

    def _next_acc(self, cm):
        self._acc_i += 1
        return cm["accs"][self._acc_i % 2]

    def _run_tile(self, cm, blocks, exp_scale, acc=None, acckey="acc"):
        p = self.p
        acc = cm["acc"] if acc is None else acc
        Sps, Pt, zeros, caus = cm["Sps"], cm["Pt"], cm["zeros"], cm["caus"]
        p.op("tensor", lambda e: e.matmul(acc[:, :], lhsT=zeros[:], rhs=caus[:, 0:512], start=True, stop=False),
             reads=["zeros", "caus"], writes=[acckey])

        def pv(bl, sb, last):
            p.op("tensor", lambda e, bl=bl, sb=sb: e.matmul(bl["pv_out"], lhsT=bl["pv_lhsT"], rhs=Pt[sb][0:bl["kr"], 0:bl["n"]], start=False, stop=False),
                 reads=[f"Pt{sb}"] + bl["rd"], writes=[acckey])

        ns = cm["ns"]
        pend = []
        for i, bl in enumerate(blocks):
            sb = self._step % ns
            self._step += 1
            nq = len(bl["qk"])
            for j, (lh, rh) in enumerate(bl["qk"]):
                p.op("tensor", lambda e, lh=lh, rh=rh, sb=sb, bl=bl, j=j, nq=nq: e.matmul(Sps[sb][0:bl["kr"], 0:bl["n"]], lhsT=lh, rhs=rh, start=(j == 0), stop=(j == nq - 1)),
                     reads=bl["rd"] + ["ident", "caus"], writes=[f"S{sb}"])
            p.op("scalar", lambda e, sb=sb, bl=bl: e.activation(out=Pt[sb][0:bl["kr"], 0:bl["n"]], in_=Sps[sb][0:bl["kr"], 0:bl["n"]], func=AF.Exp, scale=exp_scale),
                 reads=[f"S{sb}"], writes=[f"Pt{sb}"])
            pend.append((bl, sb))
            if len(pend) > cm.get("skew", 2):
                pv(pend[0][0], pend[0][1], False)
                pend.pop(0)
        for (bl, sb) in pend:
            pv(bl, sb, False)
        p.op("tensor", lambda e: e.matmul(acc[:, :], lhsT=zeros[:], rhs=caus[:, 0:512], start=False, stop=True),
             reads=["zeros", "caus"], writes=[acckey])

    def _finish_tile(self, cm, g, h, gate_tile, dst, acc=None, acckey="acc", norm=True, gkey="GT"):
        p = self.p
        acc = cm["acc"] if acc is None else acc
        rinv, ytmp = cm["rinv"], cm["ytmp"]
        yb = (g + h) % 2
        yst = cm["yst"][yb]
        c0 = g * 512
        if norm:
            p.op("vector", lambda e: e.tensor_scalar(out=rinv[:], in0=acc[64:128, :], scalar1=1e-30, scalar2=None, op0=ALU.max), reads=[acckey], writes=["rinv"])
            p.op("vector", lambda e: e.reciprocal(out=rinv[:], in_=rinv[:]), reads=["rinv"], writes=["rinv"])
            p.op("vector", lambda e: e.tensor_tensor(out=ytmp[:], in0=acc[0:64, :], in1=rinv[:], op=ALU.mult), reads=[acckey, "rinv"], writes=["ytmp"])
            p.op("gpsimd", lambda e: e.tensor_tensor(out=yst[:], in0=ytmp[:], in1=gate_tile[0:64, c0:c0 + 512], op=ALU.mult), reads=["ytmp", gkey], writes=[f"yst{yb}"])
        else:
            p.op("vector", lambda e: e.tensor_tensor(out=yst[:], in0=acc[0:64, :], in1=gate_tile[0:64, c0:c0 + 512], op=ALU.mult), reads=[acckey, gkey], writes=[f"yst{yb}"])
        p.dma("gpsimd", dst[h * 64:(h + 1) * 64, c0:c0 + 512], yst[:], reads=[f"yst{yb}"])

    def _load_vaug(self, Vt, key, src, h, dil, pieces=8):
        p, NT = self.p, self.NT
        p.op("gpsimd", lambda e: e.memset(Vt[:, :, 64:128], 1.0), writes=[key])
        njb = NT // dil
        view = src[:, h * 64:(h + 1) * 64].rearrange("(jb kk r) c -> kk jb r c", kk=128, r=dil)
        tv = Vt[:, :, 0:64].rearrange("p (jb r) c -> p jb r c", r=dil)
        step = max(1, njb // pieces) if dil == 1 else 1
        for j0 in range(0, njb, step):
            p.dma("sync", tv[:, j0:j0 + step], view[:, j0:j0 + step], writes=[key])

    def phase_a(self, l):
        nc, p, S = self.nc, self.p, self.S
        C, D = self.cst, self.dr
        with contextlib.ExitStack() as st:
            cm = self._attn_common(st, f"A{l}")
            T = cm["T"]
            band = T("band", [128, 256], BF16)
            p.dma("sync", band[:], C["band"], writes=["band"])
            KT = T("KT", [68, S], BF16)
            QT = T("QT", [68, S], BF16)
            GT = T("GT", [64, S], BF16)
            Vd = {d: T(f"V{d}", [128, self.NT, 128], BF16) for d in (1, 4, 16)}
            ident = cm["ident"]
            for h in range(4):
                p.dma("sync", KT[0:64, :], D["kaT"][h * 64:(h + 1) * 64, :], writes=["KT"])
                p.dma("sync", KT[64:68, :], C["kaug"], writes=["KT"])
                p.dma("sync", QT[0:64, :], D["qaT"][h * 64:(h + 1) * 64, :], writes=["QT"])
                p.dma("sync", QT[64:68, :], C["qaug_a"][h], writes=["QT"])
                p.dma("sync", GT[:], D["gaT"][h * 64:(h + 1) * 64, :], writes=["GT"])
                for d in (1, 4, 16):
                    self._load_vaug(Vd[d], f"V{d}", D["va"], h, d)
                for g in range(self.NG):
                    blocks = []
                    q0 = 512 * g
                    accx, acck = self._next_acc(cm)
                    for kb in range(4 * g - 1, 4 * g + 4):
                        if kb < 0:
                            continue
                        lo = max(0, 128 * kb - q0)
                        hi = min(512, 128 * kb - q0 + 256)
                        b0 = lo + q0 - 128 * kb
                        n = hi - lo
                        blocks.append(dict(qk=[(KT[:, 128 * kb:128 * kb + 128], QT[:, q0 + lo:q0 + hi]), (ident[:], band[:, b0:b0 + n])],
                                           kr=128, n=n, pv_lhsT=Vd[1][:, kb, :], pv_out=accx[:, lo:hi], rd=["KT", "QT", "band", "V1"]))
                    for r in range(4):
                        for jb in (g - 1, g):
                            if jb < 0:
                                continue
                            b0 = 0 if jb == g else 128
                            blocks.append(dict(qk=[(KT[:, 512 * jb + r:512 * jb + 512:4], QT[:, q0 + r:q0 + 512:4]), (ident[:], band[:, b0:b0 + 128])],
                                               kr=128, n=128, pv_lhsT=Vd[4][:, jb * 4 + r, :], pv_out=accx[:, r:512:4], rd=["KT", "QT", "band", "V4"]))
                    jb0, o = g // 4, 32 * (g % 4)
                    for r in range(16):
                        for jb in (jb0 - 1, jb0):
                            if jb < 0:
                                continue
                            b0 = o if jb == jb0 else 128 + o
                            blocks.append(dict(qk=[(KT[:, 2048 * jb + r:2048 * jb + 2048:16], QT[:, q0 + r:q0 + 512:16]), (ident[:], band[:, b0:b0 + 32])],
                                               kr=128, n=32, pv_lhsT=Vd[16][:, jb * 16 + r, :], pv_out=accx[:, r:512:16], rd=["KT", "QT", "band", "V16"]))
                    self._run_tile(cm, blocks, 1.0, acc=accx, acckey=acck)
                    self._finish_tile(cm, g, h, GT, D["yaT"], acc=accx, acckey=acck)
            p.finish()

    def phase_b(self, l):
        nc, p, S = self.nc, self.p, self.S
        C, D = self.cst, self.dr
        with contextlib.ExitStack() as st:
            cm = self._attn_common(st, f"B{l}", ns=6)
            T = cm["T"]
            KTs = [T(f"KT{i}", [96, S], BF16) for i in range(2)]
            QTs = [T(f"QT{i}", [96, S], BF16) for i in range(2)]
            GTs = [T(f"GT{i}", [64, S], BF16) for i in range(2)]
            V1s = [T(f"V1{i}", [128, self.NT, 128], BF16) for i in range(2)]
            ident, caus = cm["ident"], cm["caus"]
            for h in range(4):
                hb = h % 2
                KT, QT, GT, V1 = KTs[hb], QTs[hb], GTs[hb], V1s[hb]
                kK, kQ, kG, kV = f"KT{hb}", f"QT{hb}", f"GT{hb}", f"V1{hb}"
                p.dma("sync", KT[:], D["kbT"][h], writes=[kK])
                p.dma("sync", QT[:], D["qbT"][h], writes=[kQ])
                p.dma("sync", GT[:], D["gbT"][h * 64:(h + 1) * 64, :], writes=[kG])
                self._load_vaug(V1, kV, D["vb"], h, 1)
                for g in range(self.NG):
                    blocks = []
                    q0 = 512 * g
                    accx, acck = self._next_acc(cm)
                    for kb in range(0, 4 * g + 4):
                        o = kb - 4 * g
                        if o < 0:
                            blocks.append(dict(qk=[(KT[:, 128 * kb:128 * kb + 128], QT[:, q0:q0 + 512])], kr=128, n=512,
                                               pv_lhsT=V1[:, kb, :], pv_out=accx[:, :], rd=[kK, kQ, kV]))
                        else:
                            lo = 128 * o
                            n = 512 - lo
                            blocks.append(dict(qk=[(KT[:, 128 * kb:128 * kb + 128], QT[:, q0 + lo:q0 + 512]), (ident[:], caus[:, 512:512 + n])], kr=128, n=n,
                                               pv_lhsT=V1[:, kb, :], pv_out=accx[:, lo:512], rd=[kK, kQ, kV]))
                    self._run_tile(cm, blocks, 96.0 ** -0.5, acc=accx, acckey=acck)
                    self._finish_tile(cm, g, h, GT, D["ybT"], acc=accx, acckey=acck, gkey=kG)
            p.finish()

    def phase_d(self, l):
        nc, p, S = self.nc, self.p, self.S
        C, D = self.cst, self.dr
        with contextlib.ExitStack() as st:
            cm = self._attn_common(st, f"D{l}")
            T, PS = cm["T"], cm["PS"]
            KTs = [T(f"KT{i}", [64, S], BF16) for i in range(2)]
            QTs = [T(f"QT{i}", [64, S], BF16) for i in range(2)]
            GTs = [T(f"GT{i}", [64, S], BF16) for i in range(2)]
            V1s = [T(f"V1{i}", [128, self.NT, 128], BF16) for i in range(2)]
            ntri = T("ntri", [128, 128], BF16)
            nones = T("nones", [128, 128], BF16)
            p.dma("sync", ntri[:], C["ntri"], writes=["ntri"])
            p.dma("sync", nones[:], C["nones"], writes=["nones"])
            Ef = [T(f"Ef{i}", [128, 512], F32) for i in range(2)]
            Lp = [T(f"Lp{i}", [128, 512], BF16) for i in range(2)]
            Lsum = T("Lsum", [128, 512], F32)
            Lsb = T("Lsb", [128, 512], BF16)
            Xps = [PS(f"X{i}", [128, 512]) for i in range(2)]
            Sps, Pt = cm["Sps"], cm["Pt"]
            ident, caus, zeros = cm["ident"], cm["caus"], cm["zeros"]
            for h in range(4):
                hb = h % 2
                KT, QT, GT, V1 = KTs[hb], QTs[hb], GTs[hb], V1s[hb]
                kK, kQ, kG, kV = f"KT{hb}", f"QT{hb}", f"GT{hb}", f"V1{hb}"
                p.dma("sync", KT[:], D["kdT"][h * 64:(h + 1) * 64, :], writes=[kK])
                p.dma("sync", QT[:], D["qdT"][h * 64:(h + 1) * 64, :], writes=[kQ])
                p.dma("sync", GT[:], D["gdT"][h * 64:(h + 1) * 64, :], writes=[kG])
                self._load_vaug(V1, kV, D["vd"], h, 1)
                for g in range(self.NG):
                    q0 = 512 * g
                    acc, acck = self._next_acc(cm)
                    p.op("tensor", lambda e, acc=acc: e.matmul(acc[0:64, :], lhsT=zeros[:, 0:64], rhs=caus[:, 0:512], start=True, stop=False), reads=["zeros", "caus"], writes=[acck])
                    p.op("gpsimd", lambda e: e.memset(Lsum[:], 0.0), writes=["Lsum"])
                    p.op("gpsimd", lambda e: e.memset(Lsb[:], 0.0), writes=["Lsb"])
                    blks = []
                    for kb in range(4 * g + 3, -1, -1):
                        o = kb - 4 * g
                        lo = 128 * o if o >= 0 else 0
                        n = 512 - lo
                        qk = [(KT[:, 128 * kb:128 * kb + 128], QT[:, q0 + lo:q0 + 512])]
                        if o >= 0:
                            qk.append((ident[:], caus[:, 511:511 + n]))
                        blks.append((kb, lo, n, qk))
                    nb = len(blks)

                    def z1(i):
                        kb, lo, n, qk = blks[i]
                        sb = i % 2
                        for j, (lh, rh) in enumerate(qk):
                            p.op("tensor", lambda e, lh=lh, rh=rh, sb=sb, n=n, j=j, nq=len(qk): e.matmul(Sps[sb][:, 0:n], lhsT=lh, rhs=rh, start=(j == 0), stop=(j == nq - 1)),
                                 reads=[kK, kQ, "ident", "caus"], writes=[f"S{sb}"])

                    def EE(i):
                        kb, lo, n, qk = blks[i]
                        sb = i % 2
                        p.op("scalar", lambda e, sb=sb, n=n: e.activation(out=Ef[sb][:, 0:n], in_=Sps[sb][:, 0:n], func=AF.Exp), reads=[f"S{sb}"], writes=[f"Ef{sb}"])

                    def LP(i):
                        kb, lo, n, qk = blks[i]
                        sb = i % 2
                        p.op("scalar", lambda e, sb=sb, n=n: e.activation(out=Lp[sb][:, 0:n], in_=Ef[sb][:, 0:n], func=AF.Ln, bias=1.0), reads=[f"Ef{sb}"], writes=[f"Lp{sb}"])

                    def XX(i):
                        kb, lo, n, qk = blks[i]
                        sb = i % 2
                        mm = qk + [(ntri[:], Lp[sb][:, 0:n]), (nones[:], Lsb[:, lo:512])]
                        for j, (lh, rh) in enumerate(mm):
                            p.op("tensor", lambda e, lh=lh, rh=rh, sb=sb, n=n, j=j, nq=len(mm): e.matmul(Xps[sb][:, 0:n], lhsT=lh, rhs=rh, start=(j == 0), stop=(j == nq - 1)),
                                 reads=[kK, kQ, "ident", "caus", "ntri", "nones", f"Lp{sb}", "Lsb"], writes=[f"X{sb}"])

                    def AA(i):
                        kb, lo, n, qk = blks[i]
                        sb = i % 2
                        p.op("scalar", lambda e, sb=sb, n=n: e.activation(out=Pt[sb][:, 0:n], in_=Xps[sb][:, 0:n], func=AF.Exp), reads=[f"X{sb}"], writes=[f"Pt{sb}"])

                    def PV(i):
                        kb, lo, n, qk = blks[i]
                        sb = i % 2
                        p.op("tensor", lambda e, sb=sb, n=n, kb=kb, lo=lo, acc=acc, V1=V1: e.matmul(acc[0:64, lo:512], lhsT=V1[:, kb, 0:64], rhs=Pt[sb][:, 0:n], start=False, stop=False),
                             reads=[f"Pt{sb}", kV], writes=[acck])

                    def LS(i):
                        kb, lo, n, qk = blks[i]
                        sb = i % 2
                        if i < nb - 1:
                            p.op("vector", lambda e, sb=sb, n=n, lo=lo: e.tensor_tensor(out=Lsum[:, lo:512], in0=Lsum[:, lo:512], in1=Lp[sb][:, 0:n], op=ALU.add), reads=["Lsum", f"Lp{sb}"], writes=["Lsum"])
                            p.op("vector", lambda e: e.tensor_copy(out=Lsb[:], in_=Lsum[:]), reads=["Lsum"], writes=["Lsb"])

                    z1(0); EE(0)
                    if nb > 1:
                        z1(1); EE(1)
                    LP(0)
                    for i in range(nb):
                        if i + 2 < nb:
                            z1(i + 2); EE(i + 2)
                        if i + 1 < nb:
                            LP(i + 1)
                        XX(i); AA(i)
                        if i >= 1:
                            PV(i - 1)
                        LS(i)
                    PV(nb - 1)
                    p.op("tensor", lambda e, acc=acc: e.matmul(acc[0:64, :], lhsT=zeros[:, 0:64], rhs=caus[:, 0:512], start=False, stop=True), reads=["zeros", "caus"], writes=[acck])
                    self._finish_tile(cm, g, h, GT, D["ydT"], norm=False, acc=acc, acckey=acck, gkey=kG)
            p.finish()

    def phase_m(self, l, x_src, last):
        nc, p, S = self.nc, self.p, self.S
        I, C, D = self.inp, self.cst, self.dr
        with contextlib.ExitStack() as st:
            def T(name, shape, dt):
                return st.enter_context(nc.sbuf_tensor(f"M{l}_{name}", list(shape), dt))

            def PS(name, shape, dt=F32):
                p.excl.add(name)
                return st.enter_context(nc.psum_tensor(f"M{l}_{name}", list(shape), dt))

            Wm = [T(f"Wm{i}", [128, 8, 1024], BF16) for i in range(4)]
            Wb = T("Wb", [128, 4, 2, 1024], BF16)
            Wo = T("Wo", [128, 8, 1024], BF16)
            bm = T("bm", [128, 4, 8], F32)
            gT = T("gT", [128, 8], F32)
            wst = [T(f"wst{i}", [128, 1024], F32) for i in range(2)]
            hTgs = [T(f"hTg{i}", [128, 8, 512], BF16) for i in range(2)]
            YTs = [T(f"YT{i}", [128, 4, 2, 512], BF16) for i in range(2)]
            Gs = [T(f"Gs{i}", [128, 512], F32) for i in range(2)]
            mrg = T("mrg", [128, 8, 512], F32)
            mrgb = T("mrgb", [128, 8, 512], BF16)
            tmp = T("tmp", [128, 512], F32)
            xres = [T(f"xres{i}", [128, 1024], F32) for i in range(2)]
            x1t = [T(f"x1t{i}", [128, 1024], F32) for i in range(2)]
            gfin = T("gfin", [128, 1024], F32)
            sq = T("sq", [128, 1024], F32)
            ss = T("ss", [128, 2], F32)
            Gp = [PS(f"Gp{i}", [128, 512]) for i in range(2)]
            Bp = [PS(f"Bp{i}", [128, 512]) for i in range(2)]
            Op = [PS(f"Op{i}", [128, 512]) for i in range(2)]

            p.dma("sync", gT[:], I["norm_g"][l].rearrange("(c p) -> p c", p=128), writes=["gT"], allow_slow_non_contiguous=True)
            for i in range(4):
                p.dma("sync", bm[:, i, :], I["b_merge"][l, i].rearrange("(c p) -> p c", p=128), writes=["bm"], allow_slow_non_contiguous=True)
            if last:
                p.dma("sync", gfin[:], I["final_norm_g"].partition_broadcast(128), writes=["gfin"])
            k = 0
            for i in range(4):
                for c in range(8):
                    b = k % 2
                    k += 1
                    p.dma("sync", wst[b][:], I["w_merge"][l, i, c * 128:(c + 1) * 128, :], writes=[f"wst{b}"])
                    p.op("vector", lambda e, i=i, c=c, b=b: e.tensor_scalar(out=Wm[i][:, c, :], in0=wst[b][:], scalar1=gT[:, c:c + 1], scalar2=None, op0=ALU.mult),
                         reads=[f"wst{b}", "gT"], writes=["Wm"])
            for i in range(4):
                for c in range(2):
                    b = k % 2
                    k += 1
                    p.dma("sync", wst[b][:], I["w_branch"][l, i, c * 128:(c + 1) * 128, :], writes=[f"wst{b}"])
                    p.op("vector", lambda e, i=i, c=c, b=b: e.tensor_copy(out=Wb[:, i, c, :], in_=wst[b][:]), reads=[f"wst{b}"], writes=["Wb"])
            for c in range(8):
                b = k % 2
                k += 1
                p.dma("sync", wst[b][:], I["w_out"][l, c * 128:(c + 1) * 128, :], writes=[f"wst{b}"])
                p.op("vector", lambda e, c=c, b=b: e.tensor_copy(out=Wo[:, c, :], in_=wst[b][:]), reads=[f"wst{b}"], writes=["Wo"])

            step = 0
            for g in range(self.NG):
                c0 = g * 512
                hTg, YT, hk, yk = hTgs[g % 2], YTs[g % 2], f"hTg{g % 2}", f"YT{g % 2}"
                p.dma("sync", hTg[:], D["hT"].rearrange("(c p) s -> p c s", p=128)[:, :, c0:c0 + 512], writes=[hk])
                for i, m in enumerate("abcd"):
                    p.dma("sync", YT[:, i, :, :], D[f"y{m}T"].rearrange("(c p) s -> p c s", p=128)[:, :, c0:c0 + 512], writes=[yk])
                for cc in range(8):
                    cs = slice(cc * 128, (cc + 1) * 128)
                    for i in range(4):
                        fb = step % 2
                        step += 1
                        for c in range(8):
                            p.op("tensor", lambda e, i=i, c=c, cs=cs, fb=fb, hTg=hTg: e.matmul(Gp[fb][:, :], lhsT=Wm[i][:, c, cs], rhs=hTg[:, c, :], start=(c == 0), stop=(c == 7)),
                                 reads=["Wm", hk], writes=[f"Gp{fb}"])
                        p.op("scalar", lambda e, i=i, cc=cc, fb=fb: e.activation(out=Gs[fb][:], in_=Gp[fb][:, :], func=AF.Sigmoid, bias=bm[:, i, cc:cc + 1]),
                             reads=[f"Gp{fb}", "bm"], writes=[f"Gs{fb}"])
                        for c in range(2):
                            p.op("tensor", lambda e, i=i, c=c, cs=cs, fb=fb, YT=YT: e.matmul(Bp[fb][:, :], lhsT=Wb[:, i, c, cs], rhs=YT[:, i, c, :], start=(c == 0), stop=(c == 1)),
                                 reads=["Wb", yk], writes=[f"Bp{fb}"])
                        if i == 0:
                            p.op("vector", lambda e, cc=cc, fb=fb: e.tensor_tensor(out=mrg[:, cc, :], in0=Bp[fb][:, :], in1=Gs[fb][:], op=ALU.mult), reads=[f"Bp{fb}", f"Gs{fb}"], writes=["mrg"])
                        else:
                            p.op("vector", lambda e, fb=fb: e.tensor_tensor(out=tmp[:], in0=Bp[fb][:, :], in1=Gs[fb][:], op=ALU.mult), reads=[f"Bp{fb}", f"Gs{fb}"], writes=["tmp"])
                            p.op("vector", lambda e, cc=cc: e.tensor_tensor(out=mrg[:, cc, :], in0=mrg[:, cc, :], in1=tmp[:], op=ALU.add), reads=["tmp", "mrg"], writes=["mrg"])
                    p.op("scalar", lambda e, cc=cc: e.copy(out=mrgb[:, cc, :], in_=mrg[:, cc, :]), reads=["mrg"], writes=["mrgb"])
                for t in range(4):
                    r0 = c0 + t * 128
                    xb = t % 2
                    p.dma("sync", xres[xb][:], x_src[r0:r0 + 128, :], writes=[f"xres{xb}"])
                    for half in range(2):
                        for cc in range(8):
                            p.op("tensor", lambda e, cc=cc, t=t, half=half: e.matmul(Op[half][:, :], lhsT=mrgb[:, cc, t * 128:(t + 1) * 128], rhs=Wo[:, cc, half * 512:(half + 1) * 512], start=(cc == 0), stop=(cc == 7)),
                                 reads=["mrgb", "Wo"], writes=[f"Op{half}"])
                        p.op("vector", lambda e, half=half, xb=xb: e.tensor_tensor(out=x1t[xb][:, half * 512:(half + 1) * 512], in0=Op[half][:, :], in1=xres[xb][:, half * 512:(half + 1) * 512], op=ALU.add),
                             reads=[f"Op{half}", f"xres{xb}"], writes=[f"x1t{xb}"])
                    if not last:
                        p.dma("gpsimd", D["x1"][r0:r0 + 128, :], x1t[xb][:], reads=[f"x1t{xb}"])
                    else:
                        p.op("scalar", lambda e, xb=xb: e.activation(out=sq[:], in_=x1t[xb][:], func=AF.Square), reads=[f"x1t{xb}"], writes=["sq"])
                        p.op("vector", lambda e: e.reduce_sum(out=ss[:, 0:1], in_=sq[:], axis=AX.X), reads=["sq"], writes=["ss"])
                        p.op("vector", lambda e: e.tensor_scalar(out=ss[:, 1:2], in0=ss[:, 0:1], scalar1=1.0 / D_MODEL, scalar2=RMS_EPS, op0=ALU.mult, op1=ALU.add), reads=["ss"], writes=["ss"])
                        p.op("scalar", lambda e: e.sqrt(out=ss[:, 1:2], in_=ss[:, 1:2]), reads=["ss"], writes=["ss"])
                        p.op("vector", lambda e: e.reciprocal(out=ss[:, 1:2], in_=ss[:, 1:2]), reads=["ss"], writes=["ss"])
                        p.op("scalar", lambda e, xb=xb: e.activation(out=sq[:], in_=x1t[xb][:], func=AF.Copy, scale=ss[:, 1:2]), reads=[f"x1t{xb}", "ss"], writes=["sq"])
                        p.op("vector", lambda e, xb=xb: e.tensor_tensor(out=x1t[xb][:], in0=sq[:], in1=gfin[:], op=ALU.mult), reads=["sq", "gfin"], writes=[f"x1t{xb}"])
                        p.dma("gpsimd", self.out[r0:r0 + 128, :], x1t[xb][:], reads=[f"x1t{xb}"])
            p.finish()

    def phase_c(self, l):
        nc, p, S = self.nc, self.p, self.S
        I, C, D = self.inp, self.cst, self.dr
        NT = self.NT
        ncmp = S // 16 - 1
        NCH = max(1, S // 2048)
        with contextlib.ExitStack() as st:
            cm = self._attn_common(st, f"C{l}")
            T, PS = cm["T"], cm["PS"]
            ident, caus, zeros = cm["ident"], cm["caus"], cm["zeros"]
            KcT = T("KcT", [68, 512], BF16)
            Vc = T("Vc", [128, 4, 128], BF16)
            with contextlib.ExitStack() as st2:
                def T2(name, shape, dt):
                    return st2.enter_context(nc.sbuf_tensor(f"C{l}_{name}", list(shape), dt))
                srcT = [T2("kcT", [64, S], BF16), T2("vcT", [64, S], BF16)]
                w1s = T2("w1s", [64, 32, 128], F32)
                W1 = T2("W1", [64, 32, 128], BF16)
                posf = T2("posf", [64, 32], F32)
                posT = T2("posT", [64, 32], BF16)
                b1 = T2("b1", [128, 1], F32)
                bias = T2("bias", [128, 1], F32)
                w2s = T2("w2s", [128, 64], F32)
                W2 = T2("W2", [128, 64], BF16)
                b2c = T2("b2c", [64, 1], F32)
                b2r = T2("b2r", [128, 64], F32)
                hb = T2("hb", [128, 512], F32)
                u = T2("u", [128, 512], F32)
                hg = T2("hg", [128, 512], BF16)
                Hps, Cps = cm["Sps"][0], cm["Sps"][1]
                p.op("vector", lambda e: e.memset(KcT[:], 0.0), writes=["KcT"])
                p.dma("sync", KcT[64:68, :], C["kaugc"], writes=["KcT"])
                p.op("gpsimd", lambda e: e.memset(Vc[:, :, 64:128], 1.0), writes=["Vc"])
                p.op("gpsimd", lambda e: e.memset(Vc[:, :, 0:64], 0.0), writes=["Vc"])
                p.dma("sync", srcT[0][:], D["kcT"], writes=["src0"])
                p.dma("sync", srcT[1][:], D["vcT"], writes=["src1"])
                for kv in range(2):
                    p.dma("sync", w1s[:], I["nsa_w1"][l, kv].rearrange("(ll d) h -> d ll h", d=64), writes=["w1s"])
                    p.op("vector", lambda e: e.tensor_copy(out=W1[:], in_=w1s[:]), reads=["w1s"], writes=["W1"])
                    p.dma("sync", posf[:], I["nsa_pos"][l, kv].rearrange("ll d -> d ll"), writes=["posf"], allow_slow_non_contiguous=True)
                    p.op("vector", lambda e: e.tensor_copy(out=posT[:], in_=posf[:]), reads=["posf"], writes=["posT"])
                    p.dma("sync", b1[:], I["nsa_b1"][l, kv].rearrange("(p o) -> p o", o=1), writes=["b1"], allow_slow_non_contiguous=True)
                    p.dma("sync", w2s[:], I["nsa_w2"][l, kv], writes=["w2s"])
                    p.op("vector", lambda e: e.tensor_copy(out=W2[:], in_=w2s[:]), reads=["w2s"], writes=["W2"])
                    for ll in range(32):
                        p.op("tensor", lambda e, ll=ll: e.matmul(Cps[:, 0:1], lhsT=W1[:, ll, :], rhs=posT[:, ll:ll + 1], start=(ll == 0), stop=(ll == 31)), reads=["W1", "posT"], writes=["S1"])
                    p.op("vector", lambda e: e.tensor_tensor(out=bias[:], in0=Cps[:, 0:1], in1=b1[:], op=ALU.add), reads=["S1", "b1"], writes=["bias"])
                    for ll in range(32):
                        p.op("tensor", lambda e, ll=ll, kv=kv: e.matmul(Hps[:, 0:ncmp], lhsT=W1[:, ll, :], rhs=srcT[kv][:, ll:ll + 16 * (ncmp - 1) + 1:16], start=(ll == 0), stop=(ll == 31)),
                             reads=["W1", f"src{kv}"], writes=["S0"])
                    p.op("scalar", lambda e: e.activation(out=hb[:, 0:ncmp], in_=Hps[:, 0:ncmp], func=AF.Identity, bias=bias[:, 0:1]), reads=["S0", "bias"], writes=["hb"])
                    p.op("vector", lambda e: e.tensor_tensor(out=u[:, 0:ncmp], in0=hb[:, 0:ncmp], in1=hb[:, 0:ncmp], op=ALU.mult), reads=["hb"], writes=["u"])
                    p.op("vector", lambda e: e.tensor_scalar(out=u[:, 0:ncmp], in0=u[:, 0:ncmp], scalar1=0.044715, scalar2=1.0, op0=ALU.mult, op1=ALU.add), reads=["u"], writes=["u"])
                    p.op("vector", lambda e: e.tensor_tensor(out=u[:, 0:ncmp], in0=u[:, 0:ncmp], in1=hb[:, 0:ncmp], op=ALU.mult), reads=["u", "hb"], writes=["u"])
                    p.op("scalar", lambda e: e.activation(out=u[:, 0:ncmp], in_=u[:, 0:ncmp], func=AF.Sigmoid, scale=1.5957691216057308), reads=["u"], writes=["u"])
                    p.op("vector", lambda e: e.memset(hg[:], 0.0), writes=["hg"])
                    p.op("vector", lambda e: e.tensor_tensor(out=hg[:, 0:ncmp], in0=u[:, 0:ncmp], in1=hb[:, 0:ncmp], op=ALU.mult), reads=["u", "hb"], writes=["hg"])
                    if kv == 0:
                        p.dma("sync", b2c[:], I["nsa_b2"][l, 0].rearrange("(p o) -> p o", o=1), writes=["b2c"], allow_slow_non_contiguous=True)
                        p.op("tensor", lambda e: e.matmul(Cps[0:64, 0:ncmp], lhsT=W2[:], rhs=hg[:, 0:ncmp], start=True, stop=True), reads=["W2", "hg"], writes=["S1"])
                        p.op("scalar", lambda e: e.activation(out=KcT[0:64, 0:ncmp], in_=Cps[0:64, 0:ncmp], func=AF.Identity, bias=b2c[:, 0:1]), reads=["S1", "b2c"], writes=["KcT"])
                    else:
                        p.dma("sync", b2r[:], I["nsa_b2"][l, 1].partition_broadcast(128), writes=["b2r"])
                        for ch in range(NCH):
                            p.op("tensor", lambda e, ch=ch: e.matmul(Cps[:, 0:64], lhsT=hg[:, ch * 128:(ch + 1) * 128], rhs=W2[:], start=True, stop=True), reads=["W2", "hg"], writes=["S1"])
                            p.op("vector", lambda e, ch=ch: e.tensor_tensor(out=Vc[:, ch, 0:64], in0=Cps[:, 0:64], in1=b2r[:], op=ALU.add), reads=["S1", "b2r"], writes=["Vc"])
                p.finish()
            KsT = T("KsT", [68, S], BF16)
            KwT = T("KwT", [68, S], BF16)
            Vs = T("Vs", [128, NT, 128], BF16)
            Vw = T("Vw", [128, NT, 128], BF16)
            ewide = T("ewide", [128, S], BF16)
            antic = T("antic", [128, 1024], BF16)
            cmk = T("cmk", [128, 2560], BF16)
            selmap = T("selmap", [128, 4, 128], BF16)
            tkm = T("tkm", [128, 256], F32)
            tka = T("tka", [128, 256], F32)
            Qt = [T(f"Qt{i}", [68, 4, 512], BF16) for i in range(2)]
            Gt = [T(f"Gt{i}", [64, 4, 512], BF16) for i in range(2)]
            cg = [T(f"cg{i}", [64, 12, 512], BF16) for i in range(2)]
            ysum = T("ysum", [64, 4, 512], F32)
            psel = T("psel", [128, 4, 128], F32)
            rq = T("rq", [128, 4], F32)
            sc = T("sc", [128, 128], F32)
            sc2 = T("sc2", [128, 128], F32)
            m8 = T("m8", [128, 16], F32)
            selb = T("selb", [128, 128], BF16)
            selm1T = T("selm1T", [128, 512], BF16)
            psQ = PS("psQ", [128, 4, 128])
            p.op("vector", lambda e: e.memset(psel[:], 0.0), writes=["psel"])
            acc2 = cm["accs"][1][0]
            TP = PS("TP", [128, 128], BF16)
            acc = cm["acc"]
            Sps, Pt = cm["Sps"], cm["Pt"]
            rinv, ytmp = cm["rinv"], cm["ytmp"]
            for (dst, src, aug) in ((KsT, "ksT", "kaug"), (KwT, "kwT", "kaug")):
                p.dma("sync", dst[0:64, :], D[src], writes=[src])
                p.dma("sync", dst[64:68, :], C[aug], writes=[src])
            self._load_vaug(Vs, "Vs", D["vs"], 0, 1)
            self._load_vaug(Vw, "Vw", D["vw"], 0, 1)
            p.dma("sync", ewide[:], C["ewide"], writes=["ewide"])
            p.dma("sync", antic[:], C["antic"], writes=["antic"])
            p.dma("sync", cmk[:], C["cm"], writes=["cmk"])
            p.dma("sync", selmap[:], C["selmap"].rearrange("(c n) j -> n c j", n=128)[:, :, 1:129], writes=["selmap"])
            p.dma("sync", tkm[:], C["tk_mult"], writes=["tkm"])
            p.dma("sync", tka[:], C["tk_add"], writes=["tka"])

            def branch_out(accx, acckey, h, gidx, cgb, first):
                p.op("vector", lambda e: e.tensor_scalar(out=rinv[:], in0=accx[64:128, :], scalar1=1e-30, scalar2=None, op0=ALU.max), reads=[acckey], writes=["rinv"])
                p.op("vector", lambda e: e.reciprocal(out=rinv[:], in_=rinv[:]), reads=["rinv"], writes=["rinv"])
                p.op("vector", lambda e: e.tensor_tensor(out=ytmp[:], in0=accx[0:64, :], in1=rinv[:], op=ALU.mult), reads=[acckey, "rinv"], writes=["ytmp"])
                if first:
                    p.op("vector", lambda e: e.tensor_tensor(out=ysum[:, h, :], in0=ytmp[:], in1=cg[cgb][:, gidx, :], op=ALU.mult), reads=["ytmp", f"cg{cgb}"], writes=["ysum"])
                else:
                    p.op("vector", lambda e: e.tensor_tensor(out=ytmp[:], in0=ytmp[:], in1=cg[cgb][:, gidx, :], op=ALU.mult), reads=["ytmp", f"cg{cgb}"], writes=["ytmp"])
                    p.op("vector", lambda e: e.tensor_tensor(out=ysum[:, h, :], in0=ysum[:, h, :], in1=ytmp[:], op=ALU.add), reads=["ytmp", "ysum"], writes=["ysum"])

            for g in range(self.NG):
                q0 = 512 * g
                qb = g % 2
                for h in range(4):
                    p.dma("sync", Qt[qb][0:64, h, :], D["qcT"][h * 64:(h + 1) * 64, q0:q0 + 512], writes=[f"Qt{qb}"])
                    p.dma("sync", Qt[qb][64:68, h, :], C["qaug_c"][h, :, q0:q0 + 512], writes=[f"Qt{qb}"])
                    p.dma("sync", Gt[qb][:, h, :], D["gcT"][h * 64:(h + 1) * 64, q0:q0 + 512], writes=[f"Gt{qb}"])
                for j in range(12):
                    p.dma("sync", cg[qb][:, j, :], D["cgT"][j:j + 1, q0:q0 + 512].partition_broadcast(64), writes=[f"cg{qb}"])
                chs = [ch for ch in range(NCH) if q0 - 2048 * ch >= 0]
                for h in range(4):
                    p.op("tensor", lambda e: e.matmul(acc[:, :], lhsT=zeros[:], rhs=caus[:, 0:512], start=True, stop=False), reads=["zeros", "caus"], writes=["acc"])
                    used = []
                    for ci, ch in enumerate(chs):
                        sb = self._step % cm["ns"]
                        self._step += 1
                        dlt = q0 - 2048 * ch
                        qk = [(KcT[:, ch * 128:(ch + 1) * 128], Qt[qb][:, h, :])]
                        if dlt < 2560:
                            qk.append((ident[:], cmk[:, dlt:dlt + 512]))
                        for j, (lh, rh) in enumerate(qk):
                            p.op("tensor", lambda e, lh=lh, rh=rh, sb=sb, j=j, nq=len(qk): e.matmul(Sps[sb][:, :], lhsT=lh, rhs=rh, start=(j == 0), stop=(j == nq - 1)),
                                 reads=["KcT", f"Qt{qb}", "ident", "cmk"], writes=[f"S{sb}"])
                        p.op("scalar", lambda e, sb=sb: e.activation(out=Pt[sb][:], in_=Sps[sb][:, :], func=AF.Exp), reads=[f"S{sb}"], writes=[f"Pt{sb}"])
                        p.op("tensor", lambda e, sb=sb, ch=ch: e.matmul(acc[:, :], lhsT=Vc[:, ch, :], rhs=Pt[sb][:], start=False, stop=False), reads=[f"Pt{sb}", "Vc"], writes=["acc"])
                        used.append((sb, ch))
                    p.op("tensor", lambda e: e.matmul(acc[:, :], lhsT=zeros[:], rhs=caus[:, 0:512], start=False, stop=True), reads=["zeros", "caus"], writes=["acc"])
                    for qs in range(4):
                        for ci, (sb, ch) in enumerate(used):
                            p.op("tensor", lambda e, sb=sb, ch=ch, qs=qs, ci=ci, nch=len(used): e.matmul(psQ[:, qs, :], lhsT=Pt[sb][:, qs * 128:(qs + 1) * 128], rhs=selmap[:, ch, :], start=(ci == 0), stop=(ci == nch - 1)),
                                 reads=[f"Pt{sb}", "selmap"], writes=["psQ"])
                    branch_out(acc, "acc", h, h * 3 + 0, qb, True)
                    for qs in range(4):
                        p.op("vector", lambda e, qs=qs: e.tensor_scalar(out=rq[:, qs:qs + 1], in0=psQ[:, qs, 127:128], scalar1=1e-30, scalar2=None, op0=ALU.max), reads=["psQ"], writes=["rq"])
                        p.op("vector", lambda e, qs=qs: e.reciprocal(out=rq[:, qs:qs + 1], in_=rq[:, qs:qs + 1]), reads=["rq"], writes=["rq"])
                        if h == 0:
                            p.op("vector", lambda e, qs=qs: e.tensor_scalar(out=psel[:, qs, 1:128], in0=psQ[:, qs, 0:127], scalar1=rq[:, qs:qs + 1], scalar2=None, op0=ALU.mult), reads=["psQ", "rq"], writes=["psel"])
                        else:
                            p.op("vector", lambda e, qs=qs: e.scalar_tensor_tensor(out=psel[:, qs, 1:128], in0=psQ[:, qs, 0:127], scalar=rq[:, qs:qs + 1], in1=psel[:, qs, 1:128], op0=ALU.mult, op1=ALU.add), reads=["psQ", "rq", "psel"], writes=["psel"])
                for qs in range(4):
                    m = 4 * g + qs
                    j0 = 128 - 2 * m
                    p.op("vector", lambda e, qs=qs, j0=j0: e.tensor_tensor(out=sc[:], in0=psel[:, qs, :], in1=tkm[:, j0:j0 + 128], op=ALU.mult), reads=["psel", "tkm"], writes=["sc"])
                    p.op("vector", lambda e, j0=j0: e.tensor_tensor(out=sc[:], in0=sc[:], in1=tka[:, j0:j0 + 128], op=ALU.add), reads=["sc", "tka"], writes=["sc"])
                    p.op("vector", lambda e: e.memset(sc[:, 0:1], 1e9), reads=["sc"], writes=["sc"])
                    p.op("vector", lambda e: e.max(out=m8[:, 0:8], in_=sc[:]), reads=["sc"], writes=["m8"])
                    p.op("vector", lambda e: e.match_replace(out=sc2[:], in_to_replace=m8[:, 0:8], in_values=sc[:], imm_value=-2.0), reads=["sc", "m8"], writes=["sc2"])
                    p.op("vector", lambda e: e.max(out=m8[:, 8:16], in_=sc2[:]), reads=["sc2"], writes=["m8"])
                    p.op("vector", lambda e: e.tensor_scalar(out=selb[:], in0=sc[:], scalar1=m8[:, 15:16], scalar2=1.0, op0=ALU.is_ge, op1=ALU.subtract), reads=["sc", "m8"], writes=["selb"])
                    p.op("tensor", lambda e: e.transpose(TP[:, :], selb[:], ident[:]), reads=["selb", "ident"], writes=["TP"])
                    p.op("vector", lambda e, qs=qs: e.tensor_copy(out=selm1T[:, qs * 128:(qs + 1) * 128], in_=TP[:, :]), reads=["TP"], writes=["selm1T"])
                for h in range(4):
                    blocks = []
                    for kb in range(0, 4 * g + 4):
                        o = kb - 4 * g
                        lo = 128 * o if o >= 0 else 0
                        n = 512 - lo
                        qk = [(KsT[:, 128 * kb:128 * kb + 128], Qt[qb][:, h, lo:512]), (ewide[:, 128 * kb:128 * kb + 128], selm1T[:, lo:512])]
                        if o >= 0:
                            qk.append((ident[:], caus[:, 512:512 + n]))
                        blocks.append(dict(qk=qk, kr=128, n=n, pv_lhsT=Vs[:, kb, :], pv_out=acc2[:, lo:512], rd=["ksT", f"Qt{qb}", "ewide", "selm1T", "Vs"]))
                    self._run_tile(cm, blocks, 1.0, acc=acc2, acckey="acc2")
                    branch_out(acc2, "acc2", h, h * 3 + 1, qb, False)
                    blocks = []
                    for kb in range(max(0, 4 * g - 4), 4 * g + 4):
                        o = kb - 4 * g
                        if o >= 0:
                            lo, hi = 128 * o, 512
                            msk = caus[:, 512:512 + (hi - lo)]
                        else:
                            lo, hi = 0, min(512, 640 + 128 * o)
                            msk = antic[:, -128 * o:-128 * o + hi]
                        blocks.append(dict(qk=[(KwT[:, 128 * kb:128 * kb + 128], Qt[qb][:, h, lo:hi]), (ident[:], msk)], kr=128, n=hi - lo,
                                           pv_lhsT=Vw[:, kb, :], pv_out=acc[:, lo:hi], rd=["kwT", f"Qt{qb}", "antic", "Vw"]))
                    self._run_tile(cm, blocks, 1.0)
                    branch_out(acc, "acc", h, h * 3 + 2, qb, False)
                    yb = (g + h) % 2
                    yst = cm["yst"][yb]
                    p.op("vector", lambda e, h=h, yst=yst, qb=qb: e.tensor_tensor(out=yst[:], in0=ysum[:, h, :], in1=Gt[qb][:, h, :], op=ALU.mult), reads=["ysum", f"Gt{qb}"], writes=[f"yst{yb}"])
                    p.dma("gpsimd", D["ycT"][h * 64:(h + 1) * 64, q0:q0 + 512], yst[:], reads=[f"yst{yb}"])
            p.finish()

    def build_all(self):
        self.declare()
        x_src = self.inp["x"]
        for l in range(self.n_layers):
            last = (l == self.n_layers - 1)
            self.phase_proj(l, x_src)
            self.phase_a(l)
            self.phase_b(l)
            self.phase_c(l)
            self.phase_d(l)
            self.phase_m(l, x_src, last)
            x_src = self.dr["x1"]
        return self.nc


_CACHE = {}


def _get_builder(S):
    if S not in _CACHE:
        b = Builder(S)
        b.build_all()
        _CACHE[S] = b
    return _CACHE[S]


def kernel(x, norm_g, w_in, mla_q_norm, mla_w_uq, mla_kv_norm, mla_w_ukv, nsa_pos, nsa_w1,
           nsa_b1, nsa_w2, nsa_b2, w_branch, w_merge, b_merge, w_out, final_norm_g):
    x = np.asarray(x, dtype=np.float32)
    Bn, S, Dm = x.shape
    bld = _get_builder(S)
    shared = dict(norm_g=norm_g, w_in=w_in, mla_q_norm=mla_q_norm, mla_w_uq=mla_w_uq, mla_kv_norm=mla_kv_norm,
                  mla_w_ukv=mla_w_ukv, nsa_pos=nsa_pos, nsa_w1=nsa_w1, nsa_b1=nsa_b1, nsa_w2=nsa_w2, nsa_b2=nsa_b2,
                  w_branch=w_branch, w_merge=w_merge, b_merge=b_merge, w_out=w_out, final_norm_g=final_norm_g)
    shared = {k: np.ascontiguousarray(np.asarray(v, dtype=np.float32)) for k, v in shared.items()}
    for k, v in bld.consts_np.items():
        shared["c_" + k] = v
    n_cores = 8
    in_maps = []
    for c in range(n_cores):
        m = dict(shared)
        m["x"] = np.ascontiguousarray(x[(c // 2) % Bn])
        in_maps.append(m)
    res = run_bass_kernel_spmd(bld.nc, in_maps, core_ids=list(range(n_cores)))
    out = np.empty((Bn, S, Dm), np.float32)
    half = S // 2
    for b in range(Bn):
        out[b, :half] = res.results[2 * b]["out"][:half]
        out[b, half:] = res.results[2 * b + 1]["out"][half:]
    return out
```

```python
import contextlib
import numpy as np
import ml_dtypes
import concourse.bass as bass
import concourse.mybir as mybir
from concourse.bass_utils import run_bass_kernel_spmd

F32 = mybir.dt.float32
BF16 = mybir.dt.bfloat16
ALU = mybir.AluOpType
AF = mybir.ActivationFunctionType
AX = mybir.AxisListType

D_MODEL = 1024
DEPTH = 2
RMS_EPS = 1e-6
BIG = 30000.0

IN_SPLITS = (
    ("a_q", 256), ("a_k", 256), ("a_v", 256), ("a_gate", 256),
    ("b_cq", 192), ("b_ckv", 128), ("b_kpe", 32), ("b_gate", 256),
    ("c_q", 256), ("c_kc", 64), ("c_vc", 64), ("c_ks", 64),
    ("c_vs", 64), ("c_kw", 64), ("c_vw", 64), ("c_g", 12),
    ("c_gate", 256),
    ("d_q", 256), ("d_k", 256), ("d_v", 256), ("d_gate", 256),
)
OFF = {}
_o = 0
for _n, _w in IN_SPLITS:
    OFF[_n] = _o
    _o += _w
D_IN = _o

ENGS = ("tensor", "vector", "scalar", "gpsimd", "sync")


class Prog:
    NDMA = 32

    def __init__(self, nc, stack):
        self.nc = nc
        self.sem = {e: stack.enter_context(nc.semaphore(f"s_{e}")) for e in ENGS if e != "sync"}
        self.dsem = [stack.enter_context(nc.semaphore(f"d{i}")) for i in range(self.NDMA)]
        self.pbA = stack.enter_context(nc.semaphore("pbA"))
        self.pbB = stack.enter_context(nc.semaphore("pbB"))
        self.nphase = 0
        self.excl = set()
        self.semobj = {("c", e): self.sem[e] for e in self.sem}
        for i in range(self.NDMA):
            self.semobj[("d", i)] = self.dsem[i]
        self.begin()

    def begin(self):
        self.ops = {e: [] for e in ENGS}
        self.lastw = {}
        self.readers = {}
        if not hasattr(self, "cnt"):
            self.cnt = {e: 0 for e in ENGS}
            self.dcnt = [0] * self.NDMA
            self.dnext = [0, 0]
            self.waited = {e: {} for e in ENGS}

    def _wait(self, eng, tok):
        key, val, src = tok
        w = self.waited[eng]
        if w.get(key, 0) >= val:
            return
        w[key] = val
        sem = self.semobj[key]
        self.ops[eng].append(lambda e, sem=sem, val=val: e.wait_ge(sem, val))

    def _deps(self, eng, reads, writes, is_dma=False):
        def need(t):
            return is_dma or t[2] != eng or eng != "tensor"
        for b in reads:
            for t in self.lastw.get(b, {}).values():
                if need(t):
                    self._wait(eng, t)
            if b in self.excl:
                for t in self.readers.get(b, ()):
                    if t[2] != eng:
                        self._wait(eng, t)
        for b in writes:
            for t in self.lastw.get(b, {}).values():
                if need(t):
                    self._wait(eng, t)
            for t in self.readers.get(b, ()):
                if need(t):
                    self._wait(eng, t)

    def _record(self, tok, reads, writes):
        for b in writes:
            if tok[2] == "dma":
                self.lastw.setdefault(b, {})[tok[0]] = tok
            else:
                self.lastw[b] = {tok[0]: tok}
            self.readers[b] = []
        for b in reads:
            self.readers.setdefault(b, []).append(tok)

    def op(self, eng, fn, reads=(), writes=()):
        self._deps(eng, reads, writes)
        self.cnt[eng] += 1
        tok = (("c", eng), self.cnt[eng], eng)
        sem = self.sem[eng]
        self.ops[eng].append(lambda e, fn=fn, sem=sem: fn(e).then_inc(sem, 1))
        self._record(tok, reads, writes)
        return tok

    def dma(self, eng, out, in_, reads=(), writes=(), **kw):
        self._deps(eng, reads, writes, is_dma=True)
        half = self.NDMA // 2
        qi = 0 if eng == "sync" else 1
        i = qi * half + self.dnext[qi]
        self.dnext[qi] = (self.dnext[qi] + 1) % half
        key = ("d", i)
        if self.dcnt[i] > 0:
            self._wait(eng, (key, self.dcnt[i], "dma"))
        self.dcnt[i] += 16
        tok = (key, self.dcnt[i], "dma")
        sem = self.dsem[i]
        self.ops[eng].append(lambda e, out=out, in_=in_, sem=sem, kw=kw: e.dma_start(out=out, in_=in_, **kw).then_inc(sem, 16))
        self._record(tok, reads, writes)
        return tok

    def finish(self):
        final = []
        for e in ENGS:
            if e != "sync" and self.cnt[e] > 0:
                final.append((("c", e), self.cnt[e], e))
        for i in range(self.NDMA):
            if self.dcnt[i] > 0:
                final.append((("d", i), self.dcnt[i], "dma"))
        for e in ENGS:
            for t in final:
                self._wait(e, t)
        ops = self.ops
        with self.nc.Block() as block:
            @block.tensor
            def _(e):
                for f in ops["tensor"]:
                    f(e)

            @block.vector
            def _(e):
                for f in ops["vector"]:
                    f(e)

            @block.scalar
            def _(e):
                for f in ops["scalar"]:
                    f(e)

            @block.gpsimd
            def _(e):
                for f in ops["gpsimd"]:
                    f(e)

            @block.sync
            def _(e):
                for f in ops["sync"]:
                    f(e)
        self.begin()


def _bf(a):
    return np.asarray(a, dtype=np.float32).astype(ml_dtypes.bfloat16)


def make_consts(S):
    c = {}
    k = np.arange(128)[:, None]
    c["ident"] = _bf(np.eye(128))
    j = np.arange(1024)[None, :]
    c["caus"] = _bf(np.where(k <= j - 512, 0.0, -BIG))
    c["antic"] = _bf(np.where(k > j - 512, 0.0, -BIG))
    j = np.arange(256)[None, :]
    c["band"] = _bf(np.where((j - k >= 0) & (j - k <= 128), 0.0, -BIG))
    j = np.arange(2560)[None, :]
    c["cm"] = _bf(np.where(j >= 16 * k + 31, 0.0, -BIG))
    kk = np.arange(S)[None, :]
    c["ewide"] = _bf(np.where(k == kk // 64, BIG, 0.0))
    c["ntri"] = _bf(np.where(k >= np.arange(128)[None, :], -1.0, 0.0))
    c["nones"] = _bf(-np.ones((128, 128)))
    half = 16
    inv = (10000.0 ** (-np.arange(half, dtype=np.float32) / half)).astype(np.float32)
    pos = np.arange(S, dtype=np.float32)
    ang = (pos[None, :] * inv[:, None]).astype(np.float32)
    cos = np.cos(ang).astype(np.float32)
    sin = np.sin(ang).astype(np.float32)
    c["ropec"] = np.concatenate([cos, cos], 0).astype(np.float32)
    c["ropes"] = np.concatenate([-sin, sin], 0).astype(np.float32)
    t = np.arange(S)
    a, b = t // 128, t % 128
    c["kaug"] = _bf(np.stack([np.ones(S), 128.0 * a, np.ones(S), b], 0))
    ncmp = S // 16 - 1
    pc = 16 * np.arange(512) + 31
    ac, bc = pc // 128, pc % 128
    c["kaugc"] = _bf(np.stack([np.ones(512), 128.0 * ac, np.ones(512), bc], 0))
    sl = 2.0 ** (-(np.arange(8) + 1.0))
    sa, sc = sl[0::2], sl[1::2]
    c["qaug_a"] = _bf(np.stack([np.stack([-s * 128.0 * a, s * np.ones(S), -s * b, s * np.ones(S)], 0) for s in sa], 0))
    c["qaug_c"] = _bf(np.stack([np.stack([-s * 128.0 * a, s * np.ones(S), -s * b, s * np.ones(S)], 0) for s in sc], 0))
    qq = np.arange(128)[:, None]
    jj = np.arange(256)[None, :] - 128
    cur = (qq >= 64).astype(np.int64)
    mult = np.where((jj <= cur) & (jj != cur) & (jj != cur - 1), 1.0, 0.0)
    add = np.where(jj > cur, -1.0, np.where(jj == cur, 2e9, np.where(jj == cur - 1, 3e9, 0.0)))
    c["tk_mult"] = mult.astype(np.float32)
    c["tk_add"] = add.astype(np.float32)
    n_sel = S // 64
    sm = np.zeros((512, 129), np.float32)
    for jv in range(min(n_sel, 128)):
        for aa in range(4):
            for cc in range(2):
                n = jv * 4 - aa - cc
                if 0 <= n < ncmp:
                    sm[n, jv] += 1.0
    sm[:, 128] = 1.0
    c["selmap"] = _bf(sm)
    return c


CONST_DT = {"ropec": F32, "ropes": F32, "tk_mult": F32, "tk_add": F32}


class Builder:
    def __init__(self, S, debug=(), n_layers=DEPTH, out_rows=None):
        self.S = S
        self.NG = S // 512
        self.NT = S // 128
        self.debug = set(debug)
        self.n_layers = n_layers
        nc = self.nc = bass.Bass("TRN2", target_bir_lowering=False)
        self.stack = contextlib.ExitStack()
        self.p = Prog(nc, self.stack)
        self.consts_np = make_consts(S)
        self.inp = {}
        self.dr = {}

    def ext_in(self, name, shape, dt=F32):
        t = self.nc.dram_tensor(name, list(shape), dt, kind="ExternalInput").ap()
        self.inp[name] = t
        return t

    def scratch(self, name, shape, dt=BF16):
        kind = "ExternalOutput" if name in self.debug else "Internal"
        t = self.nc.dram_tensor(name, list(shape), dt, kind=kind).ap()
        self.dr[name] = t
        return t

    def declare(self):
        S = self.S
        L = DEPTH
        self.ext_in("x", [S, D_MODEL])
        self.ext_in("norm_g", [L, D_MODEL])
        self.ext_in("w_in", [L, D_MODEL, D_IN])
        self.ext_in("mla_q_norm", [L, 192])
        self.ext_in("mla_w_uq", [L, 192, 384])
        self.ext_in("mla_kv_norm", [L, 128])
        self.ext_in("mla_w_ukv", [L, 128, 512])
        self.ext_in("nsa_pos", [L, 2, 32, 64])
        self.ext_in("nsa_w1", [L, 2, 2048, 128])
        self.ext_in("nsa_b1", [L, 2, 128])
        self.ext_in("nsa_w2", [L, 2, 128, 64])
        self.ext_in("nsa_b2", [L, 2, 64])
        self.ext_in("w_branch", [L, 4, 256, D_MODEL])
        self.ext_in("w_merge", [L, 4, D_MODEL, D_MODEL])
        self.ext_in("b_merge", [L, 4, D_MODEL])
        self.ext_in("w_out", [L, D_MODEL, D_MODEL])
        self.ext_in("final_norm_g", [D_MODEL])
        self.cst = {}
        for k, v in self.consts_np.items():
            self.cst[k] = self.ext_in("c_" + k, v.shape, CONST_DT.get(k, BF16))
        sc = self.scratch
        sc("hT", [D_MODEL, S])
        for m in "acd":
            sc(f"q{m}T", [256, S])
            sc(f"g{m}T", [256, S])
        sc("gbT", [256, S])
        sc("kaT", [256, S]); sc("kdT", [256, S])
        sc("va", [S, 256]); sc("vd", [S, 256]); sc("vb", [S, 256])
        sc("vs", [S, 64]); sc("vw", [S, 64])
        sc("kcT", [64, S]); sc("vcT", [64, S]); sc("ksT", [64, S]); sc("kwT", [64, S])
        sc("cgT", [12, S])
        sc("qbT", [4, 96, S]); sc("kbT", [4, 96, S])
        for m in "abcd":
            sc(f"y{m}T", [256, S])
        sc("x1", [S, D_MODEL], F32)
        self.out = self.nc.dram_tensor("out", [S, D_MODEL], F32, kind="ExternalOutput").ap()

    def phase_proj(self, l, x_src):
        nc, p, S = self.nc, self.p, self.S
        I, C, D = self.inp, self.cst, self.dr
        with contextlib.ExitStack() as st:
            def T(name, shape, dt):
                return st.enter_context(nc.sbuf_tensor(f"P{l}_{name}", list(shape), dt))

            def PS(name, shape, dt=F32):
                p.excl.add(name)
                return st.enter_context(nc.psum_tensor(f"P{l}_{name}", list(shape), dt))

            Win = T("Win", [128, 8, D_IN], BF16)
            wst = [T(f"wst{i}", [128, D_IN], F32) for i in range(2)]
            gT = T("gT", [128, 8], F32)
            ident = T("ident", [128, 128], BF16)
            Wuq = T("Wuq", [128, 2, 384], BF16)
            Wuqs = T("Wuqs", [128, 2, 384], BF16)
            uqst = T("uqst", [128, 2, 384], F32)
            gq = T("gq", [128, 2], F32)
            Wk = T("Wk", [128, 4, 64], BF16)
            Wv = T("Wv", [128, 4, 64], BF16)
            kvst = T("kvst", [128, 4, 128], F32)
            gkv = T("gkv", [128, 1], F32)
            Wkpe = T("Wkpe", [128, 8, 96], BF16)
            xst = [T(f"xst{i}", [128, D_MODEL], F32) for i in range(2)]
            sq = T("sq", [128, D_MODEL], F32)
            xn = [T(f"xn{i}", [128, D_MODEL], BF16) for i in range(2)]
            ss = T("ss", [128, 4], F32)
            rs = T("rs", [128, 4], F32)
            hTgs = [T(f"hTg{i}", [128, 8, 512], BF16) for i in range(2)]
            fst = [T(f"fst{i}", [128, 512], BF16) for i in range(3)]
            vst = [T(f"vst{i}", [128, 512], BF16) for i in range(2)]
            v2st = [T(f"v2st{i}", [128, 128], BF16) for i in range(2)]
            vbst = [T(f"vbst{i}", [128, 256], BF16) for i in range(2)]
            lat = T("lat", [128, 320], F32)
            cqn = T("cqn", [128, 192], BF16)
            ckvn = T("ckvn", [128, 128], BF16)
            cqnT = T("cqnT", [128, 2, 512], BF16)
            ckvnT = T("ckvnT", [128, 512], BF16)
            ropec = [T(f"ropec{i}", [96, 512], F32) for i in range(2)]
            ropes = [T(f"ropes{i}", [96, 512], F32) for i in range(2)]
            t1 = T("t1", [96, 512], F32)
            t2 = T("t2", [96, 512], F32)
            qbst = [T(f"qbst{i}", [96, 512], BF16) for i in range(2)]
            kbst = [T(f"kbst{i}", [64, 512], BF16) for i in range(2)]
            kpst = T("kpst", [96, 512], BF16)

            Fps = [PS(f"F{i}", [128, 512]) for i in range(2)]
            Tps = [PS(f"T{i}", [128, 1024], BF16) for i in range(2)]
            Vps = PS("V", [128, 512])
            Lps = PS("L", [128, 512])
            Mps = PS("M", [128, 1024], BF16)
            M2ps = PS("M2", [128, 512])

            p.dma("sync", ident[:], C["ident"], writes=["ident"])
            p.dma("sync", gT[:], I["norm_g"][l].rearrange("(c p) -> p c", p=128), writes=["gT"],
                  allow_slow_non_contiguous=True)
            for c in range(8):
                b = c % 2
                p.dma("sync", wst[b][:], I["w_in"][l, c * 128:(c + 1) * 128, :], writes=[f"wst{b}"])
                p.op("vector", lambda e, c=c, b=b: e.tensor_scalar(out=Win[:, c, :], in0=wst[b][:], scalar1=gT[:, c:c + 1], scalar2=None, op0=ALU.mult),
                     reads=[f"wst{b}", "gT"], writes=["Win"])
            k0 = OFF["b_kpe"] - 64
            p.op("vector", lambda e: e.tensor_copy(out=Wkpe[:, :, 0:64], in_=Win[:, :, k0:k0 + 64]), reads=["Win"], writes=["Wkpe"])
            p.op("vector", lambda e: e.tensor_copy(out=Wkpe[:, :, 64:80], in_=Win[:, :, k0 + 80:k0 + 96]), reads=["Win"], writes=["Wkpe"])
            p.op("vector", lambda e: e.tensor_copy(out=Wkpe[:, :, 80:96], in_=Win[:, :, k0 + 64:k0 + 80]), reads=["Win"], writes=["Wkpe"])
            p.dma("sync", gq[:, 0:1], I["mla_q_norm"][l, 0:128].rearrange("(p o) -> p o", o=1), writes=["gq"], allow_slow_non_contiguous=True)
            p.dma("sync", gq[0:64, 1:2], I["mla_q_norm"][l, 128:192].rearrange("(p o) -> p o", o=1), writes=["gq"], allow_slow_non_contiguous=True)
            p.dma("sync", uqst[:, 0, :], I["mla_w_uq"][l, 0:128, :], writes=["uqst"])
            p.dma("sync", uqst[0:64, 1, :], I["mla_w_uq"][l, 128:192, :], writes=["uqst"])
            p.op("vector", lambda e: e.tensor_scalar(out=Wuq[:, 0, :], in0=uqst[:, 0, :], scalar1=gq[:, 0:1], scalar2=None, op0=ALU.mult), reads=["uqst", "gq"], writes=["Wuq"])
            p.op("vector", lambda e: e.tensor_scalar(out=Wuq[0:64, 1, :], in0=uqst[0:64, 1, :], scalar1=gq[0:64, 1:2], scalar2=None, op0=ALU.mult), reads=["uqst", "gq"], writes=["Wuq"])
            p.op("vector", lambda e: e.tensor_copy(out=Wuqs[:, 0, :], in_=Wuq[:, 0, :]), reads=["Wuq"], writes=["Wuqs"])
            p.op("vector", lambda e: e.tensor_copy(out=Wuqs[0:64, 1, :], in_=Wuq[0:64, 1, :]), reads=["Wuq"], writes=["Wuqs"])
            for h in range(4):
                o = h * 96 + 64
                for (dlo, slo) in ((o, o + 16), (o + 16, o)):
                    p.op("vector", lambda e, dlo=dlo, slo=slo: e.tensor_copy(out=Wuqs[:, 0, dlo:dlo + 16], in_=Wuq[:, 0, slo:slo + 16]), reads=["Wuq"], writes=["Wuqs"])
                    p.op("vector", lambda e, dlo=dlo, slo=slo: e.tensor_copy(out=Wuqs[0:64, 1, dlo:dlo + 16], in_=Wuq[0:64, 1, slo:slo + 16]), reads=["Wuq"], writes=["Wuqs"])
            p.dma("sync", gkv[:, 0:1], I["mla_kv_norm"][l].rearrange("(p o) -> p o", o=1), writes=["gkv"], allow_slow_non_contiguous=True)
            p.dma("sync", kvst[:], I["mla_w_ukv"][l].rearrange("r (h c) -> r h c", h=4), writes=["kvst"])
            p.op("vector", lambda e: e.tensor_scalar(out=Wk[:], in0=kvst[:, :, 0:64], scalar1=gkv[:, 0:1], scalar2=None, op0=ALU.mult), reads=["kvst", "gkv"], writes=["Wk"])
            p.op("vector", lambda e: e.tensor_scalar(out=Wv[:], in0=kvst[:, :, 64:128], scalar1=gkv[:, 0:1], scalar2=None, op0=ALU.mult), reads=["kvst", "gkv"], writes=["Wv"])

            if getattr(self, "stop", 0) == 1:
                p.finish(); return
            FM = []
            for m, nm in (("a", "a_q"), ("c", "c_q"), ("d", "d_q")):
                for cc in range(2):
                    FM.append((OFF[nm] + cc * 128, 128, "copy", 0.125, [(f"q{m}T", cc * 128, 0, 128)]))
            for dst, nm in (("kaT", "a_k"), ("kdT", "d_k")):
                for cc in range(2):
                    FM.append((OFF[nm] + cc * 128, 128, "copy", 1.0, [(dst, cc * 128, 0, 128)]))
            FM.append((OFF["c_kc"], 128, "copy", 1.0, [("kcT", 0, 0, 64), ("vcT", 0, 64, 128)]))
            FM.append((OFF["c_ks"], 128, "copy", 1.0, [("ksT", 0, 0, 64)]))
            FM.append((OFF["c_kw"], 128, "copy", 1.0, [("kwT", 0, 0, 64)]))
            for dst, nm in (("gaT", "a_gate"), ("gbT", "b_gate"), ("gcT", "c_gate"), ("gdT", "d_gate")):
                for cc in range(2):
                    FM.append((OFF[nm] + cc * 128, 128, "silu", 1.0, [(dst, cc * 128, 0, 128)]))
            FM.append((OFF["c_g"], 12, "sigmoid", 1.0, [("cgT", 0, 0, 12)]))

            fcount = [0]

            def fm_chunk(g, col0, ncols, kind, scale, dsts, wsrc=None):
                hTg, hk = hTgs[g % 2], f"hTg{g % 2}"
                i = fcount[0]
                fcount[0] += 1
                fb = i % 2
                sb = i % 3
                for c in range(8):
                    lhs = Win[:, c, col0:col0 + ncols] if wsrc is None else wsrc[:, c, :]
                    p.op("tensor", lambda e, c=c, lhs=lhs, fb=fb, hTg=hTg: e.matmul(Fps[fb][0:ncols, :], lhsT=lhs, rhs=hTg[:, c, :], start=(c == 0), stop=(c == 7)),
                         reads=[hk, "Win", "Wkpe"], writes=[f"F{fb}"])
                return fb, sb

            for g in range(self.NG):
                c0 = g * 512
                rb = g % 2
                hTg, hk = hTgs[g % 2], f"hTg{g % 2}"
                p.dma("sync", ropec[rb][64:96, :], C["ropec"][:, c0:c0 + 512], writes=[f"ropec{rb}"])
                p.dma("sync", ropes[rb][64:96, :], C["ropes"][:, c0:c0 + 512], writes=[f"ropes{rb}"])
                for t in range(4):
                    r0 = c0 + t * 128
                    xb = t % 2
                    p.dma("sync", xst[xb][:], x_src[r0:r0 + 128, :], writes=[f"xst{xb}"])
                    p.op("scalar", lambda e, xb=xb: e.activation(out=sq[:], in_=xst[xb][:], func=AF.Square), reads=[f"xst{xb}"], writes=["sq"])
                    p.op("vector", lambda e, t=t: e.reduce_sum(out=ss[:, t:t + 1], in_=sq[:], axis=AX.X), reads=["sq"], writes=["ss"])
                    p.op("vector", lambda e, t=t: e.tensor_scalar(out=rs[:, t:t + 1], in0=ss[:, t:t + 1], scalar1=1.0 / D_MODEL, scalar2=RMS_EPS, op0=ALU.mult, op1=ALU.add), reads=["ss"], writes=["rs"])
                    p.op("scalar", lambda e, t=t: e.sqrt(out=rs[:, t:t + 1], in_=rs[:, t:t + 1]), reads=["rs"], writes=["rs"])
                    p.op("vector", lambda e, t=t: e.reciprocal(out=rs[:, t:t + 1], in_=rs[:, t:t + 1]), reads=["rs"], writes=["rs"])
                    p.op("scalar", lambda e, xb=xb, t=t: e.activation(out=xn[xb][:], in_=xst[xb][:], func=AF.Copy, scale=rs[:, t:t + 1]), reads=[f"xst{xb}", "rs"], writes=[f"xn{xb}"])
                    for c in range(8):
                        p.op("tensor", lambda e, c=c, xb=xb: e.transpose(Tps[xb][:, c * 128:(c + 1) * 128], xn[xb][:, c * 128:(c + 1) * 128], ident[:]),
                             reads=[f"xn{xb}", "ident"], writes=[f"T{xb}"])
                    p.op("vector", lambda e, xb=xb, t=t, hTg=hTg: e.tensor_copy(out=hTg[:, :, t * 128:(t + 1) * 128], in_=Tps[xb][:].rearrange("p (c s) -> p c s", c=8)),
                         reads=[f"T{xb}"], writes=[hk])
                p.dma("gpsimd", D["hT"].rearrange("(c p) s -> p c s", p=128)[:, :, c0:c0 + 512], hTg[:], reads=[hk])
                if getattr(self, "stop", 0) == 2:
                    continue
                for (col0, ncols, kind, scale, dsts) in FM:
                    fb, sb = fm_chunk(g, col0, ncols, kind, scale, dsts)
                    if kind == "copy":
                        p.op("vector", lambda e, fb=fb, sb=sb, ncols=ncols, scale=scale: e.tensor_scalar(out=fst[sb][0:ncols, :], in0=Fps[fb][0:ncols, :], scalar1=scale, scalar2=None, op0=ALU.mult),
                             reads=[f"F{fb}"], writes=[f"fst{sb}"])
                    else:
                        fn = AF.Silu if kind == "silu" else AF.Sigmoid
                        p.op("scalar", lambda e, fb=fb, sb=sb, ncols=ncols, fn=fn: e.activation(out=fst[sb][0:ncols, :], in_=Fps[fb][0:ncols, :], func=fn),
                             reads=[f"F{fb}"], writes=[f"fst{sb}"])
                    for (dst, dr0, rlo, rhi) in dsts:
                        p.dma("gpsimd", D[dst][dr0:dr0 + (rhi - rlo), c0:c0 + 512], fst[sb][rlo:rhi, :], reads=[f"fst{sb}"])
                if getattr(self, "stop", 0) == 3:
                    continue
                fb1, _ = fm_chunk(g, k0, 96, "copy", 1.0, None)
                p.op("vector", lambda e, fb1=fb1, rb=rb: e.tensor_tensor(out=t1[64:96, :], in0=Fps[fb1][64:96, :], in1=ropec[rb][64:96, :], op=ALU.mult),
                     reads=[f"F{fb1}", f"ropec{rb}"], writes=["t1"])
                fb2, _ = fm_chunk(g, 0, 96, "copy", 1.0, None, wsrc=Wkpe)
                p.op("vector", lambda e, fb2=fb2, rb=rb: e.tensor_tensor(out=t2[64:96, :], in0=Fps[fb2][64:96, :], in1=ropes[rb][64:96, :], op=ALU.mult),
                     reads=[f"F{fb2}", f"ropes{rb}"], writes=["t2"])
                p.op("vector", lambda e: e.tensor_tensor(out=kpst[64:96, :], in0=t1[64:96, :], in1=t2[64:96, :], op=ALU.add), reads=["t1", "t2"], writes=["kpst"])
                for h in range(4):
                    p.dma("gpsimd", D["kbT"][h, 64:96, c0:c0 + 512], kpst[64:96, :], reads=["kpst"])
                if getattr(self, "stop", 0) == 4:
                    continue
                for t in range(4):
                    r0 = c0 + t * 128
                    vb = t % 2
                    lhs_t = lambda c, t=t: hTg[:, c, t * 128:(t + 1) * 128]
                    for (ps, pname, o0, col0, ncols) in ((Vps, "V", 0, OFF["a_v"], 256), (Vps, "V", 256, OFF["d_v"], 256),
                                                         (Lps, "L", 0, OFF["b_cq"], 352), (Lps, "L", 352, OFF["c_vs"], 64), (Lps, "L", 416, OFF["c_vw"], 64)):
                        for c in range(8):
                            p.op("tensor", lambda e, c=c, ps=ps, o0=o0, col0=col0, ncols=ncols, t=t, hTg=hTg: e.matmul(ps[:, o0:o0 + ncols], lhsT=hTg[:, c, t * 128:(t + 1) * 128], rhs=Win[:, c, col0:col0 + ncols], start=(c == 0), stop=(c == 7)),
                                 reads=[hk, "Win"], writes=[pname])
                    p.op("scalar", lambda e, vb=vb: e.copy(out=vst[vb][:], in_=Vps[:]), reads=["V"], writes=[f"vst{vb}"])
                    p.dma("gpsimd", D["va"][r0:r0 + 128, :], vst[vb][:, 0:256], reads=[f"vst{vb}"])
                    p.dma("gpsimd", D["vd"][r0:r0 + 128, :], vst[vb][:, 256:512], reads=[f"vst{vb}"])
                    p.op("scalar", lambda e, vb=vb: e.copy(out=v2st[vb][:], in_=Lps[:, 352:480]), reads=["L"], writes=[f"v2st{vb}"])
                    p.dma("gpsimd", D["vs"][r0:r0 + 128, :], v2st[vb][:, 0:64], reads=[f"v2st{vb}"])
                    p.dma("gpsimd", D["vw"][r0:r0 + 128, :], v2st[vb][:, 64:128], reads=[f"v2st{vb}"])
                    if getattr(self, "stop", 0) == 5:
                        continue
                    p.op("scalar", lambda e: e.copy(out=lat[:], in_=Lps[:, 0:320]), reads=["L"], writes=["lat"])
                    if getattr(self, "stop", 0) == 63:
                        continue
                    p.op("scalar", lambda e: e.activation(out=sq[:, 0:320], in_=lat[:], func=AF.Square), reads=["lat"], writes=["sq"])
                    if getattr(self, "stop", 0) == 64:
                        continue
                    p.op("vector", lambda e: e.reduce_sum(out=ss[:, 0:1], in_=sq[:, 0:192], axis=AX.X), reads=["sq"], writes=["ss"])
                    p.op("vector", lambda e: e.reduce_sum(out=ss[:, 1:2], in_=sq[:, 192:320], axis=AX.X), reads=["sq"], writes=["ss"])
                    if getattr(self, "stop", 0) == 61:
                        continue
                    p.op("vector", lambda e: e.tensor_scalar(out=rs[:, 0:1], in0=ss[:, 0:1], scalar1=1.0 / 192, scalar2=RMS_EPS, op0=ALU.mult, op1=ALU.add), reads=["ss"], writes=["rs"])
                    p.op("vector", lambda e: e.tensor_scalar(out=rs[:, 1:2], in0=ss[:, 1:2], scalar1=1.0 / 128, scalar2=RMS_EPS, op0=ALU.mult, op1=ALU.add), reads=["ss"], writes=["rs"])
                    p.op("scalar", lambda e: e.sqrt(out=rs[:, 0:2], in_=rs[:, 0:2]), reads=["rs"], writes=["rs"])
                    p.op("vector", lambda e: e.reciprocal(out=rs[:, 0:2], in_=rs[:, 0:2]), reads=["rs"], writes=["rs"])
                    if getattr(self, "stop", 0) == 62:
                        continue
                    p.op("vector", lambda e: e.tensor_scalar(out=cqn[:], in0=lat[:, 0:192], scalar1=rs[:, 0:1], scalar2=None, op0=ALU.mult), reads=["lat", "rs"], writes=["cqn"])
                    p.op("vector", lambda e: e.tensor_scalar(out=ckvn[:], in0=lat[:, 192:320], scalar1=rs[:, 1:2], scalar2=None, op0=ALU.mult), reads=["lat", "rs"], writes=["ckvn"])
                    if getattr(self, "stop", 0) == 6:
                        continue
                    p.op("tensor", lambda e: e.transpose(Mps[:, 0:128], cqn[:, 0:128], ident[:]), reads=["cqn", "ident"], writes=["M"])
                    p.op("tensor", lambda e: e.transpose(Mps[0:64, 128:256], cqn[:, 128:192], ident[:]), reads=["cqn", "ident"], writes=["M"])
                    p.op("tensor", lambda e: e.transpose(Mps[:, 256:384], ckvn[:], ident[:]), reads=["ckvn", "ident"], writes=["M"])
                    p.op("vector", lambda e, t=t: e.tensor_copy(out=cqnT[:, 0, t * 128:(t + 1) * 128], in_=Mps[:, 0:128]), reads=["M"], writes=["cqnT"])
                    p.op("vector", lambda e, t=t: e.tensor_copy(out=cqnT[0:64, 1, t * 128:(t + 1) * 128], in_=Mps[0:64, 128:256]), reads=["M"], writes=["cqnT"])
                    p.op("vector", lambda e, t=t: e.tensor_copy(out=ckvnT[:, t * 128:(t + 1) * 128], in_=Mps[:, 256:384]), reads=["M"], writes=["ckvnT"])
                    if getattr(self, "stop", 0) == 7:
                        continue
                    p.op("tensor", lambda e, t=t: e.matmul(M2ps[:, 0:256], lhsT=ckvnT[:, t * 128:(t + 1) * 128], rhs=Wv[:].rearrange("p h c -> p (h c)"), start=True, stop=True),
                         reads=["ckvnT", "Wv"], writes=["M2"])
                    p.op("scalar", lambda e, vb=vb: e.copy(out=vbst[vb][:], in_=M2ps[:, 0:256]), reads=["M2"], writes=[f"vbst{vb}"])
                    p.dma("gpsimd", D["vb"][r0:r0 + 128, :], vbst[vb][:], reads=[f"vbst{vb}"])
                if getattr(self, "stop", 0) in (5, 6, 7, 8, 61, 62, 63, 64):
                    continue
                for h in range(4):
                    qb = h % 2
                    hc = slice(h * 96, (h + 1) * 96)
                    for (W, pname, ps) in ((Wuq, "M2", M2ps), (Wuqs, "L", Lps)):
                        p.op("tensor", lambda e, W=W, ps=ps, hc=hc: e.matmul(ps[0:96, :], lhsT=W[:, 0, hc], rhs=cqnT[:, 0, :], start=True, stop=False), reads=["Wuq", "Wuqs", "cqnT"], writes=[pname])
                        p.op("tensor", lambda e, W=W, ps=ps, hc=hc: e.matmul(ps[0:96, :], lhsT=W[0:64, 1, hc], rhs=cqnT[0:64, 1, :], start=False, stop=True), reads=["Wuq", "Wuqs", "cqnT"], writes=[pname])
                    p.op("scalar", lambda e, qb=qb: e.copy(out=qbst[qb][0:64, :], in_=M2ps[0:64, :]), reads=["M2"], writes=[f"qbst{qb}"])
                    p.op("vector", lambda e, rb=rb: e.tensor_tensor(out=t1[64:96, :], in0=M2ps[64:96, :], in1=ropec[rb][64:96, :], op=ALU.mult), reads=["M2", f"ropec{rb}"], writes=["t1"])
                    p.op("vector", lambda e, rb=rb: e.tensor_tensor(out=t2[64:96, :], in0=Lps[64:96, :], in1=ropes[rb][64:96, :], op=ALU.mult), reads=["L", f"ropes{rb}"], writes=["t2"])
                    p.op("vector", lambda e, qb=qb: e.tensor_tensor(out=qbst[qb][64:96, :], in0=t1[64:96, :], in1=t2[64:96, :], op=ALU.add), reads=["t1", "t2"], writes=[f"qbst{qb}"])
                    p.dma("gpsimd", D["qbT"][h, :, c0:c0 + 512], qbst[qb][:], reads=[f"qbst{qb}"])
                    p.op("tensor", lambda e, h=h: e.matmul(Vps[0:64, :], lhsT=Wk[:, h, :], rhs=ckvnT[:], start=True, stop=True), reads=["Wk", "ckvnT"], writes=["V"])
                    p.op("scalar", lambda e, qb=qb: e.copy(out=kbst[qb][:], in_=Vps[0:64, :]), reads=["V"], writes=[f"kbst{qb}"])
                    p.dma("gpsimd", D["kbT"][h, 0:64, c0:c0 + 512], kbst[qb][:], reads=[f"kbst{qb}"])
            p.finish()

    def _attn_common(self, st, tag, ns=4):
        nc, p = self.nc, self.p
        C = self.cst

        def T(name, shape, dt):
            return st.enter_context(nc.sbuf_tensor(f"{tag}_{name}", list(shape), dt))

        def PS(name, shape, dt=F32):
            p.excl.add(name)
            return st.enter_context(nc.psum_tensor(f"{tag}_{name}", list(shape), dt))

        cm = dict(T=T, PS=PS)
        cm["ident"] = T("ident", [128, 128], BF16)
        cm["zeros"] = T("zeros", [128, 128], BF16)
        cm["caus"] = T("caus", [128, 1024], BF16)
        p.dma("sync", cm["ident"][:], C["ident"], writes=["ident"])
        p.dma("sync", cm["caus"][:], C["caus"], writes=["caus"])
        p.op("vector", lambda e: e.memset(cm["zeros"][:], 0.0), writes=["zeros"])
        cm["Sps"] = [PS(f"S{i}", [128, 512]) for i in range(ns)]
        cm["accs"] = [(PS("acc", [128, 512]), "acc"), (PS("acc2", [128, 512]), "acc2")]
        cm["acc"] = cm["accs"][0][0]
        self._acc_i = 0
        cm["Pt"] = [T(f"Pt{i}", [128, 512], BF16) for i in range(ns)]
        cm["ns"] = ns
        cm["skew"] = 2 if ns <= 4 else 3
        cm["rinv"] = T("rinv", [64, 512], F32)
        cm["ytmp"] = T("ytmp", [64, 512], F32)
        cm["yst"] = [T(f"yst{i}", [64, 512], BF16) for i in range(2)]
        self._step = 0
        return cm

    def _next_acc(self, cm):
        self._acc_i += 1
        return cm["accs"][self._acc_i % 2]

    def _run_tile(self, cm, blocks, exp_scale, acc=None, acckey="acc"):
        p = self.p
        acc = cm["acc"] if acc is None else acc
        Sps, Pt, zeros, caus = cm["Sps"], cm["Pt"], cm["zeros"], cm["caus"]
        p.op("tensor", lambda e: e.matmul(acc[:, :], lhsT=zeros[:], rhs=caus[:, 0:512], start=True, stop=False),
             reads=["zeros", "caus"], writes=[acckey])

        def pv(bl, sb, last):
            p.op("tensor", lambda e, bl=bl, sb=sb: e.matmul(bl["pv_out"], lhsT=bl["pv_lhsT"], rhs=Pt[sb][0:bl["kr"], 0:bl["n"]], start=False, stop=False),
                 reads=[f"Pt{sb}"] + bl["rd"], writes=[acckey])

        ns = cm["ns"]
        pend = []
        for i, bl in enumerate(blocks):
            sb = self._step % ns
            self._step += 1
            nq = len(bl["qk"])
            for j, (lh, rh) in enumerate(bl["qk"]):
                p.op("tensor", lambda e, lh=lh, rh=rh, sb=sb, bl=bl, j=j, nq=nq: e.matmul(Sps[sb][0:bl["kr"], 0:bl["n"]], lhsT=lh, rhs=rh, start=(j == 0), stop=(j == nq - 1)),
                     reads=bl["rd"] + ["ident", "caus"], writes=[f"S{sb}"])
            p.op("scalar", lambda e, sb=sb, bl=bl: e.activation(out=Pt[sb][0:bl["kr"], 0:bl["n"]], in_=Sps[sb][0:bl["kr"], 0:bl["n"]], func=AF.Exp, scale=exp_scale),
                 reads=[f"S{sb}"], writes=[f"Pt{sb}"])
            pend.append((bl, sb))
            if len(pend) > cm.get("skew", 2):
                pv(pend[0][0], pend[0][1], False)
                pend.pop(0)
        for (bl, sb) in pend:
            pv(bl, sb, False)
        p.op("tensor", lambda e: e.matmul(acc[:, :], lhsT=zeros[:], rhs=caus[:, 0:512], start=False, stop=True),
             reads=["zeros", "caus"], writes=[acckey])

    def _finish_tile(self, cm, g, h, gate_tile, dst, acc=None, acckey="acc", norm=True, gkey="GT"):
        p = self.p
        acc = cm["acc"] if acc is None else acc
        rinv, ytmp = cm["rinv"], cm["ytmp"]
        yb = (g + h) % 2
        yst = cm["yst"][yb]
        c0 = g * 512
        if norm:
            p.op("vector", lambda e: e.tensor_scalar(out=rinv[:], in0=acc[64:128, :], scalar1=1e-30, scalar2=None, op0=ALU.max), reads=[acckey], writes=["rinv"])
            p.op("vector", lambda e: e.reciprocal(out=rinv[:], in_=rinv[:]), reads=["rinv"], writes=["rinv"])
            p.op("vector", lambda e: e.tensor_tensor(out=ytmp[:], in0=acc[0:64, :], in1=rinv[:], op=ALU.mult), reads=[acckey, "rinv"], writes=["ytmp"])
            p.op("gpsimd", lambda e: e.tensor_tensor(out=yst[:], in0=ytmp[:], in1=gate_tile[0:64, c0:c0 + 512], op=ALU.mult), reads=["ytmp", gkey], writes=[f"yst{yb}"])
        else:
            p.op("vector", lambda e: e.tensor_tensor(out=yst[:], in0=acc[0:64, :], in1=gate_tile[0:64, c0:c0 + 512], op=ALU.mult), reads=[acckey, gkey], writes=[f"yst{yb}"])
        p.dma("gpsimd", dst[h * 64:(h + 1) * 64, c0:c0 + 512], yst[:], reads=[f"yst{yb}"])

    def _load_vaug(self, Vt, key, src, h, dil, pieces=8):
        p, NT = self.p, self.NT
        p.op("gpsimd", lambda e: e.memset(Vt[:, :, 64:128], 1.0), writes=[key])
        njb = NT // dil
        view = src[:, h * 64:(h + 1) * 64].rearrange("(jb kk r) c -> kk jb r c", kk=128, r=dil)
        tv = Vt[:, :, 0:64].rearrange("p (jb r) c -> p jb r c", r=dil)
        step = max(1, njb // pieces) if dil == 1 else 1
        for j0 in range(0, njb, step):
            p.dma("sync", tv[:, j0:j0 + step], view[:, j0:j0 + step], writes=[key])

    def phase_a(self, l):
        nc, p, S = self.nc, self.p, self.S
        C, D = self.cst, self.dr
        with contextlib.ExitStack() as st:
            cm = self._attn_common(st, f"A{l}", ns=6)
            T = cm["T"]
            band = T("band", [128, 256], BF16)
            p.dma("sync", band[:], C["band"], writes=["band"])
            KT = T("KT", [68, S], BF16)
            QT = T("QT", [68, S], BF16)
            GT = T("GT", [64, S], BF16)
            Vd = {d: T(f"V{d}", [128, self.NT, 128], BF16) for d in (1, 4, 16)}
            ident = cm["ident"]
            for h in range(4):
                p.dma("sync", KT[0:64, :], D["kaT"][h * 64:(h + 1) * 64, :], writes=["KT"])
                p.dma("sync", KT[64:68, :], C["kaug"], writes=["KT"])
                p.dma("sync", QT[0:64, :], D["qaT"][h * 64:(h + 1) * 64, :], writes=["QT"])
                p.dma("sync", QT[64:68, :], C["qaug_a"][h], writes=["QT"])
                p.dma("sync", GT[:], D["gaT"][h * 64:(h + 1) * 64, :], writes=["GT"])
                for d in (1, 4, 16):
                    self._load_vaug(Vd[d], f"V{d}", D["va"], h, d)
                for g in range(self.NG):
                    blocks = []
                    q0 = 512 * g
                    accx, acck = self._next_acc(cm)
                    for kb in range(4 * g - 1, 4 * g + 4):
                        if kb < 0:
                            continue
                        lo = max(0, 128 * kb - q0)
                        hi = min(512, 128 * kb - q0 + 256)
                        b0 = lo + q0 - 128 * kb
                        n = hi - lo
                        blocks.append(dict(qk=[(KT[:, 128 * kb:128 * kb + 128], QT[:, q0 + lo:q0 + hi]), (ident[:], band[:, b0:b0 + n])],
                                           kr=128, n=n, pv_lhsT=Vd[1][:, kb, :], pv_out=accx[:, lo:hi], rd=["KT", "QT", "band", "V1"]))
                    for r in range(4):
                        for jb in (g - 1, g):
                            if jb < 0:
                                continue
                            b0 = 0 if jb == g else 128
                            blocks.append(dict(qk=[(KT[:, 512 * jb + r:512 * jb + 512:4], QT[:, q0 + r:q0 + 512:4]), (ident[:], band[:, b0:b0 + 128])],
                                               kr=128, n=128, pv_lhsT=Vd[4][:, jb * 4 + r, :], pv_out=accx[:, r:512:4], rd=["KT", "QT", "band", "V4"]))
                    jb0, o = g // 4, 32 * (g % 4)
                    for r in range(16):
                        for jb in (jb0 - 1, jb0):
                            if jb < 0:
                                continue
                            b0 = o if jb == jb0 else 128 + o
                            blocks.append(dict(qk=[(KT[:, 2048 * jb + r:2048 * jb + 2048:16], QT[:, q0 + r:q0 + 512:16]), (ident[:], band[:, b0:b0 + 32])],
                                               kr=128, n=32, pv_lhsT=Vd[16][:, jb * 16 + r, :], pv_out=accx[:, r:512:16], rd=["KT", "QT", "band", "V16"]))
                    self._run_tile(cm, blocks, 1.0, acc=accx, acckey=acck)
                    self._finish_tile(cm, g, h, GT, D["yaT"], acc=accx, acckey=acck)
            p.finish()

    def phase_b(self, l):
        nc, p, S = self.nc, self.p, self.S
        C, D = self.cst, self.dr
        with contextlib.ExitStack() as st:
            cm = self._attn_common(st, f"B{l}", ns=6)
            T = cm["T"]
            KTs = [T(f"KT{i}", [96, S], BF16) for i in range(2)]
            QTs = [T(f"QT{i}", [96, S], BF16) for i in range(2)]
            GTs = [T(f"GT{i}", [64, S], BF16) for i in range(2)]
            V1s = [T(f"V1{i}", [128, self.NT, 128], BF16) for i in range(2)]
            ident, caus = cm["ident"], cm["caus"]
            for h in range(4):
                hb = h % 2
                KT, QT, GT, V1 = KTs[hb], QTs[hb], GTs[hb], V1s[hb]
                kK, kQ, kG, kV = f"KT{hb}", f"QT{hb}", f"GT{hb}", f"V1{hb}"
                p.dma("sync", KT[:], D["kbT"][h], writes=[kK])
                p.dma("sync", QT[:], D["qbT"][h], writes=[kQ])
                p.dma("sync", GT[:], D["gbT"][h * 64:(h + 1) * 64, :], writes=[kG])
                self._load_vaug(V1, kV, D["vb"], h, 1)
                for g in range(self.NG):
                    blocks = []
                    q0 = 512 * g
                    accx, acck = self._next_acc(cm)
                    for kb in range(0, 4 * g + 4):
                        o = kb - 4 * g
                        if o < 0:
                            blocks.append(dict(qk=[(KT[:, 128 * kb:128 * kb + 128], QT[:, q0:q0 + 512])], kr=128, n=512,
                                               pv_lhsT=V1[:, kb, :], pv_out=accx[:, :], rd=[kK, kQ, kV]))
                        else:
                            lo = 128 * o
                            n = 512 - lo
                            blocks.append(dict(qk=[(KT[:, 128 * kb:128 * kb + 128], QT[:, q0 + lo:q0 + 512]), (ident[:], caus[:, 512:512 + n])], kr=128, n=n,
                                               pv_lhsT=V1[:, kb, :], pv_out=accx[:, lo:512], rd=[kK, kQ, kV]))
                    self._run_tile(cm, blocks, 96.0 ** -0.5, acc=accx, acckey=acck)
                    self._finish_tile(cm, g, h, GT, D["ybT"], acc=accx, acckey=acck, gkey=kG)
            p.finish()

    def phase_d(self, l):
        nc, p, S = self.nc, self.p, self.S
        C, D = self.cst, self.dr
        with contextlib.ExitStack() as st:
            cm = self._attn_common(st, f"D{l}")
            T, PS = cm["T"], cm["PS"]
            KTs = [T(f"KT{i}", [64, S], BF16) for i in range(2)]
            QTs = [T(f"QT{i}", [64, S], BF16) for i in range(2)]
            GTs = [T(f"GT{i}", [64, S], BF16) for i in range(2)]
            V1s = [T(f"V1{i}", [128, self.NT, 128], BF16) for i in range(2)]
            ntri = T("ntri", [128, 128], BF16)
            nones = T("nones", [128, 128], BF16)
            p.dma("sync", ntri[:], C["ntri"], writes=["ntri"])
            p.dma("sync", nones[:], C["nones"], writes=["nones"])
            Ef = [T(f"Ef{i}", [128, 512], F32) for i in range(2)]
            Lp = [T(f"Lp{i}", [128, 512], BF16) for i in range(2)]
            Lsum = T("Lsum", [128, 512], F32)
            Lsb = T("Lsb", [128, 512], BF16)
            Xps = [PS(f"X{i}", [128, 512]) for i in range(2)]
            Sps, Pt = cm["Sps"], cm["Pt"]
            ident, caus, zeros = cm["ident"], cm["caus"], cm["zeros"]
            for h in range(4):
                hb = h % 2
                KT, QT, GT, V1 = KTs[hb], QTs[hb], GTs[hb], V1s[hb]
                kK, kQ, kG, kV = f"KT{hb}", f"QT{hb}", f"GT{hb}", f"V1{hb}"
                p.dma("sync", KT[:], D["kdT"][h * 64:(h + 1) * 64, :], writes=[kK])
                p.dma("sync", QT[:], D["qdT"][h * 64:(h + 1) * 64, :], writes=[kQ])
                p.dma("sync", GT[:], D["gdT"][h * 64:(h + 1) * 64, :], writes=[kG])
                self._load_vaug(V1, kV, D["vd"], h, 1)
                for g in range(self.NG):
                    q0 = 512 * g
                    acc, acck = self._next_acc(cm)
                    p.op("tensor", lambda e, acc=acc: e.matmul(acc[0:64, :], lhsT=zeros[:, 0:64], rhs=caus[:, 0:512], start=True, stop=False), reads=["zeros", "caus"], writes=[acck])
                    p.op("gpsimd", lambda e: e.memset(Lsum[:], 0.0), writes=["Lsum"])
                    p.op("gpsimd", lambda e: e.memset(Lsb[:], 0.0), writes=["Lsb"])
                    blks = []
                    for kb in range(4 * g + 3, -1, -1):
                        o = kb - 4 * g
                        lo = 128 * o if o >= 0 else 0
                        n = 512 - lo
                        qk = [(KT[:, 128 * kb:128 * kb + 128], QT[:, q0 + lo:q0 + 512])]
                        if o >= 0:
                            qk.append((ident[:], caus[:, 511:511 + n]))
                        blks.append((kb, lo, n, qk))
                    nb = len(blks)

                    def z1(i):
                        kb, lo, n, qk = blks[i]
                        sb = i % 2
                        for j, (lh, rh) in enumerate(qk):
                            p.op("tensor", lambda e, lh=lh, rh=rh, sb=sb, n=n, j=j, nq=len(qk): e.matmul(Sps[sb][:, 0:n], lhsT=lh, rhs=rh, start=(j == 0), stop=(j == nq - 1)),
                                 reads=[kK, kQ, "ident", "caus"], writes=[f"S{sb}"])

                    def EE(i):
                        kb, lo, n, qk = blks[i]
                        sb = i % 2
                        p.op("scalar", lambda e, sb=sb, n=n: e.activation(out=Ef[sb][:, 0:n], in_=Sps[sb][:, 0:n], func=AF.Exp), reads=[f"S{sb}"], writes=[f"Ef{sb}"])

                    def LP(i):
                        kb, lo, n, qk = blks[i]
                        sb = i % 2
                        p.op("scalar", lambda e, sb=sb, n=n: e.activation(out=Lp[sb][:, 0:n], in_=Ef[sb][:, 0:n], func=AF.Ln, bias=1.0), reads=[f"Ef{sb}"], writes=[f"Lp{sb}"])

                    def XX(i):
                        kb, lo, n, qk = blks[i]
                        sb = i % 2
                        mm = qk + [(ntri[:], Lp[sb][:, 0:n]), (nones[:], Lsb[:, lo:512])]
                        for j, (lh, rh) in enumerate(mm):
                            p.op("tensor", lambda e, lh=lh, rh=rh, sb=sb, n=n, j=j, nq=len(mm): e.matmul(Xps[sb][:, 0:n], lhsT=lh, rhs=rh, start=(j == 0), stop=(j == nq - 1)),
                                 reads=[kK, kQ, "ident", "caus", "ntri", "nones", f"Lp{sb}", "Lsb"], writes=[f"X{sb}"])

                    def AA(i):
                        kb, lo, n, qk = blks[i]
                        sb = i % 2
                        p.op("scalar", lambda e, sb=sb, n=n: e.activation(out=Pt[sb][:, 0:n], in_=Xps[sb][:, 0:n], func=AF.Exp), reads=[f"X{sb}"], writes=[f"Pt{sb}"])

                    def PV(i):
                        kb, lo, n, qk = blks[i]
                        sb = i % 2
                        p.op("tensor", lambda e, sb=sb, n=n, kb=kb, lo=lo, acc=acc, V1=V1: e.matmul(acc[0:64, lo:512], lhsT=V1[:, kb, 0:64], rhs=Pt[sb][:, 0:n], start=False, stop=False),
                             reads=[f"Pt{sb}", kV], writes=[acck])

                    def LS(i):
                        kb, lo, n, qk = blks[i]
                        sb = i % 2
                        if i < nb - 1:
                            p.op("vector", lambda e, sb=sb, n=n, lo=lo: e.tensor_tensor(out=Lsum[:, lo:512], in0=Lsum[:, lo:512], in1=Lp[sb][:, 0:n], op=ALU.add), reads=["Lsum", f"Lp{sb}"], writes=["Lsum"])
                            p.op("vector", lambda e: e.tensor_copy(out=Lsb[:], in_=Lsum[:]), reads=["Lsum"], writes=["Lsb"])

                    z1(0); EE(0)
                    if nb > 1:
                        z1(1); EE(1)
                    LP(0)
                    for i in range(nb):
                        if i + 2 < nb:
                            z1(i + 2); EE(i + 2)
                        if i + 1 < nb:
                            LP(i + 1)
                        XX(i); AA(i)
                        if i >= 1:
                            PV(i - 1)
                        LS(i)
                    PV(nb - 1)
                    p.op("tensor", lambda e, acc=acc: e.matmul(acc[0:64, :], lhsT=zeros[:, 0:64], rhs=caus[:, 0:512], start=False, stop=True), reads=["zeros", "caus"], writes=[acck])
                    self._finish_tile(cm, g, h, GT, D["ydT"], norm=False, acc=acc, acckey=acck, gkey=kG)
            p.finish()

    def phase_m(self, l, x_src, last):
        nc, p, S = self.nc, self.p, self.S
        I, C, D = self.inp, self.cst, self.dr
        with contextlib.ExitStack() as st:
            def T(name, shape, dt):
                return st.enter_context(nc.sbuf_tensor(f"M{l}_{name}", list(shape), dt))

            def PS(name, shape, dt=F32):
                p.excl.add(name)
                return st.enter_context(nc.psum_tensor(f"M{l}_{name}", list(shape), dt))

            Wm = [T(f"Wm{i}", [128, 8, 1024], BF16) for i in range(4)]
            Wb = T("Wb", [128, 4, 2, 1024], BF16)
            Wo = T("Wo", [128, 8, 1024], BF16)
            bm = T("bm", [128, 4, 8], F32)
            gT = T("gT", [128, 8], F32)
            wst = [T(f"wst{i}", [128, 1024], F32) for i in range(2)]
            hTgs = [T(f"hTg{i}", [128, 8, 512], BF16) for i in range(2)]
            YTs = [T(f"YT{i}", [128, 4, 2, 512], BF16) for i in range(2)]
            Gs = [T(f"Gs{i}", [128, 512], F32) for i in range(2)]
            mrg = T("mrg", [128, 8, 512], F32)
            mrgb = T("mrgb", [128, 8, 512], BF16)
            tmp = T("tmp", [128, 512], F32)
            xres = [T(f"xres{i}", [128, 1024], F32) for i in range(2)]
            x1t = [T(f"x1t{i}", [128, 1024], F32) for i in range(2)]
            gfin = T("gfin", [128, 1024], F32)
            sq = T("sq", [128, 1024], F32)
            ss = T("ss", [128, 2], F32)
            Gp = [PS(f"Gp{i}", [128, 512]) for i in range(2)]
            Bp = [PS(f"Bp{i}", [128, 512]) for i in range(2)]
            Op = [PS(f"Op{i}", [128, 512]) for i in range(2)]

            p.dma("sync", gT[:], I["norm_g"][l].rearrange("(c p) -> p c", p=128), writes=["gT"], allow_slow_non_contiguous=True)
            for i in range(4):
                p.dma("sync", bm[:, i, :], I["b_merge"][l, i].rearrange("(c p) -> p c", p=128), writes=["bm"], allow_slow_non_contiguous=True)
            if last:
                p.dma("sync", gfin[:], I["final_norm_g"].partition_broadcast(128), writes=["gfin"])
            k = 0
            for i in range(4):
                for c in range(8):
                    b = k % 2
                    k += 1
                    p.dma("sync", wst[b][:], I["w_merge"][l, i, c * 128:(c + 1) * 128, :], writes=[f"wst{b}"])
                    p.op("vector", lambda e, i=i, c=c, b=b: e.tensor_scalar(out=Wm[i][:, c, :], in0=wst[b][:], scalar1=gT[:, c:c + 1], scalar2=None, op0=ALU.mult),
                         reads=[f"wst{b}", "gT"], writes=["Wm"])
            for i in range(4):
                for c in range(2):
                    b = k % 2
                    k += 1
                    p.dma("sync", wst[b][:], I["w_branch"][l, i, c * 128:(c + 1) * 128, :], writes=[f"wst{b}"])
                    p.op("vector", lambda e, i=i, c=c, b=b: e.tensor_copy(out=Wb[:, i, c, :], in_=wst[b][:]), reads=[f"wst{b}"], writes=["Wb"])
            for c in range(8):
                b = k % 2
                k += 1
                p.dma("sync", wst[b][:], I["w_out"][l, c * 128:(c + 1) * 128, :], writes=[f"wst{b}"])
                p.op("vector", lambda e, c=c, b=b: e.tensor_copy(out=Wo[:, c, :], in_=wst[b][:]), reads=[f"wst{b}"], writes=["Wo"])

            step = 0
            for g in range(self.NG):
                c0 = g * 512
                hTg, YT, hk, yk = hTgs[g % 2], YTs[g % 2], f"hTg{g % 2}", f"YT{g % 2}"
                p.dma("sync", hTg[:], D["hT"].rearrange("(c p) s -> p c s", p=128)[:, :, c0:c0 + 512], writes=[hk])
                for i, m in enumerate("abcd"):
                    p.dma("sync", YT[:, i, :, :], D[f"y{m}T"].rearrange("(c p) s -> p c s", p=128)[:, :, c0:c0 + 512], writes=[yk])
                for cc in range(8):
                    cs = slice(cc * 128, (cc + 1) * 128)
                    for i in range(4):
                        fb = step % 2
                        step += 1
                        for c in range(8):
                            p.op("tensor", lambda e, i=i, c=c, cs=cs, fb=fb, hTg=hTg: e.matmul(Gp[fb][:, :], lhsT=Wm[i][:, c, cs], rhs=hTg[:, c, :], start=(c == 0), stop=(c == 7)),
                                 reads=["Wm", hk], writes=[f"Gp{fb}"])
                        p.op("scalar", lambda e, i=i, cc=cc, fb=fb: e.activation(out=Gs[fb][:], in_=Gp[fb][:, :], func=AF.Sigmoid, bias=bm[:, i, cc:cc + 1]),
                             reads=[f"Gp{fb}", "bm"], writes=[f"Gs{fb}"])
                        for c in range(2):
                            p.op("tensor", lambda e, i=i, c=c, cs=cs, fb=fb, YT=YT: e.matmul(Bp[fb][:, :], lhsT=Wb[:, i, c, cs], rhs=YT[:, i, c, :], start=(c == 0), stop=(c == 1)),
                                 reads=["Wb", yk], writes=[f"Bp{fb}"])
                        if i == 0:
                            p.op("vector", lambda e, cc=cc, fb=fb: e.tensor_tensor(out=mrg[:, cc, :], in0=Bp[fb][:, :], in1=Gs[fb][:], op=ALU.mult), reads=[f"Bp{fb}", f"Gs{fb}"], writes=["mrg"])
                        else:
                            p.op("vector", lambda e, fb=fb: e.tensor_tensor(out=tmp[:], in0=Bp[fb][:, :], in1=Gs[fb][:], op=ALU.mult), reads=[f"Bp{fb}", f"Gs{fb}"], writes=["tmp"])
                            p.op("vector", lambda e, cc=cc: e.tensor_tensor(out=mrg[:, cc, :], in0=mrg[:, cc, :], in1=tmp[:], op=ALU.add), reads=["tmp", "mrg"], writes=["mrg"])
                    p.op("scalar", lambda e, cc=cc: e.copy(out=mrgb[:, cc, :], in_=mrg[:, cc, :]), reads=["mrg"], writes=["mrgb"])
                for t in range(4):
                    r0 = c0 + t * 128
                    xb = t % 2
                    p.dma("sync", xres[xb][:], x_src[r0:r0 + 128, :], writes=[f"xres{xb}"])
                    for half in range(2):
                        for cc in range(8):
                            p.op("tensor", lambda e, cc=cc, t=t, half=half: e.matmul(Op[half][:, :], lhsT=mrgb[:, cc, t * 128:(t + 1) * 128], rhs=Wo[:, cc, half * 512:(half + 1) * 512], start=(cc == 0), stop=(cc == 7)),
                                 reads=["mrgb", "Wo"], writes=[f"Op{half}"])
                        p.op("vector", lambda e, half=half, xb=xb: e.tensor_tensor(out=x1t[xb][:, half * 512:(half + 1) * 512], in0=Op[half][:, :], in1=xres[xb][:, half * 512:(half + 1) * 512], op=ALU.add),
                             reads=[f"Op{half}", f"xres{xb}"], writes=[f"x1t{xb}"])
                    if not last:
                        p.dma("gpsimd", D["x1"][r0:r0 + 128, :], x1t[xb][:], reads=[f"x1t{xb}"])
                    else:
                        p.op("scalar", lambda e, xb=xb: e.activation(out=sq[:], in_=x1t[xb][:], func=AF.Square), reads=[f"x1t{xb}"], writes=["sq"])
                        p.op("vector", lambda e: e.reduce_sum(out=ss[:, 0:1], in_=sq[:], axis=AX.X), reads=["sq"], writes=["ss"])
                        p.op("vector", lambda e: e.tensor_scalar(out=ss[:, 1:2], in0=ss[:, 0:1], scalar1=1.0 / D_MODEL, scalar2=RMS_EPS, op0=ALU.mult, op1=ALU.add), reads=["ss"], writes=["ss"])
                        p.op("scalar", lambda e: e.sqrt(out=ss[:, 1:2], in_=ss[:, 1:2]), reads=["ss"], writes=["ss"])
                        p.op("vector", lambda e: e.reciprocal(out=ss[:, 1:2], in_=ss[:, 1:2]), reads=["ss"], writes=["ss"])
                        p.op("scalar", lambda e, xb=xb: e.activation(out=sq[:], in_=x1t[xb][:], func=AF.Copy, scale=ss[:, 1:2]), reads=[f"x1t{xb}", "ss"], writes=["sq"])
                        p.op("vector", lambda e, xb=xb: e.tensor_tensor(out=x1t[xb][:], in0=sq[:], in1=gfin[:], op=ALU.mult), reads=["sq", "gfin"], writes=[f"x1t{xb}"])
                        p.dma("gpsimd", self.out[r0:r0 + 128, :], x1t[xb][:], reads=[f"x1t{xb}"])
            p.finish()

    def phase_c(self, l):
        nc, p, S = self.nc, self.p, self.S
        I, C, D = self.inp, self.cst, self.dr
        NT = self.NT
        ncmp = S // 16 - 1
        NCH = max(1, S // 2048)
        with contextlib.ExitStack() as st:
            cm = self._attn_common(st, f"C{l}")
            T, PS = cm["T"], cm["PS"]
            ident, caus, zeros = cm["ident"], cm["caus"], cm["zeros"]
            KcT = T("KcT", [68, 512], BF16)
            Vc = T("Vc", [128, 4, 128], BF16)
            with contextlib.ExitStack() as st2:
                def T2(name, shape, dt):
                    return st2.enter_context(nc.sbuf_tensor(f"C{l}_{name}", list(shape), dt))
                srcT = [T2("kcT", [64, S], BF16), T2("vcT", [64, S], BF16)]
                w1s = T2("w1s", [64, 32, 128], F32)
                W1 = T2("W1", [64, 32, 128], BF16)
                posf = T2("posf", [64, 32], F32)
                posT = T2("posT", [64, 32], BF16)
                b1 = T2("b1", [128, 1], F32)
                bias = T2("bias", [128, 1], F32)
                w2s = T2("w2s", [128, 64], F32)
                W2 = T2("W2", [128, 64], BF16)
                b2c = T2("b2c", [64, 1], F32)
                b2r = T2("b2r", [128, 64], F32)
                hb = T2("hb", [128, 512], F32)
                u = T2("u", [128, 512], F32)
                hg = T2("hg", [128, 512], BF16)
                Hps, Cps = cm["Sps"][0], cm["Sps"][1]
                p.op("vector", lambda e: e.memset(KcT[:], 0.0), writes=["KcT"])
                p.dma("sync", KcT[64:68, :], C["kaugc"], writes=["KcT"])
                p.op("gpsimd", lambda e: e.memset(Vc[:, :, 64:128], 1.0), writes=["Vc"])
                p.op("gpsimd", lambda e: e.memset(Vc[:, :, 0:64], 0.0), writes=["Vc"])
                p.dma("sync", srcT[0][:], D["kcT"], writes=["src0"])
                p.dma("sync", srcT[1][:], D["vcT"], writes=["src1"])
                for kv in range(2):
                    p.dma("sync", w1s[:], I["nsa_w1"][l, kv].rearrange("(ll d) h -> d ll h", d=64), writes=["w1s"])
                    p.op("vector", lambda e: e.tensor_copy(out=W1[:], in_=w1s[:]), reads=["w1s"], writes=["W1"])
                    p.dma("sync", posf[:], I["nsa_pos"][l, kv].rearrange("ll d -> d ll"), writes=["posf"], allow_slow_non_contiguous=True)
                    p.op("vector", lambda e: e.tensor_copy(out=posT[:], in_=posf[:]), reads=["posf"], writes=["posT"])
                    p.dma("sync", b1[:], I["nsa_b1"][l, kv].rearrange("(p o) -> p o", o=1), writes=["b1"], allow_slow_non_contiguous=True)
                    p.dma("sync", w2s[:], I["nsa_w2"][l, kv], writes=["w2s"])
                    p.op("vector", lambda e: e.tensor_copy(out=W2[:], in_=w2s[:]), reads=["w2s"], writes=["W2"])
                    for ll in range(32):
                        p.op("tensor", lambda e, ll=ll: e.matmul(Cps[:, 0:1], lhsT=W1[:, ll, :], rhs=posT[:, ll:ll + 1], start=(ll == 0), stop=(ll == 31)), reads=["W1", "posT"], writes=["S1"])
                    p.op("vector", lambda e: e.tensor_tensor(out=bias[:], in0=Cps[:, 0:1], in1=b1[:], op=ALU.add), reads=["S1", "b1"], writes=["bias"])
                    for ll in range(32):
                        p.op("tensor", lambda e, ll=ll, kv=kv: e.matmul(Hps[:, 0:ncmp], lhsT=W1[:, ll, :], rhs=srcT[kv][:, ll:ll + 16 * (ncmp - 1) + 1:16], start=(ll == 0), stop=(ll == 31)),
                             reads=["W1", f"src{kv}"], writes=["S0"])
                    p.op("scalar", lambda e: e.activation(out=hb[:, 0:ncmp], in_=Hps[:, 0:ncmp], func=AF.Identity, bias=bias[:, 0:1]), reads=["S0", "bias"], writes=["hb"])
                    p.op("vector", lambda e: e.tensor_tensor(out=u[:, 0:ncmp], in0=hb[:, 0:ncmp], in1=hb[:, 0:ncmp], op=ALU.mult), reads=["hb"], writes=["u"])
                    p.op("vector", lambda e: e.tensor_scalar(out=u[:, 0:ncmp], in0=u[:, 0:ncmp], scalar1=0.044715, scalar2=1.0, op0=ALU.mult, op1=ALU.add), reads=["u"], writes=["u"])
                    p.op("vector", lambda e: e.tensor_tensor(out=u[:, 0:ncmp], in0=u[:, 0:ncmp], in1=hb[:, 0:ncmp], op=ALU.mult), reads=["u", "hb"], writes=["u"])
                    p.op("scalar", lambda e: e.activation(out=u[:, 0:ncmp], in_=u[:, 0:ncmp], func=AF.Sigmoid, scale=1.5957691216057308), reads=["u"], writes=["u"])
                    p.op("vector", lambda e: e.memset(hg[:], 0.0), writes=["hg"])
                    p.op("vector", lambda e: e.tensor_tensor(out=hg[:, 0:ncmp], in0=u[:, 0:ncmp], in1=hb[:, 0:ncmp], op=ALU.mult), reads=["u", "hb"], writes=["hg"])
                    if kv == 0:
                        p.dma("sync", b2c[:], I["nsa_b2"][l, 0].rearrange("(p o) -> p o", o=1), writes=["b2c"], allow_slow_non_contiguous=True)
                        p.op("tensor", lambda e: e.matmul(Cps[0:64, 0:ncmp], lhsT=W2[:], rhs=hg[:, 0:ncmp], start=True, stop=True), reads=["W2", "hg"], writes=["S1"])
                        p.op("scalar", lambda e: e.activation(out=KcT[0:64, 0:ncmp], in_=Cps[0:64, 0:ncmp], func=AF.Identity, bias=b2c[:, 0:1]), reads=["S1", "b2c"], writes=["KcT"])
                    else:
                        p.dma("sync", b2r[:], I["nsa_b2"][l, 1].partition_broadcast(128), writes=["b2r"])
                        for ch in range(NCH):
                            p.op("tensor", lambda e, ch=ch: e.matmul(Cps[:, 0:64], lhsT=hg[:, ch * 128:(ch + 1) * 128], rhs=W2[:], start=True, stop=True), reads=["W2", "hg"], writes=["S1"])
                            p.op("vector", lambda e, ch=ch: e.tensor_tensor(out=Vc[:, ch, 0:64], in0=Cps[:, 0:64], in1=b2r[:], op=ALU.add), reads=["S1", "b2r"], writes=["Vc"])
                p.finish()
            KsT = T("KsT", [68, S], BF16)
            KwT = T("KwT", [68, S], BF16)
            Vs = T("Vs", [128, NT, 128], BF16)
            Vw = T("Vw", [128, NT, 128], BF16)
            ewide = T("ewide", [128, S], BF16)
            antic = T("antic", [128, 1024], BF16)
            cmk = T("cmk", [128, 2560], BF16)
            selmap = T("selmap", [128, 4, 128], BF16)
            tkm = T("tkm", [128, 256], F32)
            tka = T("tka", [128, 256], F32)
            Qt = [T(f"Qt{i}", [68, 4, 512], BF16) for i in range(2)]
            Gt = [T(f"Gt{i}", [64, 4, 512], BF16) for i in range(2)]
            cg = [T(f"cg{i}", [64, 12, 512], BF16) for i in range(2)]
            ysum = T("ysum", [64, 4, 512], F32)
            psel = T("psel", [128, 4, 128], F32)
            rq = T("rq", [128, 4], F32)
            sc = T("sc", [128, 128], F32)
            sc2 = T("sc2", [128, 128], F32)
            m8 = T("m8", [128, 16], F32)
            selb = T("selb", [128, 128], BF16)
            selm1T = T("selm1T", [128, 512], BF16)
            psQ = PS("psQ", [128, 4, 128])
            p.op("vector", lambda e: e.memset(psel[:], 0.0), writes=["psel"])
            acc2 = cm["accs"][1][0]
            TP = PS("TP", [128, 128], BF16)
            acc = cm["acc"]
            Sps, Pt = cm["Sps"], cm["Pt"]
            rinv, ytmp = cm["rinv"], cm["ytmp"]
            for (dst, src, aug) in ((KsT, "ksT", "kaug"), (KwT, "kwT", "kaug")):
                p.dma("sync", dst[0:64, :], D[src], writes=[src])
                p.dma("sync", dst[64:68, :], C[aug], writes=[src])
            self._load_vaug(Vs, "Vs", D["vs"], 0, 1)
            self._load_vaug(Vw, "Vw", D["vw"], 0, 1)
            p.dma("sync", ewide[:], C["ewide"], writes=["ewide"])
            p.dma("sync", antic[:], C["antic"], writes=["antic"])
            p.dma("sync", cmk[:], C["cm"], writes=["cmk"])
            p.dma("sync", selmap[:], C["selmap"].rearrange("(c n) j -> n c j", n=128)[:, :, 1:129], writes=["selmap"])
            p.dma("sync", tkm[:], C["tk_mult"], writes=["tkm"])
            p.dma("sync", tka[:], C["tk_add"], writes=["tka"])

            def branch_out(accx, acckey, h, gidx, cgb, first):
                p.op("vector", lambda e: e.tensor_scalar(out=rinv[:], in0=accx[64:128, :], scalar1=1e-30, scalar2=None, op0=ALU.max), reads=[acckey], writes=["rinv"])
                p.op("vector", lambda e: e.reciprocal(out=rinv[:], in_=rinv[:]), reads=["rinv"], writes=["rinv"])
                p.op("vector", lambda e: e.tensor_tensor(out=ytmp[:], in0=accx[0:64, :], in1=rinv[:], op=ALU.mult), reads=[acckey, "rinv"], writes=["ytmp"])
                if first:
                    p.op("vector", lambda e: e.tensor_tensor(out=ysum[:, h, :], in0=ytmp[:], in1=cg[cgb][:, gidx, :], op=ALU.mult), reads=["ytmp", f"cg{cgb}"], writes=["ysum"])
                else:
                    p.op("vector", lambda e: e.tensor_tensor(out=ytmp[:], in0=ytmp[:], in1=cg[cgb][:, gidx, :], op=ALU.mult), reads=["ytmp", f"cg{cgb}"], writes=["ytmp"])
                    p.op("vector", lambda e: e.tensor_tensor(out=ysum[:, h, :], in0=ysum[:, h, :], in1=ytmp[:], op=ALU.add), reads=["ytmp", "ysum"], writes=["ysum"])

            for g in range(self.NG):
                q0 = 512 * g
                qb = g % 2
                for h in range(4):
                    p.dma("sync", Qt[qb][0:64, h, :], D["qcT"][h * 64:(h + 1) * 64, q0:q0 + 512], writes=[f"Qt{qb}"])
                    p.dma("sync", Qt[qb][64:68, h, :], C["qaug_c"][h, :, q0:q0 + 512], writes=[f"Qt{qb}"])
                    p.dma("sync", Gt[qb][:, h, :], D["gcT"][h * 64:(h + 1) * 64, q0:q0 + 512], writes=[f"Gt{qb}"])
                for j in range(12):
                    p.dma("sync", cg[qb][:, j, :], D["cgT"][j:j + 1, q0:q0 + 512].partition_broadcast(64), writes=[f"cg{qb}"])
                chs = [ch for ch in range(NCH) if q0 - 2048 * ch >= 0]
                for h in range(4):
                    p.op("tensor", lambda e: e.matmul(acc[:, :], lhsT=zeros[:], rhs=caus[:, 0:512], start=True, stop=False), reads=["zeros", "caus"], writes=["acc"])
                    used = []
                    for ci, ch in enumerate(chs):
                        sb = self._step % cm["ns"]
                        self._step += 1
                        dlt = q0 - 2048 * ch
                        qk = [(KcT[:, ch * 128:(ch + 1) * 128], Qt[qb][:, h, :])]
                        if dlt < 2560:
                            qk.append((ident[:], cmk[:, dlt:dlt + 512]))
                        for j, (lh, rh) in enumerate(qk):
                            p.op("tensor", lambda e, lh=lh, rh=rh, sb=sb, j=j, nq=len(qk): e.matmul(Sps[sb][:, :], lhsT=lh, rhs=rh, start=(j == 0), stop=(j == nq - 1)),
                                 reads=["KcT", f"Qt{qb}", "ident", "cmk"], writes=[f"S{sb}"])
                        p.op("scalar", lambda e, sb=sb: e.activation(out=Pt[sb][:], in_=Sps[sb][:, :], func=AF.Exp), reads=[f"S{sb}"], writes=[f"Pt{sb}"])
                        p.op("tensor", lambda e, sb=sb, ch=ch: e.matmul(acc[:, :], lhsT=Vc[:, ch, :], rhs=Pt[sb][:], start=False, stop=False), reads=[f"Pt{sb}", "Vc"], writes=["acc"])
                        used.append((sb, ch))
                    p.op("tensor", lambda e: e.matmul(acc[:, :], lhsT=zeros[:], rhs=caus[:, 0:512], start=False, stop=True), reads=["zeros", "caus"], writes=["acc"])
                    for qs in range(4):
                        for ci, (sb, ch) in enumerate(used):
                            p.op("tensor", lambda e, sb=sb, ch=ch, qs=qs, ci=ci, nch=len(used): e.matmul(psQ[:, qs, :], lhsT=Pt[sb][:, qs * 128:(qs + 1) * 128], rhs=selmap[:, ch, :], start=(ci == 0), stop=(ci == nch - 1)),
                                 reads=[f"Pt{sb}", "selmap"], writes=["psQ"])
                    branch_out(acc, "acc", h, h * 3 + 0, qb, True)
                    for qs in range(4):
                        p.op("vector", lambda e, qs=qs: e.tensor_scalar(out=rq[:, qs:qs + 1], in0=psQ[:, qs, 127:128], scalar1=1e-30, scalar2=None, op0=ALU.max), reads=["psQ"], writes=["rq"])
                        p.op("vector", lambda e, qs=qs: e.reciprocal(out=rq[:, qs:qs + 1], in_=rq[:, qs:qs + 1]), reads=["rq"], writes=["rq"])
                        if h == 0:
                            p.op("vector", lambda e, qs=qs: e.tensor_scalar(out=psel[:, qs, 1:128], in0=psQ[:, qs, 0:127], scalar1=rq[:, qs:qs + 1], scalar2=None, op0=ALU.mult), reads=["psQ", "rq"], writes=["psel"])
                        else:
                            p.op("vector", lambda e, qs=qs: e.scalar_tensor_tensor(out=psel[:, qs, 1:128], in0=psQ[:, qs, 0:127], scalar=rq[:, qs:qs + 1], in1=psel[:, qs, 1:128], op0=ALU.mult, op1=ALU.add), reads=["psQ", "rq", "psel"], writes=["psel"])
                for qs in range(4):
                    m = 4 * g + qs
                    j0 = 128 - 2 * m
                    p.op("vector", lambda e, qs=qs, j0=j0: e.tensor_tensor(out=sc[:], in0=psel[:, qs, :], in1=tkm[:, j0:j0 + 128], op=ALU.mult), reads=["psel", "tkm"], writes=["sc"])
                    p.op("vector", lambda e, j0=j0: e.tensor_tensor(out=sc[:], in0=sc[:], in1=tka[:, j0:j0 + 128], op=ALU.add), reads=["sc", "tka"], writes=["sc"])
                    p.op("vector", lambda e: e.memset(sc[:, 0:1], 1e9), reads=["sc"], writes=["sc"])
                    p.op("vector", lambda e: e.max(out=m8[:, 0:8], in_=sc[:]), reads=["sc"], writes=["m8"])
                    p.op("vector", lambda e: e.match_replace(out=sc2[:], in_to_replace=m8[:, 0:8], in_values=sc[:], imm_value=-2.0), reads=["sc", "m8"], writes=["sc2"])
                    p.op("vector", lambda e: e.max(out=m8[:, 8:16], in_=sc2[:]), reads=["sc2"], writes=["m8"])
                    p.op("vector", lambda e: e.tensor_scalar(out=selb[:], in0=sc[:], scalar1=m8[:, 15:16], scalar2=1.0, op0=ALU.is_ge, op1=ALU.subtract), reads=["sc", "m8"], writes=["selb"])
                    p.op("tensor", lambda e: e.transpose(TP[:, :], selb[:], ident[:]), reads=["selb", "ident"], writes=["TP"])
                    p.op("vector", lambda e, qs=qs: e.tensor_copy(out=selm1T[:, qs * 128:(qs + 1) * 128], in_=TP[:, :]), reads=["TP"], writes=["selm1T"])
                for h in range(4):
                    blocks = []
                    for kb in range(0, 4 * g + 4):
                        o = kb - 4 * g
                        lo = 128 * o if o >= 0 else 0
                        n = 512 - lo
                        qk = [(KsT[:, 128 * kb:128 * kb + 128], Qt[qb][:, h, lo:512]), (ewide[:, 128 * kb:128 * kb + 128], selm1T[:, lo:512])]
                        if o >= 0:
                            qk.append((ident[:], caus[:, 512:512 + n]))
                        blocks.append(dict(qk=qk, kr=128, n=n, pv_lhsT=Vs[:, kb, :], pv_out=acc2[:, lo:512], rd=["ksT", f"Qt{qb}", "ewide", "selm1T", "Vs"]))
                    self._run_tile(cm, blocks, 1.0, acc=acc2, acckey="acc2")
                    branch_out(acc2, "acc2", h, h * 3 + 1, qb, False)
                    blocks = []
                    for kb in range(max(0, 4 * g - 4), 4 * g + 4):
                        o = kb - 4 * g
                        if o >= 0:
                            lo, hi = 128 * o, 512
                            msk = caus[:, 512:512 + (hi - lo)]
                        else:
                            lo, hi = 0, min(512, 640 + 128 * o)
                            msk = antic[:, -128 * o:-128 * o + hi]
                        blocks.append(dict(qk=[(KwT[:, 128 * kb:128 * kb + 128], Qt[qb][:, h, lo:hi]), (ident[:], msk)], kr=128, n=hi - lo,
                                           pv_lhsT=Vw[:, kb, :], pv_out=acc[:, lo:hi], rd=["kwT", f"Qt{qb}", "antic", "Vw"]))
                    self._run_tile(cm, blocks, 1.0)
                    branch_out(acc, "acc", h, h * 3 + 2, qb, False)
                    yb = (g + h) % 2
                    yst = cm["yst"][yb]
                    p.op("vector", lambda e, h=h, yst=yst, qb=qb: e.tensor_tensor(out=yst[:], in0=ysum[:, h, :], in1=Gt[qb][:, h, :], op=ALU.mult), reads=["ysum", f"Gt{qb}"], writes=[f"yst{yb}"])
                    p.dma("gpsimd", D["ycT"][h * 64:(h + 1) * 64, q0:q0 + 512], yst[:], reads=[f"yst{yb}"])
            p.finish()

    def build_all(self):
        self.declare()
        x_src = self.inp["x"]
        for l in range(self.n_layers):
            last = (l == self.n_layers - 1)
            self.phase_proj(l, x_src)
            self.phase_a(l)
            self.phase_b(l)
            self.phase_c(l)
            self.phase_d(l)
            self.phase_m(l, x_src, last)
            x_src = self.dr["x1"]
        return self.nc


_CACHE = {}


def _get_builder(S):
    if S not in _CACHE:
        b = Builder(S)
        b.build_all()
        _CACHE[S] = b
    return _CACHE[S]


def kernel(x, norm_g, w_in, mla_q_norm, mla_w_uq, mla_kv_norm, mla_w_ukv, nsa_pos, nsa_w1,
           nsa_b1, nsa_w2, nsa_b2, w_branch, w_merge, b_merge, w_out, final_norm_g):
    x = np.asarray(x, dtype=np.float32)
    Bn, S, Dm = x.shape
    bld = _get_builder(S)
    shared = dict(norm_g=norm_g, w_in=w_in, mla_q_norm=mla_q_norm, mla_w_uq=mla_w_uq, mla_kv_norm=mla_kv_norm,
                  mla_w_ukv=mla_w_ukv, nsa_pos=nsa_pos, nsa_w1=nsa_w1, nsa_b1=nsa_b1, nsa_w2=nsa_w2, nsa_b2=nsa_b2,
                  w_branch=w_branch, w_merge=w_merge, b_merge=b_merge, w_out=w_out, final_norm_g=final_norm_g)
    shared = {k: np.ascontiguousarray(np.asarray(v, dtype=np.float32)) for k, v in shared.items()}
    for k, v in bld.consts_np.items():
        shared["c_" + k] = v
    n_cores = 8
    in_maps = []
    for c in range(n_cores):
        m = dict(shared)
        m["x"] = np.ascontiguousarray(x[(c // 2) % Bn])
        in_maps.append(m)
    res = run_bass_kernel_spmd(bld.nc, in_maps, core_ids=list(range(n_cores)))
    out = np.empty((Bn, S, Dm), np.float32)
    half = S // 2
    for b in range(Bn):
        out[b, :half] = res.results[2 * b]["out"][:half]
        out[b, half:] = res.results[2 * b + 1]["out"][half:]
    return out
```
